# Optimizing a Trainium2 kernel written in Bass

```python
import jax
import jax.numpy as jnp
from jax import lax
import numpy as np

D_MODEL = 1024
BATCH = 8
SEQ = 2048
DEPTH = 2
DEC_BATCH = 128
DEC_SEQ = 8
PAST_LEN = 16384
PAGE_SIZE = 128

N_META = 16
N_EVEN = (DEPTH + 1) // 2
N_ODD = DEPTH // 2
CHUNK = 64
CONV_W = 4
LN_EPS = 1e-5
RMS_EPS = 1e-6
ALPHA = (2.0 * DEPTH) ** 0.25
BETA = (8.0 * DEPTH) ** -0.25
D_FF = 4 * D_MODEL
SSD_HEADDIM = 64
SSD_DINNER = D_MODEL
SSD_HEADS = SSD_DINNER // SSD_HEADDIM
SSD_GROUPS = 2
SSD_HPG = SSD_HEADS // SSD_GROUPS
SSD_DSTATE = 128
SSD_CONV_DIM = SSD_DINNER + 2 * SSD_GROUPS * SSD_DSTATE
RG_WIDTH = D_MODEL
RG_BLOCKS = 8
RG_BW = RG_WIDTH // RG_BLOCKS
RG_C = 8.0
GLA_HEADS = 4
GLA_DK = D_MODEL // 2 // GLA_HEADS
GLA_DV = D_MODEL // GLA_HEADS
GLA_RANK = 16
GLA_GATE_NORM = 16.0
HGRN_DK = 128
HGRN_HEADS = D_MODEL // HGRN_DK
HGRN_DV = HGRN_DK
EVEN_SIZES = (SSD_DINNER, SSD_CONV_DIM, SSD_HEADS, RG_WIDTH, RG_WIDTH)
ODD_SIZES = (GLA_HEADS * GLA_DK, GLA_HEADS * GLA_DK, GLA_HEADS * GLA_DV, GLA_HEADS * GLA_DV, GLA_RANK,
             HGRN_HEADS * HGRN_DK, HGRN_HEADS * HGRN_DK, HGRN_HEADS * HGRN_DV, HGRN_HEADS * HGRN_DV)
EVEN_IN = sum(EVEN_SIZES)
ODD_IN = sum(ODD_SIZES)
EVEN_SPLIT = [int(v) for v in np.cumsum(EVEN_SIZES)[:-1]]
ODD_SPLIT = [int(v) for v in np.cumsum(ODD_SIZES)[:-1]]
EVEN_MIX = SSD_DINNER + RG_WIDTH
ODD_MIX = GLA_HEADS * GLA_DV + HGRN_HEADS * HGRN_DV
F32 = jnp.float32

kernel_name = 'hybrid_ssd_rglru_gla_hgrn2_step'


def _layer_norm(x, g, b):
    xf = x.astype(F32)
    mu = jnp.mean(xf, -1, keepdims=True)
    var = jnp.mean(jnp.square(xf - mu), -1, keepdims=True)
    return ((xf - mu) * lax.rsqrt(var + LN_EPS) * g.astype(F32) + b.astype(F32)).astype(x.dtype)


def _rms_norm(x, g):
    xf = x.astype(F32)
    return (xf * lax.rsqrt(jnp.mean(jnp.square(xf), -1, keepdims=True) + RMS_EPS) * g.astype(F32)).astype(x.dtype)


def _causal_dwconv(u, buf, w, b):
    L = u.shape[1]
    full = jnp.concatenate([buf.astype(u.dtype), u], axis=1)
    out = b + sum(full[:, t:t + L] * w[t] for t in range(CONV_W))
    return out, full[:, L:]


def _to_chunks(a, q):
    bsz, L = a.shape[:2]
    n = -(-L // q)
    a = jnp.pad(a, [(0, 0), (0, n * q - L)] + [(0, 0)] * (a.ndim - 2))
    return jnp.moveaxis(a.reshape((bsz, n, q) + a.shape[2:]), 1, 0)


def _from_chunks(a, L):
    a = jnp.moveaxis(a, 0, 1)
    return a.reshape((a.shape[0], a.shape[1] * a.shape[2]) + a.shape[3:])[:, :L]


def _run_segments(fn, seg_lens, arrays, state):
    outs, start = [], 0
    for n in seg_lens:
        o, state = fn(*[a[:, start:start + n] for a in arrays], state)
        outs.append(o)
        start += n
    return jnp.concatenate(outs, axis=1), state


def _ssd_chunked(x, dt, log_a, bm, cm, s0):
    L = x.shape[1]
    q = min(CHUNK, L)
    xd = _to_chunks(x.astype(F32) * dt[..., None], q)
    b = jnp.cumsum(_to_chunks(log_a, q), axis=2)
    bm = _to_chunks(bm.astype(F32), q)
    cm = _to_chunks(cm.astype(F32), q)
    causal = jnp.tril(jnp.ones((q, q), dtype=bool))[:, :, None, None]
    seg = b[:, :, :, None] - b[:, :, None, :]
    decay = jnp.exp(jnp.where(causal, seg, -jnp.inf))
    cb = jnp.einsum('nbtgN,nbsgN->nbtsg', cm, bm)
    y_intra = jnp.einsum('nbtsgh,nbsghp->nbtghp', cb[..., None] * decay, xd)
    b_last = b[:, :, -1]
    to_end = jnp.exp(b_last[:, :, None] - b)

    def step(s, inp):
        c_c, eb_c, b_c, te_c, xd_c, bl_c = inp
        y = jnp.einsum('btgN,bghpN->btghp', c_c, s) * eb_c[..., None]
        s = s * jnp.exp(bl_c)[..., None, None] + jnp.einsum('bsgN,bsghp->bghpN', b_c, xd_c * te_c[..., None])
        return s, y

    s_fin, y_inter = lax.scan(step, s0, (cm, jnp.exp(b), bm, to_end, xd, b_last))
    return _from_chunks(y_intra + y_inter, L), s_fin


def _gla_chunked(qv, kv, vv, log_f, s0):
    L = qv.shape[1]
    c = min(CHUNK, L)
    qv, kv, vv, log_f = (_to_chunks(t.astype(F32), c) for t in (qv, kv, vv, log_f))
    b = jnp.cumsum(log_f, axis=2)
    b_ref = b[:, :, c // 2][:, :, None]
    scores = jnp.einsum('nbthk,nbshk->nbhts', qv * jnp.exp(b - b_ref), kv * jnp.exp(b_ref - b))
    causal = jnp.tril(jnp.ones((c, c), dtype=bool))
    scores = jnp.where(causal, scores, 0.0)
    o_intra = jnp.einsum('nbhts,nbshv->nbthv', scores, vv)
    b_last = b[:, :, -1]

    def step(s, inp):
        q_c, k_c, v_c, bl_c = inp
        o = jnp.einsum('bthk,bhkv->bthv', q_c, s)
        s = s * jnp.exp(bl_c)[..., None] + jnp.einsum('bshk,bshv->bhkv', k_c, v_c)
        return s, o

    s_fin, o_inter = lax.scan(step, s0, (qv * jnp.exp(b), kv * jnp.exp(b_last[:, :, None] - b), vv, b_last))
    return _from_chunks(o_intra + o_inter, L), s_fin


def _rglru_scan(a, bterm, h0):
    bterm = bterm.at[:, 0].add(a[:, 0] * h0)

    def combine(left, right):
        return left[0] * right[0], right[0] * left[1] + right[1]

    _, hs = lax.associative_scan(combine, (a, bterm), axis=1)
    return hs, hs[:, -1]


def _even_mixer(h, segs, st, p, j):
    s_ssd, s_ssd_conv, s_rg, s_rg_conv = st
    bsz, L, _ = h.shape
    u = h @ p['w_in_even'][j]
    z, xbc, dt_raw, rg_gate, rg_x = jnp.split(u, EVEN_SPLIT, axis=-1)
    xbc, ssd_conv_new = _causal_dwconv(xbc, s_ssd_conv, p['ssd_conv_w'][j], p['ssd_conv_b'][j])
    xbc = jax.nn.silu(xbc)
    xs, bm, cm = jnp.split(xbc, [SSD_DINNER, SSD_DINNER + SSD_GROUPS * SSD_DSTATE], axis=-1)
    xs = xs.reshape(bsz, L, SSD_GROUPS, SSD_HPG, SSD_HEADDIM)
    bm = bm.reshape(bsz, L, SSD_GROUPS, SSD_DSTATE)
    cm = cm.reshape(bsz, L, SSD_GROUPS, SSD_DSTATE)
    dt = jax.nn.softplus(dt_raw.astype(F32) + p['ssd_dt_bias'][j].astype(F32)).reshape(bsz, L, SSD_GROUPS, SSD_HPG)
    a = -jnp.exp(p['ssd_a_log'][j].astype(F32)).reshape(SSD_GROUPS, SSD_HPG)
    s0 = s_ssd.astype(F32).reshape(bsz, SSD_GROUPS, SSD_HPG, SSD_HEADDIM, SSD_DSTATE)
    y, ssd_new = _run_segments(_ssd_chunked, segs, (xs, dt, dt * a, bm, cm), s0)
    y = y + p['ssd_d'][j].astype(F32).reshape(SSD_GROUPS, SSD_HPG, 1) * xs.astype(F32)
    y = y.reshape(bsz, L, SSD_DINNER) * jax.nn.silu(z.astype(F32))
    y_ssd = _rms_norm(y.reshape(bsz, L, SSD_GROUPS, -1),
                      p['ssd_norm_g'][j].reshape(SSD_GROUPS, -1)).reshape(bsz, L, SSD_DINNER)
    xr, rg_conv_new = _causal_dwconv(rg_x, s_rg_conv, p['rg_conv_w'][j], p['rg_conv_b'][j])
    xb = xr.reshape(bsz, L, RG_BLOCKS, RG_BW)
    r = jax.nn.sigmoid((jnp.einsum('blnc,ncd->blnd', xb, p['rg_wa'][j]).reshape(bsz, L, RG_WIDTH)
                        + p['rg_ba'][j]).astype(F32))
    i = jax.nn.sigmoid((jnp.einsum('blnc,ncd->blnd', xb, p['rg_wx'][j]).reshape(bsz, L, RG_WIDTH)
                        + p['rg_bx'][j]).astype(F32))
    log_a = -RG_C * r * jax.nn.softplus(-p['rg_lambda'][j].astype(F32))
    gated = jnp.sqrt(-jnp.expm1(2.0 * log_a)) * (i * xr.astype(F32))
    hs, rg_new = _rglru_scan(jnp.exp(log_a), gated, s_rg.astype(F32))
    y_rg = hs * jax.nn.gelu(rg_gate.astype(F32))
    mixed = jnp.concatenate([y_ssd, y_rg], axis=-1).astype(h.dtype) @ p['w_out_even'][j]
    ssd_new = ssd_new.reshape(bsz, SSD_HEADS, SSD_HEADDIM, SSD_DSTATE)
    return mixed, (ssd_new, ssd_conv_new, rg_new, rg_conv_new)


def _odd_mixer(h, segs, st, p, j, layer):
    s_gla, s_hgrn = st
    bsz, L, _ = h.shape
    u = h @ p['w_in_odd'][j]
    qg, kg, vg, gg, lrg, qh, fh, ih, gh = jnp.split(u, ODD_SPLIT, axis=-1)
    gla_h = lambda t, d: t.reshape(bsz, L, GLA_HEADS, d)
    log_f = jax.nn.log_sigmoid((lrg @ p['gla_wg2'][j] + p['gla_bg2'][j]).astype(F32)) / GLA_GATE_NORM
    o_g, gla_new = _run_segments(
        _gla_chunked, segs,
        (gla_h(qg * GLA_DK ** -0.5, GLA_DK), gla_h(kg, GLA_DK), gla_h(vg, GLA_DV), gla_h(log_f, GLA_DK)),
        s_gla.astype(F32))
    o_g = _rms_norm(o_g, p['gla_norm_g'][j]) * jax.nn.silu(gla_h(gg, GLA_DV).astype(F32))
    gamma = jax.nn.softmax(p['hgrn_lb_logits'].astype(F32), axis=0)
    lb = jnp.cumsum(gamma, axis=0)[layer] - gamma[0]
    fs = fh.astype(F32)
    log_fh = jnp.log(lb + (1.0 - lb) * jax.nn.sigmoid(fs))
    kh = (1.0 - lb) * jax.nn.sigmoid(-fs)
    hg_h = lambda t, d: t.reshape(bsz, L, HGRN_HEADS, d)
    o_h, hgrn_new = _run_segments(
        _gla_chunked, segs,
        (hg_h(jax.nn.silu(qh), HGRN_DK), hg_h(kh, HGRN_DK), hg_h(ih, HGRN_DV), hg_h(log_fh, HGRN_DK)),
        s_hgrn.astype(F32))
    o_h = _rms_norm(o_h, p['hgrn_norm_g'][j]) * jax.nn.silu(hg_h(gh, HGRN_DV).astype(F32))
    mixed = jnp.concatenate([o_g.reshape(bsz, L, -1), o_h.reshape(bsz, L, -1)], axis=-1).astype(h.dtype) @ p['w_out_odd'][j]
    return mixed, (gla_new, hgrn_new)


def _sq_relu_mlp(h, w1, w2):
    return jnp.square(jax.nn.relu(h @ w1)) @ w2


def _zero_states(bsz, dtype):
    return (jnp.zeros((N_EVEN, bsz, SSD_HEADS, SSD_HEADDIM, SSD_DSTATE), F32),
            jnp.zeros((N_EVEN, bsz, CONV_W - 1, SSD_CONV_DIM), dtype),
            jnp.zeros((N_EVEN, bsz, RG_WIDTH), F32),
            jnp.zeros((N_EVEN, bsz, CONV_W - 1, RG_WIDTH), dtype),
            jnp.zeros((N_ODD, bsz, GLA_HEADS, GLA_DK, GLA_DV), F32),
            jnp.zeros((N_ODD, bsz, HGRN_HEADS, HGRN_DK, HGRN_DV), F32))


def _trunk(h, segs, states, p):
    ssd, ssd_conv, rg, rg_conv, gla, hgrn = states
    n_ssd, n_ssd_conv, n_rg, n_rg_conv, n_gla, n_hgrn = [], [], [], [], [], []
    for layer in range(DEPTH):
        j = layer // 2
        if layer % 2 == 0:
            mixed, (a1, a2, a3, a4) = _even_mixer(h, segs, (ssd[j], ssd_conv[j], rg[j], rg_conv[j]), p, j)
            n_ssd.append(a1)
            n_ssd_conv.append(a2)
            n_rg.append(a3)
            n_rg_conv.append(a4)
        else:
            mixed, (a5, a6) = _odd_mixer(h, segs, (gla[j], hgrn[j]), p, j, layer)
            n_gla.append(a5)
            n_hgrn.append(a6)
        h = _layer_norm(ALPHA * h + mixed, p['ln1_g'][layer], p['ln1_b'][layer])
        h = _layer_norm(ALPHA * h + _sq_relu_mlp(h, p['mlp_w1'][layer], p['mlp_w2'][layer]),
                        p['ln2_g'][layer], p['ln2_b'][layer])
    return h, (jnp.stack(n_ssd), jnp.stack(n_ssd_conv), jnp.stack(n_rg), jnp.stack(n_rg_conv),
               jnp.stack(n_gla), jnp.stack(n_hgrn))


def setup_inputs(seed: int = 0) -> dict:
    key = jax.random.key(seed)
    ks = list(jax.random.split(key, 64))
    nrm = lambda shape, s: s * jax.random.normal(ks.pop(), shape, F32)
    uni = lambda shape, lo, hi: jax.random.uniform(ks.pop(), shape, F32, lo, hi)
    dt0 = jnp.exp(uni((N_EVEN, SSD_HEADS), float(np.log(1e-3)), float(np.log(1e-1))))
    lam_s = uni((N_EVEN, RG_WIDTH), 0.9, 0.999) ** (1.0 / RG_C)
    return {
        'x_prompt': nrm((BATCH, SEQ, D_MODEL), 1.0),
        'x_sample': nrm((DEC_BATCH, DEC_SEQ, D_MODEL), 1.0),
        'state_ssd': nrm((N_EVEN, DEC_BATCH, SSD_HEADS, SSD_HEADDIM, SSD_DSTATE), 0.1),
        'state_ssd_conv': nrm((N_EVEN, DEC_BATCH, CONV_W - 1, SSD_CONV_DIM), 1.0),
        'state_rglru': nrm((N_EVEN, DEC_BATCH, RG_WIDTH), 0.5),
        'state_rglru_conv': nrm((N_EVEN, DEC_BATCH, CONV_W - 1, RG_WIDTH), 1.0),
        'state_gla': nrm((N_ODD, DEC_BATCH, GLA_HEADS, GLA_DK, GLA_DV), 0.1),
        'state_hgrn': nrm((N_ODD, DEC_BATCH, HGRN_HEADS, HGRN_DK, HGRN_DV), 0.3),
        'meta_tokens': nrm((N_META, D_MODEL), 1.0),
        'w_in_even': nrm((N_EVEN, D_MODEL, EVEN_IN), D_MODEL ** -0.5),
        'ssd_conv_w': nrm((N_EVEN, CONV_W, SSD_CONV_DIM), CONV_W ** -0.5),
        'ssd_conv_b': nrm((N_EVEN, SSD_CONV_DIM), 0.01),
        'ssd_dt_bias': dt0 + jnp.log(-jnp.expm1(-dt0)),
        'ssd_a_log': jnp.log(uni((N_EVEN, SSD_HEADS), 1.0, 16.0)),
        'ssd_d': 1.0 + nrm((N_EVEN, SSD_HEADS), 0.01),
        'ssd_norm_g': 1.0 + nrm((N_EVEN, SSD_DINNER), 0.01),
        'rg_conv_w': nrm((N_EVEN, CONV_W, RG_WIDTH), CONV_W ** -0.5),
        'rg_conv_b': nrm((N_EVEN, RG_WIDTH), 0.01),
        'rg_wa': nrm((N_EVEN, RG_BLOCKS, RG_BW, RG_BW), RG_BW ** -0.5),
        'rg_ba': nrm((N_EVEN, RG_WIDTH), 0.01),
        'rg_wx': nrm((N_EVEN, RG_BLOCKS, RG_BW, RG_BW), RG_BW ** -0.5),
        'rg_bx': nrm((N_EVEN, RG_WIDTH), 0.01),
        'rg_lambda': jnp.log(lam_s) - jnp.log1p(-lam_s),
        'w_out_even': nrm((N_EVEN, EVEN_MIX, D_MODEL), EVEN_MIX ** -0.5 * BETA),
        'w_in_odd': nrm((N_ODD, D_MODEL, ODD_IN), D_MODEL ** -0.5),
        'gla_wg2': nrm((N_ODD, GLA_RANK, GLA_HEADS * GLA_DK), GLA_RANK ** -0.5),
        'gla_bg2': nrm((N_ODD, GLA_HEADS * GLA_DK), 0.01),
        'gla_norm_g': 1.0 + nrm((N_ODD, GLA_DV), 0.01),
        'hgrn_lb_logits': nrm((DEPTH, HGRN_HEADS * HGRN_DK), 0.1),
        'hgrn_norm_g': 1.0 + nrm((N_ODD, HGRN_DV), 0.01),
        'w_out_odd': nrm((N_ODD, ODD_MIX, D_MODEL), ODD_MIX ** -0.5 * BETA),
        'mlp_w1': nrm((DEPTH, D_MODEL, D_FF), D_MODEL ** -0.5),
        'mlp_w2': nrm((DEPTH, D_FF, D_MODEL), D_FF ** -0.5 * BETA),
        'ln1_g': 1.0 + nrm((DEPTH, D_MODEL), 0.01),
        'ln1_b': nrm((DEPTH, D_MODEL), 0.01),
        'ln2_g': 1.0 + nrm((DEPTH, D_MODEL), 0.01),
        'ln2_b': nrm((DEPTH, D_MODEL), 0.01),
    }


def reference(x_prompt, x_sample, state_ssd, state_ssd_conv, state_rglru, state_rglru_conv, state_gla,
              state_hgrn, meta_tokens, w_in_even, ssd_conv_w, ssd_conv_b, ssd_dt_bias, ssd_a_log, ssd_d,
              ssd_norm_g, rg_conv_w, rg_conv_b, rg_wa, rg_ba, rg_wx, rg_bx, rg_lambda, w_out_even, w_in_odd,
              gla_wg2, gla_bg2, gla_norm_g, hgrn_lb_logits, hgrn_norm_g, w_out_odd, mlp_w1, mlp_w2,
              ln1_g, ln1_b, ln2_g, ln2_b):
    p = {'w_in_even': w_in_even, 'ssd_conv_w': ssd_conv_w, 'ssd_conv_b': ssd_conv_b,
         'ssd_dt_bias': ssd_dt_bias, 'ssd_a_log': ssd_a_log, 'ssd_d': ssd_d, 'ssd_norm_g': ssd_norm_g,
         'rg_conv_w': rg_conv_w, 'rg_conv_b': rg_conv_b, 'rg_wa': rg_wa, 'rg_ba': rg_ba, 'rg_wx': rg_wx,
         'rg_bx': rg_bx, 'rg_lambda': rg_lambda, 'w_out_even': w_out_even, 'w_in_odd': w_in_odd,
         'gla_wg2': gla_wg2, 'gla_bg2': gla_bg2, 'gla_norm_g': gla_norm_g, 'hgrn_lb_logits': hgrn_lb_logits,
         'hgrn_norm_g': hgrn_norm_g, 'w_out_odd': w_out_odd, 'mlp_w1': mlp_w1, 'mlp_w2': mlp_w2,
         'ln1_g': ln1_g, 'ln1_b': ln1_b, 'ln2_g': ln2_g, 'ln2_b': ln2_b}
    bsz = x_prompt.shape[0]
    meta = jnp.broadcast_to(meta_tokens.astype(x_prompt.dtype)[None], (bsz, N_META, D_MODEL))
    h_p = jnp.concatenate([meta, x_prompt], axis=1)
    h_p, (p_ssd, p_ssd_conv, p_rglru, p_rglru_conv, p_gla, p_hgrn) = _trunk(
        h_p, (N_META, x_prompt.shape[1]), _zero_states(bsz, x_prompt.dtype), p)
    y_prompt = h_p[:, N_META:]
    y_sample, (s_ssd, s_ssd_conv, s_rglru, s_rglru_conv, s_gla, s_hgrn) = _trunk(
        x_sample, (x_sample.shape[1],),
        (state_ssd, state_ssd_conv, state_rglru, state_rglru_conv, state_gla, state_hgrn), p)
    return (y_prompt, y_sample, p_ssd, p_ssd_conv, p_rglru, p_rglru_conv, p_gla, p_hgrn,
            s_ssd, s_ssd_conv, s_rglru, s_rglru_conv, s_gla, s_hgrn)
```

```python
import contextlib
import numpy as np
import concourse.bass as bass
import concourse.mybir as mybir
from concourse.bass_utils import run_bass_kernel_spmd

F32 = mybir.dt.float32
BF16 = mybir.dt.bfloat16
AF = mybir.ActivationFunctionType
ALU = mybir.AluOpType

NCORES = 8
D = 1024
NTOK = 2192
DEPTH = 2
ALPHA = (2.0 * DEPTH) ** 0.25
LN_EPS = 1e-5
RMS_EPS = 1e-6
TILES = [(0, 16, "m")] + [(16 + 128 * j, 128, "p") for j in range(16)] + [(2064, 128, "s")]
CT512 = [(0, 512), (512, 512), (1024, 512), (1536, 512), (2048, 144)]

ENGS = ("pe", "act", "dve", "pool", "sp")
NDMA_SEMS = 32
NSW_SEMS = 8


class _Op:
    __slots__ = ("eng", "fn", "deps", "odeps", "dma", "idx", "signal", "semval", "dsem", "dprev", "cost", "lat", "tbl")

    def __init__(self, eng, fn, dma):
        self.eng = eng
        self.fn = fn
        self.dma = dma
        self.deps = set()
        self.odeps = set()
        self.signal = False
        self.semval = None
        self.dsem = None
        self.dprev = None
        self.cost = 0.3
        self.lat = 0.0
        self.tbl = None


def _rng(ap):
    t = ap.tensor
    if type(t).__name__.startswith("DRam"):
        return None
    row = 1
    for s in list(t.shape)[1:]:
        row *= int(s)
    lo = int(ap.offset) % row
    dims = sorted((abs(int(st)), int(cnt)) for (st, cnt) in list(ap.ap)[1:] if int(cnt) > 1 and int(st) != 0)
    ivs = [(lo, lo + 1)]
    for (st, cnt) in dims:
        ext = ivs[-1][1] - ivs[0][0]
        if st <= ext or len(ivs) * cnt > 32:
            ivs = [(ivs[0][0], ivs[-1][1] + (cnt - 1) * st)]
        else:
            ivs = [(a + k * st, b + k * st) for k in range(cnt) for (a, b) in ivs]
    if type(t).__name__.startswith("PSum"):
        ivs = sorted(set(((a // 512) * 512, ((b + 511) // 512) * 512) for (a, b) in ivs))
    return (t.name, ivs)


def _fsize(ap):
    n = 1
    for (st, cnt) in list(ap.ap)[1:]:
        n *= int(cnt)
    return n


_TBL = {AF.Exp: "le", AF.Ln: "le", AF.Sigmoid: "sg", AF.Silu: "si", AF.Gelu: "ge", AF.Sqrt: "sq", AF.Tanh: "th"}


class Prog:
    def __init__(self, nc):
        self.nc = nc
        self.ops = []
        self.acc = {}
        self.do_sched = True

    def _access(self, o, ap, is_write):
        r = _rng(ap)
        if r is None:
            return
        for (lo, hi) in r[1]:
            self._access1(o, ap, r[0], lo, hi, is_write)

    def _access1(self, o, ap, name, lo, hi, is_write):
        psum = type(ap.tensor).__name__.startswith("PSum")
        lst = self.acc.setdefault(name, [])
        keep = []
        for rec in lst:
            rlo, rhi, oi, w, eng, dma = rec
            overlap = (rlo < hi) and (lo < rhi)
            if overlap and (w or is_write or (psum and eng != o.eng)) and oi != o.idx:
                if o.eng == "pe" and eng == "pe":
                    o.odeps.add(oi)
                else:
                    o.deps.add(oi)
            if is_write and lo <= rlo and rhi <= hi:
                continue
            if (not is_write) and (not w) and rlo == lo and rhi == hi and eng == o.eng and not dma and not o.dma:
                if oi != o.idx:
                    o.odeps.add(oi)
                continue
            keep.append(rec)
        keep.append((lo, hi, o.idx, is_write, o.eng, o.dma))
        self.acc[name] = keep

    def op(self, eng, fn, ins=(), outs=(), dma=False, cost=None, lat=0.0, tbl=None):
        o = _Op(eng, fn, dma)
        o.tbl = tbl
        o.idx = len(self.ops)
        if cost is not None:
            o.cost = cost
        o.lat = lat
        self.ops.append(o)
        for a in ins:
            if a is not None and not isinstance(a, (int, float)):
                self._access(o, a, False)
        for a in outs:
            self._access(o, a, True)
        return o

    def schedule(self):
        import heapq
        ops = self.ops
        n = len(ops)
        succ = [[] for _ in range(n)]
        npred = [0] * n
        for o in ops:
            ps = set(o.deps) | set(o.odeps)
            npred[o.idx] = len(ps)
            for d in ps:
                succ[d].append(o.idx)
        est = [0.0] * n
        rank = [0.0] * n
        import os
        use_cp = os.environ.get("MK_CP", "1") == "1"
        for i_ in range(n - 1, -1, -1):
            o_ = ops[i_]
            m_ = 0.0
            for s_ in succ[i_]:
                if rank[s_] > m_:
                    m_ = rank[s_]
            rank[i_] = o_.cost + o_.lat + m_
        def key(i_):
            return (-rank[i_], i_) if use_cp else (i_, i_)
        fut = {e: [] for e in ENGS}
        avail = {e: [] for e in ENGS}
        free = {e: 0.0 for e in ENGS}
        for o in ops:
            if npred[o.idx] == 0:
                heapq.heappush(fut[o.eng], (0.0, o.idx))
        order = {e: [] for e in ENGS}
        self.tstart = [0.0] * n
        placed = 0
        SEMLAT = float(os.environ.get("MK_SEMLAT", "0.05"))
        TBL_WINDOW = int(os.environ.get("MK_TBLW", "2500"))
        TBL_COST = 1.3
        act_av = {}
        cur_tbl = [None]

        def act_pick():
            heads = [(h[0], k) for k, h in act_av.items() if h]
            if not heads:
                return None
            any_best = min(heads, key=lambda x: x[0])
            same = [x for x in heads if x[1] is None or x[1] == cur_tbl[0]]
            if same:
                sb = min(same, key=lambda x: x[0])
                if sb[0][1] <= any_best[0][1] + TBL_WINDOW:
                    return (sb[0][1], sb[1])
            return (any_best[0][1], any_best[1])

        while placed < n:
            best = None
            for e in ENGS:
                f, a = fut[e], avail[e]
                if e == "act":
                    while f and f[0][0] <= free[e]:
                        ii = heapq.heappop(f)[1]
                        heapq.heappush(act_av.setdefault(ops[ii].tbl, []), (key(ii), ii))
                    pk = act_pick()
                    if pk is not None:
                        cand = (free[e], pk[0], e, True)
                    elif f:
                        cand = (f[0][0], f[0][1], e, False)
                    else:
                        continue
                else:
                    while f and f[0][0] <= free[e]:
                        i2 = heapq.heappop(f)[1]
                        heapq.heappush(a, (key(i2), i2))
                    if a:
                        cand = (free[e], a[0][1], e, True)
                    elif f:
                        cand = (f[0][0], f[0][1], e, False)
                    else:
                        continue
                if best is None or cand[:2] < best[:2]:
                    best = cand
            start, idx, e, from_avail = best
            if e == "act":
                if from_avail:
                    heapq.heappop(act_av[ops[idx].tbl])
                else:
                    heapq.heappop(fut[e])
                t_ = ops[idx].tbl
                if t_ is not None and t_ != cur_tbl[0]:
                    start += TBL_COST
                    cur_tbl[0] = t_
            elif from_avail:
                heapq.heappop(avail[e])
            else:
                heapq.heappop(fut[e])
            o = ops[idx]
            self.tstart[idx] = start
            fin_eng = start + o.cost
            free[e] = fin_eng
            fin = fin_eng + o.lat
            order[e].append(o)
            placed += 1
            for sidx in succ[idx]:
                so = ops[sidx]
                t = fin + (SEMLAT if idx in so.deps else 0.0)
                if idx in so.odeps and idx not in so.deps:
                    t = start
                if t > est[sidx]:
                    est[sidx] = t
                npred[sidx] -= 1
                if npred[sidx] == 0:
                    heapq.heappush(fut[so.eng], (est[sidx], sidx))
        self.est_total = max(free.values())
        return order

    def emit(self):
        nc = self.nc
        ops = self.ops
        if self.do_sched:
            per_eng = self.schedule()
        else:
            per_eng = {e: [o for o in ops if o.eng == e] for e in ENGS}
        for o in ops:
            for d in o.deps:
                ops[d].signal = True
        for e in ENGS:
            last = [o for o in per_eng[e] if not o.dma]
            if last:
                last[-1].signal = True
        cnt = {e: 0 for e in ENGS}
        dcnt = [0] * NDMA_SEMS
        dlast = [None] * NDMA_SEMS
        rr = {"sw": 0, "hw": 0}
        for e in ENGS:
            for o in per_eng[e]:
                if o.dma:
                    if e == "pool":
                        sidx = rr["sw"] % NSW_SEMS
                        rr["sw"] += 1
                    else:
                        sidx = NSW_SEMS + rr["hw"] % (NDMA_SEMS - NSW_SEMS)
                        rr["hw"] += 1
                    o.dsem = sidx
                    o.dprev = dlast[sidx]
                    dlast[sidx] = o.idx
                    dcnt[sidx] += 16
                    o.semval = dcnt[sidx]
                elif o.signal:
                    cnt[e] += 1
                    o.semval = cnt[e]
        hw_queues = [e for e in ENGS if e != "pool" and any(o.dma for o in per_eng[e])]
        assert len(hw_queues) <= 1, hw_queues
        with contextlib.ExitStack() as st:
            esem = {e: st.enter_context(nc.semaphore("s_" + e)) for e in ENGS}
            dsem = [st.enter_context(nc.semaphore("d_%d" % i)) for i in range(NDMA_SEMS)]
            block = st.enter_context(nc.Block())

            def semof(o):
                if o.dma:
                    return ("d", o.dsem), dsem[o.dsem]
                return ("e", o.eng), esem[o.eng]

            def run(eng_name, eng):
                waited = {}
                for o in per_eng[eng_name]:
                    need = {}
                    deps = set(o.deps)
                    if o.dma and o.dprev is not None:
                        deps.add(o.dprev)
                    for d in deps:
                        dop = ops[d]
                        k, s = semof(dop)
                        if need.get(k, (None, 0))[1] < dop.semval:
                            need[k] = (s, dop.semval)
                    for k, (s, v) in need.items():
                        if waited.get(k, 0) >= v:
                            continue
                        eng.wait_ge(s, v)
                        waited[k] = v
                    ins = o.fn(eng)
                    if o.dma:
                        ins.then_inc(dsem[o.dsem], 16)
                    elif o.signal:
                        ins.then_inc(esem[eng_name], 1)
                if eng_name == "sp":
                    for i in range(NDMA_SEMS):
                        if dcnt[i] > 0 and waited.get(("d", i), 0) < dcnt[i]:
                            eng.wait_ge(dsem[i], dcnt[i])
                    for e in ENGS:
                        if e != eng_name and cnt[e] > 0:
                            eng.wait_ge(esem[e], cnt[e])

            @block.sync
            def _(e):
                run("sp", e)

            @block.tensor
            def _(e):
                run("pe", e)

            @block.scalar
            def _(e):
                run("act", e)

            @block.vector
            def _(e):
                run("dve", e)

            @block.gpsimd
            def _(e):
                run("pool", e)


def _cols(v):
    v = np.asarray(v, np.float32).reshape(-1, 128)
    return np.ascontiguousarray(v.T)


PC = {}
_pc_n = 0
for _nm, _n in [("ssd_cw", 48), ("ssd_cb", 12), ("rg_cw", 32), ("rg_cb", 8), ("rg_ba", 8), ("rg_bx", 8),
                ("rg_lam", 8), ("gla_bg2", 4), ("gla_ng", 2), ("lb0", 8), ("lb1", 8), ("hg_ng", 1),
                ("ln1g", 16), ("ln1b", 16), ("ln2g", 16), ("ln2b", 16)]:
    PC[_nm] = _pc_n
    _pc_n += _n
NPC = _pc_n
DC = {"rg_sc": 0, "lb": 8, "oml": 16, "noml": 24, "nbg2": 32}
NDC = 36
PR = {"dtb": 0, "alog": 16, "dd": 32, "ng": 48}
NPR = 48 + 1024
CM = {"ident": 0, "mcumP": 128, "mcumS": 256, "mlastS": 384, "mc64": 512, "rst64": 640, "rst8": 768,
      "ones": 896, "seqm": 1024, "rst128": 1040}
NCM = 1040 + 512


def _pack_pcol(inp):
    parts = []
    cw = np.asarray(inp["ssd_conv_w"][0], np.float32)
    parts.append(np.ascontiguousarray(cw.reshape(4, 12, 128).transpose(2, 1, 0)).reshape(128, 48))
    parts.append(_cols(inp["ssd_conv_b"][0]))
    rw = np.asarray(inp["rg_conv_w"][0], np.float32)
    parts.append(np.ascontiguousarray(rw.reshape(4, 8, 128).transpose(2, 1, 0)).reshape(128, 32))
    parts.append(_cols(inp["rg_conv_b"][0]))
    parts.append(_cols(inp["rg_ba"][0]))
    parts.append(_cols(inp["rg_bx"][0]))
    parts.append(_cols(inp["rg_lambda"][0]))
    parts.append(_cols(inp["gla_bg2"][0]))
    parts.append(_cols(inp["gla_norm_g"][0]))
    parts.append(_cols(inp["hgrn_lb_logits"][0]))
    parts.append(_cols(inp["hgrn_lb_logits"][1]))
    parts.append(_cols(inp["hgrn_norm_g"][0]))
    for nm in ("ln1_g", "ln1_b", "ln2_g", "ln2_b"):
        parts.append(np.concatenate([_cols(inp[nm][0]), _cols(inp[nm][1])], axis=1))
    out = np.ascontiguousarray(np.concatenate(parts, axis=1), dtype=np.float32)
    assert out.shape == (128, NPC), out.shape
    return out


def _pack_prow(inp):
    row = np.concatenate([np.asarray(inp["ssd_dt_bias"][0], np.float32), np.asarray(inp["ssd_a_log"][0], np.float32),
                          np.asarray(inp["ssd_d"][0], np.float32), np.asarray(inp["ssd_norm_g"][0], np.float32)])
    return np.ascontiguousarray(np.broadcast_to(row[None, :], (128, NPR)), dtype=np.float32)


def _const_masks():
    r = np.arange(128)[:, None]
    t = np.arange(128)[None, :]
    m = np.zeros((128, NCM), np.float32)
    m[:, CM["ident"]:CM["ident"] + 128] = (r == t)
    m[:, CM["mcumP"]:CM["mcumP"] + 128] = (r <= t)
    m[:, CM["mcumS"]:CM["mcumS"] + 128] = (r <= t) & (r // 8 == t // 8)
    m[:, CM["mlastS"]:CM["mlastS"] + 128] = (r // 8 == t // 8)
    m[:, CM["mc64"]:CM["mc64"] + 128] = (r <= t) & (r // 64 == t // 64)
    m[:, CM["rst64"]:CM["rst64"] + 128] = (t % 64 != 0)
    m[:, CM["rst8"]:CM["rst8"] + 128] = (t % 8 != 0)
    m[:, CM["ones"]:CM["ones"] + 128] = 1.0
    m[:, CM["seqm"]:CM["seqm"] + 16] = (r // 8 == np.arange(16)[None, :])
    m[:, CM["rst128"]:CM["rst128"] + 512] = (np.arange(512)[None, :] % 128 != 0)
    return m


class Builder:
    def __init__(self):
        self.nc = bass.Bass("TRN2", target_bir_lowering=False)
        self.P = Prog(self.nc)
        self.st = contextlib.ExitStack()

    def din(self, name, shape):
        return self.nc.dram_tensor(name, list(shape), F32, kind="ExternalInput").ap()

    def dout(self, name, shape):
        return self.nc.dram_tensor(name, list(shape), F32, kind="ExternalOutput").ap()

    def sb(self, name, shape, dt):
        return self.st.enter_context(self.nc.sbuf_tensor(name, list(shape), dt))

    def psum(self, name, shape, dt=F32):
        return self.st.enter_context(self.nc.psum_tensor(name, list(shape), dt))

    def mm(self, out, lhsT, rhs, start=True, stop=True):
        c = 0.03 + _fsize(rhs) / 2400.0
        if rhs.dtype == F32:
            c *= 4.0
        self.P.op("pe", lambda e: e.matmul(out, lhsT=lhsT, rhs=rhs, start=start, stop=stop),
                  ins=[lhsT, rhs], outs=[out], cost=max(c, 0.064))

    def tr(self, out, in_):
        k = in_.shape[0]
        ident = self.cm[0:k, CM["ident"]:CM["ident"] + k]
        self.P.op("pe", lambda e: e.transpose(out=out, in_=in_, identity=ident), ins=[in_, ident], outs=[out], cost=0.12)

    def act(self, out, in_, func, bias=None, scale=None, accum=None):
        kw = {}
        if bias is not None:
            kw["bias"] = bias
        if scale is not None:
            kw["scale"] = scale
        if accum is not None:
            kw["accum_out"] = accum
        outs = [out] + ([accum] if accum is not None else [])
        self.P.op("act", lambda e: e.activation(out=out, in_=in_, func=func, **kw), ins=[in_, bias, scale], outs=outs,
                  cost=0.25 + _fsize(out) * 0.00075, tbl=_TBL.get(func))

    def tt(self, eng, out, a, b, op):
        self.P.op(eng, lambda e: e.tensor_tensor(out=out, in0=a, in1=b, op=op), ins=[a, b], outs=[out],
                  cost=self._ecost(eng, out, 1.0))

    def ts(self, eng, out, a, s1, op0, s2=None, op1=None):
        if op1 is None:
            self.P.op(eng, lambda e: e.tensor_scalar(out=out, in0=a, scalar1=s1, scalar2=None, op0=op0),
                      ins=[a, s1], outs=[out], cost=self._ecost(eng, out, 0.6))
        else:
            self.P.op(eng, lambda e: e.tensor_scalar(out=out, in0=a, scalar1=s1, scalar2=s2, op0=op0, op1=op1),
                      ins=[a, s1, s2], outs=[out], cost=self._ecost(eng, out, 0.6))

    def stt(self, out, a, s, b, op0, op1):
        self.P.op("dve", lambda e: e.scalar_tensor_tensor(out=out, in0=a, scalar=s, in1=b, op0=op0, op1=op1),
                  ins=[a, s, b], outs=[out], cost=self._ecost("dve", out, 1.2))

    def cp(self, eng, out, in_):
        if eng == "act":
            self.P.op("act", lambda e: e.copy(out=out, in_=in_), ins=[in_], outs=[out], cost=0.25 + _fsize(out) * 0.00075)
        else:
            self.P.op(eng, lambda e: e.tensor_copy(out=out, in_=in_), ins=[in_], outs=[out],
                      cost=self._ecost(eng, out, 1.5 if eng == "pool" else 0.7))

    def scan(self, out, d0, d1, init):
        self.P.op("dve", lambda e: e.tensor_tensor_scan(out=out, data0=d0, data1=d1, initial=init,
                                                        op0=ALU.mult, op1=ALU.add), ins=[d0, d1, init], outs=[out],
                  cost=0.1 + _fsize(out) * 0.0021)

    def recip(self, out, in_):
        self.P.op("dve", lambda e: e.reciprocal(out=out, in_=in_), ins=[in_], outs=[out], cost=0.1 + _fsize(out) * 0.0065)

    def memset(self, eng, out, val):
        self.P.op(eng, lambda e: e.memset(out, val), outs=[out], cost=0.1 + _fsize(out) * 0.0006)

    def dma(self, q, out, in_):
        if q != "pool":
            q = "sp"
        nbytes = _fsize(out) * 4 * int(out.shape[0])
        self.P.op(q, lambda e: e.dma_start(out=out, in_=in_), ins=[in_], outs=[out], dma=True,
                  cost=(1.0 if q == "pool" else 0.08), lat=2.0 + nbytes / 150000.0)

    def _ecost(self, eng, out, f):
        n = _fsize(out)
        if eng == "pool":
            return 0.12 + n * 0.0023 * max(f, 0.6)
        return 0.08 + n * 0.00105 * max(f, 0.55) / 0.55 * 0.55 if f <= 0.7 else 0.08 + n * 0.00105 * f

    def build(self):
        nc = self.nc
        I = {}
        I["xp"] = self.din("xp", [2048, D])
        I["xs"] = self.din("xs", [128, D])
        I["meta"] = self.din("meta", [16, D])
        I["st_ssd"] = self.din("st_ssd", [16, 1024, 128])
        I["st_ssdc"] = self.din("st_ssdc", [48, 1536])
        I["st_rg"] = self.din("st_rg", [16, 1024])
        I["st_rgc"] = self.din("st_rgc", [48, 1024])
        I["st_gla"] = self.din("st_gla", [16, 4, 128, 256])
        I["st_hg"] = self.din("st_hg", [16, 8, 128, 128])
        I["w_in_e"] = self.din("w_in_e", [D, 4624])
        I["w_out_e"] = self.din("w_out_e", [2048, D])
        I["w_in_o"] = self.din("w_in_o", [D, 7184])
        I["w_out_o"] = self.din("w_out_o", [2048, D])
        I["w1"] = self.din("w1", [2, D, 4096])
        I["w2"] = self.din("w2", [2, 4096, D])
        I["rg_wa"] = self.din("rg_wa", [8, 128, 128])
        I["rg_wx"] = self.din("rg_wx", [8, 128, 128])
        I["wg2"] = self.din("wg2", [16, 512])
        I["pcol"] = self.din("pcol", [128, NPC])
        I["prow"] = self.din("prow", [128, NPR])
        I["cmask"] = self.din("cmask", [128, NCM])
        O = {}
        O["y_p"] = self.dout("y_p", [2048, D])
        O["y_s"] = self.dout("y_s", [128, D])
        O["ssd_p"] = self.dout("ssd_p", [1024, 128])
        O["ssd_s"] = self.dout("ssd_s", [16, 1024, 128])
        O["ssdc"] = self.dout("ssdc", [51, 1536])
        O["rg"] = self.dout("rg", [17, 1024])
        O["rgc"] = self.dout("rgc", [51, 1024])
        O["gla_p"] = self.dout("gla_p", [4, 128, 256])
        O["gla_s"] = self.dout("gla_s", [16, 4, 128, 256])
        O["hg_p"] = self.dout("hg_p", [8, 128, 128])
        O["hg_s"] = self.dout("hg_s", [16, 8, 128, 128])
        self.I, self.O = I, O

        self.h32 = self.sb("h32", [128, 8, NTOK], F32)
        self.hb = self.sb("hb", [128, 8, NTOK], BF16)
        self.wbuf = self.sb("wbuf", [128, 16896], BF16)
        self.pcol = self.sb("pcol_s", [128, NPC], F32)
        self.dcol = self.sb("dcol_s", [128, NDC], F32)
        self.prow = self.sb("prow_s", [128, NPR], F32)
        self.arow = self.sb("arow_s", [128, 16], F32)
        self.cm = self.sb("cm_s", [128, NCM], F32)
        self.AF = self.sb("arenaF", [128, 10240], F32)
        self.AB = self.sb("arenaB", [128, 9216], BF16)
        self.onesb = self.sb("onesb", [128, 128], BF16)
        self.ps = [self.psum("ps%d" % i, [128, 1024]) for i in range(4)]
        self.af_off = 0
        self.ab_off = 0

        import os
        stop = int(os.environ.get("MK_STOP", "99"))
        phases = [self.setup, self.load_inputs,
                  lambda: self.ssd_unit(0, True), lambda: self.ssd_unit(1, False),
                  lambda: [self.rg_unit(b) for b in range(8)],
                  lambda: self.ln(self.PCcol("ln1g", 0), self.PCcol("ln1b", 0)),
                  lambda: self.mlp(0),
                  lambda: self.ln(self.PCcol("ln2g", 0), self.PCcol("ln2b", 0)),
                  self.layer1,
                  lambda: self.ln(self.PCcol("ln1g", 8), self.PCcol("ln1b", 8)),
                  lambda: self.mlp(1),
                  lambda: self.ln(self.PCcol("ln2g", 8), self.PCcol("ln2b", 8), final=True)]
        for pi, ph in enumerate(phases):
            if pi > stop:
                break
            ph()
        self.store_outputs()
        self.P.emit()
        self.st.close()
        return nc

    def reset_arena(self):
        self.af_off = 0
        self.ab_off = 0

    def fa(self, n):
        o = self.af_off
        self.af_off += n
        assert self.af_off <= 10240, self.af_off
        return self.AF[:, o:o + n]

    def ba(self, n):
        o = self.ab_off
        self.ab_off += n + (n % 2)
        assert self.ab_off <= 9216, self.ab_off
        return self.AB[:, o:o + n]

    def bank(self, i):
        return self.ps[i // 2][:, (i % 2) * 512:(i % 2) * 512 + 512]

    def PCcol(self, name, j=0):
        o = PC[name] + j
        return o

    def pc(self, name, j=0, p=128):
        o = PC[name] + j
        return self.pcol[0:p, o:o + 1]

    def dc(self, name, j=0):
        o = DC[name] + j
        return self.dcol[:, o:o + 1]

    def setup(self):
        I = self.I
        self.dma("sp", self.pcol[:], I["pcol"])
        self.dma("sp", self.prow[:], I["prow"])
        self.dma("sp", self.cm[:], I["cmask"])
        self.memset("pool", self.onesb[:, :], 1.0)
        self.act(self.arow[:], self.prow[:, PR["alog"]:PR["alog"] + 16], AF.Exp)
        self.ts("dve", self.arow[:], self.arow[:], -1.0, ALU.mult)
        sc = self.dcol[:, DC["rg_sc"]:DC["rg_sc"] + 8]
        self.act(sc, self.pcol[:, PC["rg_lam"]:PC["rg_lam"] + 8], AF.Exp, scale=-1.0)
        self.act(sc, sc, AF.Ln, bias=1.0)
        self.ts("dve", sc, sc, -8.0, ALU.mult)
        lb = self.dcol[:, DC["lb"]:DC["lb"] + 8]
        oml = self.dcol[:, DC["oml"]:DC["oml"] + 8]
        noml = self.dcol[:, DC["noml"]:DC["noml"] + 8]
        self.tt("dve", lb, self.pcol[:, PC["lb1"]:PC["lb1"] + 8], self.pcol[:, PC["lb0"]:PC["lb0"] + 8], ALU.subtract)
        self.act(lb, lb, AF.Sigmoid)
        self.ts("dve", oml, lb, -1.0, ALU.mult, 1.0, ALU.add)
        self.ts("dve", noml, lb, -1.0, ALU.add)
        self.ts("dve", self.dcol[:, DC["nbg2"]:DC["nbg2"] + 4], self.pcol[:, PC["gla_bg2"]:PC["gla_bg2"] + 4], -1.0, ALU.mult)

    def load_inputs(self):
        I = self.I
        self.reset_arena()
        stg = [self.fa(1024), self.fa(1024)]
        import os
        tsel = os.environ.get("MK_TSEL", "")
        for ti, (c0, n, kind) in enumerate(TILES):
            if tsel and str(ti) not in tsel.split(","):
                continue
            s = stg[ti % 2]
            if kind == "m":
                src = I["meta"]
            elif kind == "p":
                r0 = c0 - 16
                src = I["xp"][r0:r0 + n, :]
            else:
                src = I["xs"]
            import os
            li = int(os.environ.get("MK_LI", "9"))
            self.dma("sp", s[0:n, :], src)
            pp = self.ps[ti % 2]
            if li >= 1:
                for c in range(8):
                    self.tr(pp[:, c * 128:c * 128 + n], s[0:n, c * 128:(c + 1) * 128])
            pv = pp[:, :].rearrange("p (c t) -> p c t", c=8)[:, :, 0:n]
            if li >= 2 and li != 6:
                self.cp("act", self.h32[:, :, c0:c0 + n], pv)
            if li == 6:
                self.cp("dve", self.h32[:, :, c0:c0 + n], pv)
            if li == 3:
                self.cp("dve", self.hb[:, :, c0:c0 + n], self.h32[:, :, c0:c0 + n])
            if li == 4:
                self.cp("pool", self.hb[:, :, c0:c0 + n], self.h32[:, :, c0:c0 + n])
            if li == 5:
                for c in range(8):
                    self.cp("dve", self.hb[:, c, c0:c0 + n], pp[:, c * 128:c * 128 + n])
            if li >= 9:
                self.cp("dve", self.hb[:, :, c0:c0 + n], pv)

    def wload(self, dst, src):
        self.dma("pool", dst, src)

    def wview(self, off, kc, ncol):
        return self.wbuf[:, off:off + kc * ncol].rearrange("p (k j) -> p k j", k=kc)

    def inproj_fm(self, out_ps, W, kcs, colsl, c0, n):
        for k in range(kcs):
            self.mm(out_ps, W[:, k, colsl], self.hb[:, k, c0:c0 + n], start=(k == 0), stop=(k == kcs - 1))

    def inproj_tm(self, out_ps, W, colsl, c0, n):
        for k in range(8):
            self.mm(out_ps, self.hb[:, k, c0:c0 + n], W[:, k, colsl], start=(k == 0), stop=(k == 7))

    def outproj_acc(self, Wo, kcs, yT, c0, n, first):
        po = self.ps[0]
        for oc in range(8):
            for k in range(kcs):
                self.mm(po[:, oc * 128:oc * 128 + n], Wo[:, k, oc * 128:(oc + 1) * 128], yT[:, k, 0:n],
                        start=(k == 0), stop=(k == kcs - 1))
        pv = po[:, :].rearrange("p (c t) -> p c t", c=8)[:, :, 0:n]
        hv = self.h32[:, :, c0:c0 + n]
        if first:
            self.stt(hv, hv, ALPHA, pv, ALU.mult, ALU.add)
        else:
            self.tt("dve", hv, hv, pv, ALU.add)

    def outproj_wide(self, Wo, kcs, yT, c0, n, first, banks=(0, 1, 6, 7)):
        for oc in range(8):
            po = self.bank(banks[oc % len(banks)])
            for k in range(kcs):
                self.mm(po[:, 0:n], Wo[:, k, oc * 128:(oc + 1) * 128], yT[:, k, 0:n], start=(k == 0), stop=(k == kcs - 1))
            hv = self.h32[:, oc, c0:c0 + n]
            if first:
                self.stt(hv, hv, ALPHA, po[:, 0:n], ALU.mult, ALU.add)
            else:
                self.tt("dve", hv, hv, po[:, 0:n], ALU.add)

    def layer0(self):
        for g in range(2):
            self.ssd_unit(g, first=(g == 0))
        for blk in range(8):
            self.rg_unit(blk)

    def ssd_unit(self, g, first):
        I, O = self.I, self.O
        self.reset_arena()
        cm = self.cm
        Wz = self.wview(0, 8, 512)
        Wx = self.wview(4096, 8, 512)
        WB = self.wview(8192, 8, 128)
        WC = self.wview(9216, 8, 128)
        Wdt = self.wview(10240, 8, 8)
        Wo = self.wview(10304, 4, 1024)
        wie = I["w_in_e"]

        def wsrc(c0, ncol):
            return wie[:, c0:c0 + ncol].rearrange("(k p) j -> p k j", p=128)
        self.wload(Wx, wsrc(1024 + g * 512, 512))
        self.wload(WB, wsrc(2048 + g * 128, 128))
        self.wload(WC, wsrc(2304 + g * 128, 128))
        self.wload(Wz, wsrc(g * 512, 512))
        self.wload(Wdt, wsrc(2560 + g * 8, 8))
        self.wload(Wo, I["w_out_e"][g * 512:(g + 1) * 512, :].rearrange("(k p) j -> p k j", p=128))
        cch = [g * 4 + i for i in range(4)] + [8 + g, 10 + g]
        ub = [self.fa(1056), self.fa(1056)]
        xc = self.fa(6 * 128).rearrange("p (c t) -> p c t", c=6)
        xtok = self.fa(512)
        zs = self.fa(512)
        cbm = self.fa(128)
        segc = self.fa(512)
        dec = self.fa(512)
        yy = self.fa(512)
        t1 = self.fa(512)
        S = self.fa(512)
        stg = self.fa(512)
        sm = self.fa(128)
        ebl = self.fa(8)
        tail = self.fa(6 * 51).rearrange("p (c t) -> p c t", c=6)
        cst = self.fa(1536)
        ss = self.fa(2)
        ebl_all = self.fa(128)
        BT = self.ba(128)
        CT = self.ba(128)
        Btok = self.ba(128)
        xd = self.ba(512)
        xdte = self.ba(512)
        MT = self.ba(1024)
        yT = self.ba(512).rearrange("p (k t) -> p k t", k=4)
        Sb = self.ba(512)
        Sb2 = self.ba(512)
        cpf = self.ba(2176)
        Cpad = cpf[:, 0:2048].rearrange("p (s t) -> p s t", s=16)
        Cdiag = cpf[:, 0:2176].rearrange("p (a b) -> p a b", b=136)[:, :, 0:8]
        Bm = self.ba(2048).rearrange("p (s t) -> p s t", s=16)
        dt_, la, nla, bb, blb, eb, te, dtte = [sm[:, 8 * i:8 * i + 8] for i in range(8)]

        self.memset("dve", S, 0.0)
        self.memset("dve", Sb, 0.0)
        self.memset("dve", ub[1][:, 0:1056], 0.0)
        self.dma("sp", cst[0:48, :], I["st_ssdc"])

        prev_n = None
        for ti, (c0, n, kind) in enumerate(TILES):
            u = ub[ti % 2]
            up = ub[(ti + 1) % 2]
            if kind == "s":
                uv = u[:, 0:1056].rearrange("p (c s t) -> p c s t", c=6, s=16)
                pt = self.bank(4)
                for i, ch in enumerate(cch):
                    self.tr(pt[:, i * 48:(i + 1) * 48], cst[0:48, ch * 128:(ch + 1) * 128])
                self.cp("act", uv[:, :, :, 0:3],
                        pt[:, 0:288].rearrange("p (c s t) -> p c s t", c=6, s=16))
            else:
                uv = u[:, 0:6 * 131].rearrange("p (c t) -> p c t", c=6)
                if prev_n is not None:
                    upv = up[:, 0:6 * 131].rearrange("p (c t) -> p c t", c=6)
                    self.cp("pool", uv[:, :, 0:3], upv[:, :, prev_n:prev_n + 3])
                else:
                    self.memset("pool", uv[:, :, 0:3], 0.0)
            for i in range(6):
                pb = self.bank(2 + (i % 2))
                if i < 4:
                    self.inproj_fm(pb[:, 0:n], Wx, 8, slice(i * 128, (i + 1) * 128), c0, n)
                elif i == 4:
                    self.inproj_fm(pb[:, 0:n], WB, 8, slice(0, 128), c0, n)
                else:
                    self.inproj_fm(pb[:, 0:n], WC, 8, slice(0, 128), c0, n)
                if kind == "s":
                    self.cp("act", uv[:, i, :, 3:11], pb[:, 0:128].rearrange("p (s t) -> p s t", s=16))
                else:
                    self.cp("act", uv[:, i, 3:3 + n], pb[:, 0:n])
            for i, ch in enumerate(cch):
                w = [self.pcol[:, PC["ssd_cw"] + ch * 4 + j:PC["ssd_cw"] + ch * 4 + j + 1] for j in range(4)]
                bcol = self.pcol[:, PC["ssd_cb"] + ch:PC["ssd_cb"] + ch + 1]
                if kind == "s":
                    o_ = xc[:, i, :].rearrange("p (s t) -> p s t", s=16)
                    src = lambda j: uv[:, i, :, j:j + 8]
                else:
                    o_ = xc[:, i, 0:n]
                    src = lambda j: uv[:, i, j:j + n]
                self.ts("dve", o_, src(3), w[3], ALU.mult, bcol, ALU.add)
                for j in (2, 1, 0):
                    self.stt(o_, src(j), w[j], o_, ALU.mult, ALU.add)
            self.act(xc[:, 0:4, 0:n], xc[:, 0:4, 0:n], AF.Silu)
            self.act(BT[:, 0:n], xc[:, 4, 0:n], AF.Silu)
            self.act(CT[:, 0:n], xc[:, 5, 0:n], AF.Silu)
            self.act(xc[:, 4, 0:n], xc[:, 4, 0:n], AF.Silu)
            pt = self.bank(4)
            for i in range(4):
                self.tr(pt[0:n, i * 128:(i + 1) * 128], xc[:, i, 0:n])
            self.cp("act", xtok[0:n, :], pt[0:n, 0:512])
            pt2 = self.bank(5)
            self.tr(pt2[0:n, 0:128], xc[:, 4, 0:n])
            self.cp("act", Btok[0:n, :], pt2[0:n, 0:128])
            pd = self.bank(5)
            self.inproj_tm(pd[0:n, 128:136], Wdt, slice(0, 8), c0, n)
            self.tt("dve", dt_[0:n, :], pd[0:n, 128:136], self.prow[0:n, PR["dtb"] + g * 8:PR["dtb"] + g * 8 + 8], ALU.add)
            self.act(dt_[0:n, :], dt_[0:n, :], AF.Exp)
            self.act(dt_[0:n, :], dt_[0:n, :], AF.Ln, bias=1.0)
            self.tt("dve", la[0:n, :], dt_[0:n, :], self.arow[0:n, g * 8:g * 8 + 8], ALU.mult)
            self.ts("dve", nla[0:n, :], la[0:n, :], -1.0, ALU.mult)
            pz = self.bank(6)
            self.inproj_tm(pz[0:n, 0:512], Wz, slice(0, 512), c0, n)
            self.act(zs[0:n, :], pz[0:n, 0:512], AF.Silu)
            if kind == "s":
                mcum = cm[0:n, CM["mcumS"]:CM["mcumS"] + n]
                mlast = cm[0:n, CM["mlastS"]:CM["mlastS"] + n]
            else:
                mcum = cm[0:n, CM["mcumP"]:CM["mcumP"] + n]
                mlast = cm[0:n, CM["ones"]:CM["ones"] + n]
            pc_ = self.bank(5)
            self.mm(pc_[0:n, 256:264], mcum, la[0:n, :])
            self.mm(pc_[0:n, 264:272], mlast, la[0:n, :])
            self.cp("dve", bb[0:n, :], pc_[0:n, 256:264])
            self.act(eb[0:n, :], pc_[0:n, 256:264], AF.Exp)
            self.tt("dve", te[0:n, :], pc_[0:n, 264:272], bb[0:n, :], ALU.subtract)
            self.act(te[0:n, :], te[0:n, :], AF.Exp)
            self.tt("dve", dtte[0:n, :], dt_[0:n, :], te[0:n, :], ALU.mult)
            xv = xtok[0:n, :].rearrange("p (h d) -> p h d", h=8)
            self.tt("dve", xd[0:n, :].rearrange("p (h d) -> p h d", h=8), xv,
                    dt_[0:n, :].unsqueeze(2).broadcast_to([n, 8, 64]), ALU.mult)
            self.tt("dve", xdte[0:n, :].rearrange("p (h d) -> p h d", h=8), xv,
                    dtte[0:n, :].unsqueeze(2).broadcast_to([n, 8, 64]), ALU.mult)
            pcb = self.bank(5)
            self.mm(pcb[0:n, 384:384 + n], BT[:, 0:n], CT[:, 0:n])
            self.tt("dve", cbm[0:n, 0:n], pcb[0:n, 384:384 + n], mcum, ALU.mult)
            py = self.bank(7)
            for q in range(2):
                psg = self.bank(2 + q)
                for hh in range(4):
                    h = q * 4 + hh
                    o_ = psg[0:n, hh * 128:hh * 128 + n]
                    self.mm(o_, la[0:n, h:h + 1].broadcast_to([n, n]), mcum, start=True, stop=False)
                    self.mm(o_, mcum, nla[0:n, h:h + 1].broadcast_to([n, n]), start=False, stop=True)
                sv = psg[0:n, :].rearrange("p (h t) -> p h t", h=4)[:, :, 0:n]
                segv = segc[0:n, :].rearrange("p (h t) -> p h t", h=4)[:, :, 0:n]
                decv = dec[0:n, :].rearrange("p (h t) -> p h t", h=4)[:, :, 0:n]
                mtv = MT[0:n, q * 512:(q + 1) * 512].rearrange("p (h t) -> p h t", h=4)[:, :, 0:n]
                self.ts("dve", segv, sv, 0.0, ALU.min)
                self.act(decv, segv, AF.Exp)
                self.tt("dve", mtv, decv, cbm[0:n, 0:n].unsqueeze(1).broadcast_to([n, 4, n]), ALU.mult)
                for hh in range(4):
                    h = q * 4 + hh
                    self.mm(py[0:n, h * 64:(h + 1) * 64], MT[0:n, q * 512 + hh * 128:q * 512 + hh * 128 + n],
                            xd[0:n, h * 64:(h + 1) * 64])
            pyi = self.bank(6)
            pu = self.bank(3)
            if kind != "s":
                self.mm(pyi[0:n, 0:512], CT[:, 0:n], Sb[:, :])
                pe_ = self.bank(5)
                self.mm(pe_[:, 272:280], cm[0:n, CM["ones"]:CM["ones"] + 128], la[0:n, :])
                self.act(ebl[:, :], pe_[:, 272:280], AF.Exp)
                self.mm(pu[:, 0:512], Btok[0:n, :], xdte[0:n, :])
                self.tt("dve", S.rearrange("p (h d) -> p h d", h=8), S.rearrange("p (h d) -> p h d", h=8),
                        ebl[:, :].unsqueeze(2).broadcast_to([128, 8, 64]), ALU.mult)
                self.tt("dve", S, S, pu[:, 0:512], ALU.add)
                self.cp("act", Sb, S)
            else:
                self.memset("pool", Cpad[:, :, :], 0.0)
                self.cp("pool", Cdiag, CT[:, 0:128].rearrange("p (s t) -> p s t", s=16))
                self.tt("pool", Bm[:, :, :], Btok[:, :].unsqueeze(1).broadcast_to([128, 16, 128]),
                        cm[:, CM["seqm"]:CM["seqm"] + 16].unsqueeze(2).broadcast_to([128, 16, 128]), ALU.mult)
                pe_ = self.bank(5)
                for s in range(16):
                    self.mm(pe_[:, s * 8:(s + 1) * 8], cm[:, CM["seqm"] + s:CM["seqm"] + s + 1].broadcast_to([128, 128]), la[:, :])
                self.act(ebl_all[:, :], pe_[:, 0:128], AF.Exp)
                for s in range(16):
                    par = s % 2
                    sin = (stg, segc)[par][:, 0:512].rearrange("p (q n) -> p q n", q=4)
                    Ss = (S, dec)[par]
                    Sbs = (Sb, Sb2)[par]
                    outs_ = (t1, cst[:, 0:512])[par]
                    self.dma("sp", sin, I["st_ssd"][s, g * 512:(g + 1) * 512, :].rearrange("(q p) n -> p q n", p=128))
                    ptr = self.bank((4, 2)[par])
                    for q in range(4):
                        self.tr(ptr[:, q * 128:(q + 1) * 128], sin[:, q, :])
                    self.cp("act", Sbs, ptr[:, 0:512])
                    self.mm(pyi[0:n, 0:512], Cpad[:, s, :], Sbs[:, :], start=(s == 0), stop=(s == 15))
                    pus = self.bank((3, 0)[par])
                    self.mm(pus[:, 0:512], Bm[:, s, :], xdte[:, :])
                    self.tt("dve", Ss.rearrange("p (h d) -> p h d", h=8), ptr[:, 0:512].rearrange("p (h d) -> p h d", h=8),
                            ebl_all[:, s * 8:(s + 1) * 8].unsqueeze(2).broadcast_to([128, 8, 64]), ALU.mult)
                    self.tt("dve", Ss, Ss, pus[:, 0:512], ALU.add)
                    pto = self.bank(1)
                    for q in range(4):
                        self.tr(pto[:, q * 128:(q + 1) * 128], Ss[:, q * 128:(q + 1) * 128])
                    self.cp("act", outs_, pto[:, 0:512])
                    self.dma("act", O["ssd_s"][s, g * 512:(g + 1) * 512, :].rearrange("(q p) n -> p q n", p=128),
                             outs_.rearrange("p (q n) -> p q n", q=4))
            yv = yy[0:n, :].rearrange("p (h d) -> p h d", h=8)
            self.tt("dve", yv, pyi[0:n, 0:512].rearrange("p (h d) -> p h d", h=8),
                    eb[0:n, :].unsqueeze(2).broadcast_to([n, 8, 64]), ALU.mult)
            self.tt("dve", yy[0:n, :], yy[0:n, :], py[0:n, 0:512], ALU.add)
            self.tt("pool", t1[0:n, :].rearrange("p (h d) -> p h d", h=8), xv,
                    self.prow[0:n, PR["dd"] + g * 8:PR["dd"] + g * 8 + 8].unsqueeze(2).broadcast_to([n, 8, 64]), ALU.mult)
            self.tt("dve", yy[0:n, :], yy[0:n, :], t1[0:n, :], ALU.add)
            self.tt("dve", yy[0:n, :], yy[0:n, :], zs[0:n, :], ALU.mult)
            self.act(t1[0:n, :], yy[0:n, :], AF.Square, accum=ss[0:n, 0:1])
            self.act(ss[0:n, 1:2], ss[0:n, 0:1], AF.Sqrt, scale=1.0 / 512.0, bias=RMS_EPS)
            self.recip(ss[0:n, 1:2], ss[0:n, 1:2])
            self.stt(yy[0:n, :], yy[0:n, :], ss[0:n, 1:2], self.prow[0:n, PR["ng"] + g * 512:PR["ng"] + (g + 1) * 512],
                     ALU.mult, ALU.mult)
            pt = self.bank(4)
            for q in range(4):
                self.tr(pt[:, q * 128:q * 128 + n], yy[0:n, q * 128:(q + 1) * 128])
            self.cp("act", yT[:, :, 0:n], pt[:, 0:512].rearrange("p (k t) -> p k t", k=4)[:, :, 0:n])
            self.outproj_acc(Wo, 4, yT, c0, n, first)
            if kind == "s":
                self.cp("pool", tail[:, :, 0:48].rearrange("p c (s t) -> p c s t", s=16), uv[:, :, :, 8:11])
            elif ti == 16:
                self.cp("pool", tail[:, :, 48:51], uv[:, :, n:n + 3])
                pto = self.bank(1)
                for q in range(4):
                    self.tr(pto[:, q * 128:(q + 1) * 128], S[:, q * 128:(q + 1) * 128])
                self.cp("act", t1, pto[:, 0:512])
                self.dma("act", O["ssd_p"][g * 512:(g + 1) * 512, :].rearrange("(q p) n -> p q n", p=128),
                         t1.rearrange("p (q n) -> p q n", q=4))
            prev_n = n
        pt = self.bank(4)
        for i, ch in enumerate(cch):
            self.tr(pt[0:51, (i % 4) * 128:(i % 4) * 128 + 128], tail[:, i, :])
            self.cp("act", stg[0:51, (i % 4) * 128:(i % 4) * 128 + 128], pt[0:51, (i % 4) * 128:(i % 4) * 128 + 128])
            self.dma("act", O["ssdc"][:, ch * 128:(ch + 1) * 128], stg[0:51, (i % 4) * 128:(i % 4) * 128 + 128])

    def rg_unit(self, blk):
        I, O = self.I, self.O
        self.reset_arena()
        half = 5632 * (blk % 3)
        Wg = self.wview(half + 0, 8, 128)
        Wxr = self.wview(half + 1024, 8, 128)
        Wa = self.wbuf[:, half + 2048:half + 2176]
        Wx2 = self.wbuf[:, half + 2176:half + 2304]
        Wo = self.wview(half + 2304, 1, 1024)
        wie = I["w_in_e"]
        self.wload(Wxr, wie[:, 3600 + blk * 128:3600 + (blk + 1) * 128].rearrange("(k p) j -> p k j", p=128))
        self.wload(Wa, I["rg_wa"][blk])
        self.wload(Wx2, I["rg_wx"][blk])
        self.wload(Wg, wie[:, 2576 + blk * 128:2576 + (blk + 1) * 128].rearrange("(k p) j -> p k j", p=128))
        self.wload(Wo, I["w_out_e"][1024 + blk * 128:1024 + (blk + 1) * 128, :].rearrange("(k p) j -> p k j", p=128))
        RT = [(0, 16, "m")] + [(16 + 512 * j, 512, "p") for j in range(4)] + [(2064, 128, "s")]
        sets = []
        for i in range(2):
            d = {}
            d["ub"] = self.fa(516)
            for nm in ("xr", "rr", "ii", "aa", "gt", "hs", "gg"):
                d[nm] = self.fa(512)
            d["xrb"] = self.ba(512)
            d["yT"] = self.ba(512).rearrange("p (k t) -> p k t", k=1)
            sets.append(d)
        st0 = self.fa(128)
        cst = self.fa(128)
        sT = self.fa(16)
        fin = self.fa(17)
        tail = self.fa(51)
        stg = self.fa(128)
        stg2 = self.fa(128)
        hprev = self.fa(1)
        self.dma("sp", st0[0:16, :], I["st_rg"][:, blk * 128:(blk + 1) * 128])
        self.dma("sp", cst[0:48, :], I["st_rgc"][:, blk * 128:(blk + 1) * 128])
        self.memset("dve", hprev, 0.0)
        cw = [self.pcol[:, PC["rg_cw"] + blk * 4 + j:PC["rg_cw"] + blk * 4 + j + 1] for j in range(4)]
        cb = self.pc("rg_cb", blk)

        def stage_a(ti):
            c0, n, kind = RT[ti]
            d = sets[ti % 2]
            dp = sets[(ti + 1) % 2]
            u = d["ub"]
            xr, rr, ii, aa, gt, hs, gg, xrb, yT = (d[k] for k in ("xr", "rr", "ii", "aa", "gt", "hs", "gg", "xrb", "yT"))
            if kind == "s":
                uv = u[:, 0:176].rearrange("p (s t) -> p s t", s=16)
                pt = self.bank(4)
                self.tr(pt[:, 0:48], cst[0:48, :])
                self.cp("act", uv[:, :, 0:3], pt[:, 0:48].rearrange("p (s t) -> p s t", s=16))
                self.tr(pt[:, 64:80], st0[0:16, :])
                self.cp("act", sT[:, :], pt[:, 64:80])
            else:
                uv = u
                if ti > 0:
                    pn = RT[ti - 1][1]
                    self.cp("pool", uv[:, 0:3], dp["ub"][:, pn:pn + 3])
                else:
                    self.memset("pool", uv[:, 0:3], 0.0)
            pb = self.bank(2)
            self.inproj_fm(pb[:, 0:n], Wxr, 8, slice(0, 128), c0, n)
            if kind == "s":
                self.cp("act", uv[:, :, 3:11], pb[:, 0:128].rearrange("p (s t) -> p s t", s=16))
                o_ = xr[:, 0:128].rearrange("p (s t) -> p s t", s=16)
                t_ = gg[:, 0:128].rearrange("p (s t) -> p s t", s=16)
                src = lambda j: uv[:, :, j:j + 8]
            else:
                self.cp("act", uv[:, 3:3 + n], pb[:, 0:n])
                o_ = xr[:, 0:n]
                t_ = gg[:, 0:n]
                src = lambda j: uv[:, j:j + n]
            pgt = self.bank(5)
            self.inproj_fm(pgt[:, 0:n], Wg, 8, slice(0, 128), c0, n)
            self.ts("pool", o_, src(3), cw[3], ALU.mult, cb, ALU.add)
            for j in (2, 1):
                self.ts("pool", t_, src(j), cw[j], ALU.mult, 0.0, ALU.add)
                self.tt("pool", o_, o_, t_, ALU.add)
            self.stt(o_, src(0), cw[0], o_, ALU.mult, ALU.add)
            self.cp("dve", xrb[:, 0:n], xr[:, 0:n])
            pg = self.bank(3)
            pg2 = self.bank(4)
            self.mm(pg[:, 0:n], Wa, xrb[:, 0:n])
            self.mm(pg2[:, 0:n], Wx2, xrb[:, 0:n])
            self.act(rr[:, 0:n], pg[:, 0:n], AF.Sigmoid, bias=self.pc("rg_ba", blk))
            self.act(ii[:, 0:n], pg2[:, 0:n], AF.Sigmoid, bias=self.pc("rg_bx", blk))
            self.act(aa[:, 0:n], rr[:, 0:n], AF.Exp, scale=self.dc("rg_sc", blk))
            self.act(rr[:, 0:n], aa[:, 0:n], AF.Square)
            self.act(rr[:, 0:n], rr[:, 0:n], AF.Sqrt, scale=-1.0, bias=1.0)
            self.tt("pool", gt[:, 0:n], ii[:, 0:n], xr[:, 0:n], ALU.mult)
            self.tt("dve", gt[:, 0:n], gt[:, 0:n], rr[:, 0:n], ALU.mult)
            if kind == "s":
                for s_ in range(16):
                    self.scan(hs[:, s_ * 8:(s_ + 1) * 8], aa[:, s_ * 8:(s_ + 1) * 8], gt[:, s_ * 8:(s_ + 1) * 8], sT[:, s_:s_ + 1])
                self.cp("pool", fin[:, 0:16].unsqueeze(2), hs[:, 0:128].rearrange("p (s t) -> p s t", s=16)[:, :, 7:8])
            else:
                self.scan(hs[:, 0:n], aa[:, 0:n], gt[:, 0:n], hprev[:, 0:1])
                self.cp("pool", hprev[:, 0:1], hs[:, n - 1:n])
                if ti == 4:
                    self.cp("pool", fin[:, 16:17], hs[:, n - 1:n])
            self.act(gg[:, 0:n], pgt[:, 0:n], AF.Gelu)
            self.tt("dve", yT[:, 0, 0:n], hs[:, 0:n], gg[:, 0:n], ALU.mult)
            if kind == "s":
                self.cp("pool", tail[:, 0:48].rearrange("p (s t) -> p s t", s=16), uv[:, :, 8:11])
            elif ti == 4:
                self.cp("pool", tail[:, 48:51], uv[:, n:n + 3])

        def stage_b(ti):
            c0, n, kind = RT[ti]
            self.outproj_wide(Wo, 1, sets[ti % 2]["yT"], c0, n, False)

        stage_a(0)
        for ti in range(len(RT)):
            if ti + 1 < len(RT):
                stage_a(ti + 1)
            stage_b(ti)
        pt = self.bank(4)
        self.tr(pt[0:51, 0:128], tail[:, :])
        self.cp("act", stg[0:51, :], pt[0:51, 0:128])
        self.dma("act", O["rgc"][:, blk * 128:(blk + 1) * 128], stg[0:51, :])
        pt2 = self.bank(5)
        self.tr(pt2[0:17, 0:128], fin[:, :])
        self.cp("act", stg2[0:17, :], pt2[0:17, 0:128])
        self.dma("act", O["rg"][:, blk * 128:(blk + 1) * 128], stg2[0:17, :])

    def ln(self, gcol, bcol, final=False):
        self.reset_arena()
        onesb = self.onesb[:, :]
        W = 256
        tiles = [(c, min(W, NTOK - c)) for c in range(0, NTOK, W)]
        bs = [(self.ba(8 * W).rearrange("p (c t) -> p c t", c=8), self.ba(8 * W).rearrange("p (c t) -> p c t", c=8))
              for _ in range(2)]
        sm = [[self.fa(W) for _ in range(4)] for _ in range(2)]
        for ti, (c0, w) in enumerate(tiles):
            mean, rstd, mr, tmp = sm[ti % 2]
            xb, sqb = bs[ti % 2]
            hv = self.h32[:, :, c0:c0 + w]
            pb = self.bank(2 + (ti % 2))
            p1 = pb[:, 0:w]
            p2 = pb[:, 256:256 + w]
            self.cp("dve", xb[:, :, 0:w], hv)
            self.act(sqb[:, :, 0:w], hv, AF.Square)
            for c in range(8):
                self.mm(p1, onesb, xb[:, c, 0:w], start=(c == 0), stop=(c == 7))
            for c in range(8):
                self.mm(p2, onesb, sqb[:, c, 0:w], start=(c == 0), stop=(c == 7))
            self.act(mean[:, 0:w], p1, AF.Copy, scale=1.0 / D)
            self.tt("pool", tmp[:, 0:w], mean[:, 0:w], mean[:, 0:w], ALU.mult)
            self.stt(rstd[:, 0:w], p2, 1.0 / D, tmp[:, 0:w], ALU.mult, ALU.subtract)
            self.act(rstd[:, 0:w], rstd[:, 0:w], AF.Ln, bias=LN_EPS)
            self.act(rstd[:, 0:w], rstd[:, 0:w], AF.Exp, scale=-0.5)
            self.tt("pool", mr[:, 0:w], mean[:, 0:w], rstd[:, 0:w], ALU.mult)
            self.tt("dve", hv, hv, rstd[:, 0:w].unsqueeze(1).broadcast_to([128, 8, w]), ALU.mult)
            self.tt("pool", self.h32[:, 0:5, c0:c0 + w], self.h32[:, 0:5, c0:c0 + w],
                    mr[:, 0:w].unsqueeze(1).broadcast_to([128, 5, w]), ALU.subtract)
            self.tt("dve", self.h32[:, 5:8, c0:c0 + w], self.h32[:, 5:8, c0:c0 + w],
                    mr[:, 0:w].unsqueeze(1).broadcast_to([128, 3, w]), ALU.subtract)
            for c in range(8):
                hc = self.h32[:, c, c0:c0 + w]
                gc = self.pcol[:, gcol + c:gcol + c + 1]
                bc = self.pcol[:, bcol + c:bcol + c + 1]
                self.act(hc, hc, AF.Identity, scale=gc, bias=bc)
            if not final:
                self.cp("dve", self.hb[:, :, c0:c0 + w], hv)

    def mlp(self, layer):
        I = self.I
        for f in range(8):
            self.reset_arena()
            half = 8448 * (f % 2)
            W1 = self.wview(half, 8, 512)
            W2 = self.wview(half + 4096, 4, 1024)
            self.wload(W1, I["w1"][layer, :, f * 512:(f + 1) * 512].rearrange("(k p) j -> p k j", p=128))
            self.wload(W2, I["w2"][layer, f * 512:(f + 1) * 512, :].rearrange("(k p) j -> p k j", p=128))
            rbuf = self.fa(2048).rearrange("p (c t) -> p c t", c=4)
            abuf = self.ba(2048).rearrange("p (c t) -> p c t", c=4)
            for (c0, w) in CT512:
                for fc in range(4):
                    pb = self.bank(2 + (fc % 2))
                    self.inproj_fm(pb[:, 0:w], W1, 8, slice(fc * 128, (fc + 1) * 128), c0, w)
                    self.act(rbuf[:, fc, 0:w], pb[:, 0:w], AF.Relu)
                    self.tt("pool", abuf[:, fc, 0:w], rbuf[:, fc, 0:w], rbuf[:, fc, 0:w], ALU.mult)
                for oc in range(8):
                    po = self.bank(4 + (oc % 4))
                    for k in range(4):
                        self.mm(po[:, 0:w], W2[:, k, oc * 128:(oc + 1) * 128], abuf[:, k, 0:w], start=(k == 0), stop=(k == 3))
                    hv = self.h32[:, oc, c0:c0 + w]
                    if f == 0:
                        self.stt(hv, hv, ALPHA, po[:, 0:w], ALU.mult, ALU.add)
                    else:
                        self.tt("dve", hv, hv, po[:, 0:w], ALU.add)

    def layer1(self):
        for h in range(4):
            self.gla_unit(h, gla=True, first=(h == 0))
        for h in range(8):
            self.gla_unit(h, gla=False, first=False)

    def gla_unit(self, h, gla, first):
        I, O = self.I, self.O
        self.reset_arena()
        cm = self.cm
        wio = I["w_in_o"]
        uidx = h if gla else 4 + h
        half = 8448 * (uidx % 2) if gla else 5632 * (h % 3)
        dv = 256 if gla else 128
        nvc = dv // 128

        def wsrc(c0, ncol):
            return wio[:, c0:c0 + ncol].rearrange("(k p) j -> p k j", p=128)
        if gla:
            Wq = self.wview(half, 8, 128)
            Wk = self.wview(half + 1024, 8, 128)
            Wv = self.wview(half + 2048, 8, 256)
            Wg = self.wview(half + 4096, 8, 256)
            Wl = self.wview(half + 6144, 8, 16)
            Wg2 = self.wbuf[0:16, half + 6272:half + 6400]
            Wo = self.wview(half + 6400, 2, 1024)
            self.wload(Wq, wsrc(h * 128, 128))
            self.wload(Wk, wsrc(512 + h * 128, 128))
            self.wload(Wl, wsrc(3072, 16))
            self.wload(Wg2, I["wg2"][:, h * 128:(h + 1) * 128])
            self.wload(Wv, wsrc(1024 + h * 256, 256))
            self.wload(Wg, wsrc(2048 + h * 256, 256))
            self.wload(Wo, I["w_out_o"][h * 256:(h + 1) * 256, :].rearrange("(k p) j -> p k j", p=128))
            st_all_in = I["st_gla"][:, h].rearrange("s k v -> k s v")
            st_all_out = O["gla_s"][:, h].rearrange("s k v -> k s v")
            st_out_p = O["gla_p"][h]
            s1 = -1.0 / 16.0
        else:
            Wq = self.wview(half, 8, 128)
            Wk = self.wview(half + 1024, 8, 128)
            Wv = self.wview(half + 2048, 8, 128)
            Wg = self.wview(half + 3072, 8, 128)
            Wo = self.wview(half + 4096, 1, 1024)
            self.wload(Wq, wsrc(3088 + h * 128, 128))
            self.wload(Wk, wsrc(4112 + h * 128, 128))
            self.wload(Wv, wsrc(5136 + h * 128, 128))
            self.wload(Wg, wsrc(6160 + h * 128, 128))
            self.wload(Wo, I["w_out_o"][1024 + h * 128:1024 + (h + 1) * 128, :].rearrange("(k p) j -> p k j", p=128))
            st_all_in = I["st_hg"][:, h].rearrange("s k v -> k s v")
            st_all_out = O["hg_s"][:, h].rearrange("s k v -> k s v")
            st_out_p = O["hg_p"][h]
            s1 = 1.0
        FB = self.fa(7168)
        fbs = [[FB[:, (7 * i + k) * 512:(7 * i + k + 1) * 512] for k in range(7)] for i in range(2)]
        Sall = FB[:, 0:16 * dv].rearrange("p (s v) -> p s v", s=16)
        gsb = self.fa(1024)
        S = self.fa(256)
        S2 = self.fa(256)
        smalls = self.fa(32)
        brf, ebr, ebl, e1l = [smalls[:, 4 * i:4 * i + 4] for i in range(4)]
        lrb = self.ba(512)
        Sb = self.ba(256)
        Sb2 = self.ba(256)
        bsets = []
        for i in range(2):
            Bk = self.ba(4096)
            bsets.append(dict(qt=Bk[:, 0:512], ktb=Bk[:, 512:1024], ktok=Bk[:, 1024:1536], AT=Bk[:, 1536:2048],
                              vtok=Bk[:, 2048:3072], yT=Bk[:, 3072:4096], blk=Bk))
        km = bsets[1]["blk"][:, 0:2048].rearrange("p (s k) -> p s k", s=16)
        qtB, ktbB, ktokB, vtokB, ATB, yTB = (bsets[0][k] for k in ("qt", "ktb", "ktok", "vtok", "AT", "yT"))
        ones = cm[:, CM["ones"]:CM["ones"] + 128]
        self.memset("dve", S[:, 0:dv], 0.0)
        self.memset("dve", Sb[:, 0:dv], 0.0)
        self.memset("pool", bsets[0]["AT"], 0.0)
        self.memset("pool", bsets[1]["AT"], 0.0)
        ng_col = [self.pc("gla_ng", vc) if gla else self.pc("hg_ng", 0) for vc in range(nvc)]

        def tile_small(c0, n, kind):
            G = fbs[1][3:7]
            qf, kf, lf, bp = (G[0][:, 128 * i:128 * i + 128] for i in range(4))
            E1, E2, ktl, rstd = (G[1][:, 128 * i:128 * i + 128] for i in range(4))
            sqo = G[2][:, 0:256].rearrange("p (c t) -> p c t", c=2)
            gs = G[2][:, 256:512].rearrange("p (c t) -> p c t", c=2)
            ot = G[3][:, 0:256].rearrange("p (c t) -> p c t", c=2)
            BSm = bsets[1] if kind == "m" else bsets[0]
            qt, ktb, ktok, AT = BSm["qt"][:, 0:128], BSm["ktb"][:, 0:128], BSm["ktok"][:, 0:128], BSm["AT"][:, 0:128]
            vtok = BSm["vtok"][:, 0:256]
            yT = BSm["yT"][:, 0:256].rearrange("p (k t) -> p k t", k=2)
            pq = self.bank(2)
            self.inproj_fm(pq[:, 0:n], Wq, 8, slice(0, 128), c0, n)
            pk = self.bank(3)
            self.inproj_fm(pk[:, 0:n], Wk, 8, slice(0, 128), c0, n)
            if gla:
                self.act(qf[:, 0:n], pq[:, 0:n], AF.Copy, scale=128.0 ** -0.5)
                self.cp("act", kf[:, 0:n], pk[:, 0:n])
                pl = self.bank(5)
                self.inproj_fm(pl[0:16, 0:n], Wl, 8, slice(0, 16), c0, n)
                self.cp("act", lrb[0:16, 0:n], pl[0:16, 0:n])
                self.mm(pl[:, 128:128 + n], Wg2, lrb[0:16, 0:n])
                self.act(lf[:, 0:n], pl[:, 128:128 + n], AF.Exp, scale=-1.0, bias=self.dc("nbg2", h))
                self.act(lf[:, 0:n], lf[:, 0:n], AF.Ln, bias=1.0)
            else:
                self.act(qf[:, 0:n], pq[:, 0:n], AF.Silu)
                self.act(E1[:, 0:n], pk[:, 0:n], AF.Sigmoid)
                self.ts("dve", lf[:, 0:n], E1[:, 0:n], self.dc("oml", h), ALU.mult, self.dc("lb", h), ALU.add)
                self.act(lf[:, 0:n], lf[:, 0:n], AF.Ln)
                self.ts("dve", kf[:, 0:n], E1[:, 0:n], self.dc("noml", h), ALU.mult, self.dc("oml", h), ALU.add)
            if kind == "s":
                rst = cm[:, CM["rst8"]:CM["rst8"] + n]
                msk = cm[0:n, CM["mcumS"]:CM["mcumS"] + n]
                chunks = [(8 * s_, 8 * s_ + 8) for s_ in range(16)]
            else:
                rst = cm[:, CM["rst64"]:CM["rst64"] + n]
                msk = cm[0:n, CM["mcumP"]:CM["mcumP"] + n]
                chunks = [(0, n)]
            self.scan(bp[:, 0:n], rst, lf[:, 0:n], 0.0)
            self.act(E1[:, 0:n], bp[:, 0:n], AF.Exp, scale=s1)
            self.act(E2[:, 0:n], bp[:, 0:n], AF.Exp, scale=-s1)
            self.tt("dve", qt[:, 0:n], qf[:, 0:n], E1[:, 0:n], ALU.mult)
            self.tt("dve", ktl[:, 0:n], kf[:, 0:n], E2[:, 0:n], ALU.mult)
            self.cp("pool", ktb[:, 0:n], ktl[:, 0:n])
            ptk = self.bank(4)
            self.tr(ptk[0:n, 0:128], ktl[:, 0:n])
            self.cp("act", ktok[0:n, :], ptk[0:n, 0:128])
            pv = self.bank(6)
            self.inproj_tm(pv[0:n, 0:dv], Wv, slice(0, dv), c0, n)
            self.cp("act", vtok[0:n, 0:dv], pv[0:n, 0:dv])
            psc = self.bank(5)
            self.mm(psc[0:n, 256:256 + n], ktb[:, 0:n], qt[:, 0:n])
            self.tt("dve", AT[0:n, 0:n], psc[0:n, 256:256 + n], msk, ALU.mult)
            if kind == "s":
                self.tt("pool", km[:, :, :], ktok[:, :].unsqueeze(1).broadcast_to([128, 16, 128]),
                        cm[:, CM["seqm"]:CM["seqm"] + 16].unsqueeze(2).broadcast_to([128, 16, 128]), ALU.mult)
            pov = [self.bank(0), self.bank(1)]
            for vc in range(nvc):
                self.mm(pov[vc][:, 0:n], vtok[0:n, vc * 128:(vc + 1) * 128], AT[0:n, 0:n], start=True, stop=False)
            if kind == "s":
                self.dma("sp", Sall[:, :, :], st_all_in)
            for ci, (a0, a1) in enumerate(chunks):
                last = (ci == len(chunks) - 1)
                Sbr = Sb
                if kind == "s":
                    Sbr = (Sb, Sb2)[ci % 2]
                    self.cp("act", Sbr[:, 0:dv], Sall[:, ci, :])
                for vc in range(nvc):
                    self.mm(pov[vc][:, a0:a1], Sbr[:, vc * 128:(vc + 1) * 128], qt[:, a0:a1], start=False, stop=last)
                pu = self.bank(2 + ci % 2)
                if kind == "s":
                    self.mm(pu[:, 0:dv], km[:, ci, :], vtok[:, 0:dv])
                    self.tt("dve", Sall[:, ci, :], Sall[:, ci, :], pu[:, 0:dv], ALU.add)
                    self.ts("dve", Sall[:, ci, :], Sall[:, ci, :], E1[:, a1 - 1:a1], ALU.mult)
                else:
                    self.mm(pu[:, 0:dv], ktok[a0:a1, :], vtok[a0:a1, 0:dv])
                    self.tt("dve", S2[:, 0:dv], S[:, 0:dv], pu[:, 0:dv], ALU.add)
                    self.ts("dve", S[:, 0:dv], S2[:, 0:dv], E1[:, a1 - 1:a1], ALU.mult)
            if kind == "s":
                self.dma("act", st_all_out, Sall[:, :, :])
            for vc in range(nvc):
                self.act(sqo[:, vc, 0:n], pov[vc][:, 0:n], AF.Square)
            pss = self.bank(5)
            for vc in range(nvc):
                self.mm(pss[:, 384:384 + n], ones, sqo[:, vc, 0:n], start=(vc == 0), stop=(vc == nvc - 1))
            self.act(rstd[:, 0:n], pss[:, 384:384 + n], AF.Ln, scale=1.0 / dv, bias=RMS_EPS)
            self.act(rstd[:, 0:n], rstd[:, 0:n], AF.Exp, scale=-0.5)
            for vc in range(nvc):
                pg = self.bank(6 + vc)
                self.inproj_fm(pg[:, 0:n], Wg, 8, slice(vc * 128, (vc + 1) * 128), c0, n)
                self.act(gs[:, vc, 0:n], pg[:, 0:n], AF.Silu)
                self.tt("dve", ot[:, vc, 0:n], pov[vc][:, 0:n], rstd[:, 0:n], ALU.mult)
                self.stt(yT[:, vc, 0:n], ot[:, vc, 0:n], ng_col[vc], gs[:, vc, 0:n], ALU.mult, ALU.mult)
            self.outproj_acc(Wo, nvc, yT, c0, n, first)

        def tile_super(j):
            c0 = 16 + 512 * j
            n = 512
            F = fbs[j % 2]
            qf, kf, lf, bp, E1, E2, ktl = F
            sqo = [F[0], F[1]]
            rstd = F[2]
            gs = [gsb[:, 0:512], gsb[:, 512:1024]]
            ot = [F[5], F[6]]
            BS = bsets[j % 2]
            qt, ktb, ktok, AT = BS["qt"], BS["ktb"], BS["ktok"], BS["AT"]
            vtokB = BS["vtok"]
            vtok = vtokB[:, 0:4 * dv].rearrange("p (b v) -> p b v", b=4)
            yT = BS["yT"][:, :].rearrange("p (k t) -> p k t", k=2)
            pq = self.bank(0)
            self.inproj_fm(pq[:, 0:n], Wq, 8, slice(0, 128), c0, n)
            pk = self.bank(1)
            self.inproj_fm(pk[:, 0:n], Wk, 8, slice(0, 128), c0, n)
            if gla:
                pl = self.bank(2)
                self.inproj_fm(pl[0:16, 0:n], Wl, 8, slice(0, 16), c0, n)
            if gla:
                self.act(qf[:, :], pq[:, 0:n], AF.Copy, scale=128.0 ** -0.5)
                self.cp("act", kf[:, :], pk[:, 0:n])
                self.cp("act", lrb[0:16, 0:n], pl[0:16, 0:n])
                pg0 = self.bank(3)
                self.inproj_fm(pg0[:, 0:n], Wg, 8, slice(0, 128), c0, n)
                pl2 = self.bank(0)
                self.mm(pl2[:, 0:n], Wg2, lrb[0:16, 0:n])
                pg1 = self.bank(1)
                self.inproj_fm(pg1[:, 0:n], Wg, 8, slice(128, 256), c0, n)
                self.act(gs[0][:, :], pg0[:, 0:n], AF.Silu)
                self.act(gs[1][:, :], pg1[:, 0:n], AF.Silu)
                self.act(lf[:, :], pl2[:, 0:n], AF.Exp, scale=-1.0, bias=self.dc("nbg2", h))
                self.act(lf[:, :], lf[:, :], AF.Ln, bias=1.0)
            else:
                pg0 = self.bank(2)
                self.inproj_fm(pg0[:, 0:n], Wg, 8, slice(0, 128), c0, n)
                self.act(qf[:, :], pq[:, 0:n], AF.Silu)
                self.act(gs[0][:, :], pg0[:, 0:n], AF.Silu)
                self.act(E1[:, :], pk[:, 0:n], AF.Sigmoid)
                self.ts("dve", lf[:, :], E1[:, :], self.dc("oml", h), ALU.mult, self.dc("lb", h), ALU.add)
                self.act(lf[:, :], lf[:, :], AF.Ln)
                self.ts("dve", kf[:, :], E1[:, :], self.dc("noml", h), ALU.mult, self.dc("oml", h), ALU.add)
            for b in range(4):
                if gla:
                    pvb = self.bank(2 + b // 2)[:, (b % 2) * 256:(b % 2) * 256 + 256]
                else:
                    pvb = self.bank(3)[:, b * 128:(b + 1) * 128]
                self.inproj_tm(pvb, Wv, slice(0, dv), c0 + b * 128, 128)
            if gla:
                self.cp("act", vtokB[:, 0:512], self.bank(2)[:, 0:512])
                self.cp("act", vtokB[:, 512:1024], self.bank(3)[:, 0:512])
            else:
                self.cp("act", vtokB[:, 0:512], self.bank(3)[:, 0:512])
            self.scan(bp[:, :], cm[:, CM["rst128"]:CM["rst128"] + 512], lf[:, :], 0.0)
            bpv = bp[:, :].rearrange("p (b t) -> p b t", b=4)
            self.cp("dve", brf[:, :].unsqueeze(2), bpv[:, :, 64:65])
            self.act(ebr[:, :].unsqueeze(2), bpv[:, :, 64:65], AF.Exp, scale=s1)
            self.act(ebl[:, :].unsqueeze(2), bpv[:, :, 127:128], AF.Exp, scale=s1)
            self.tt("dve", bpv, bpv, brf[:, :].unsqueeze(2).broadcast_to([128, 4, 128]), ALU.subtract)
            self.act(E1[:, :], bp[:, :], AF.Exp, scale=s1)
            self.act(E2[:, :], bp[:, :], AF.Exp, scale=-s1)
            self.cp("dve", e1l[:, :].unsqueeze(2), E1[:, :].rearrange("p (b t) -> p b t", b=4)[:, :, 127:128])
            self.tt("dve", qt[:, :], qf[:, :], E1[:, :], ALU.mult)
            self.tt("dve", ktl[:, :], kf[:, :], E2[:, :], ALU.mult)
            self.cp("pool", ktb[:, :], ktl[:, :])
            ptk = self.bank(4)
            for b in range(4):
                self.tr(ptk[:, b * 128:(b + 1) * 128], ktl[:, b * 128:(b + 1) * 128])
            self.cp("act", ktok[:, :], ptk[:, 0:512])
            psc = self.bank(5)
            for b in range(4):
                o = b * 128
                self.mm(psc[:, o + 64:o + 128], ktb[:, o:o + 128], qt[:, o + 64:o + 128])
                self.mm(psc[0:64, o:o + 64], ktb[:, o:o + 64], qt[:, o:o + 64])
            pscv = psc[:, 0:512].rearrange("p (b t) -> p b t", b=4)
            ATv = AT[:, :].rearrange("p (b t) -> p b t", b=4)
            mk = cm[:, CM["mcumP"]:CM["mcumP"] + 128]
            self.tt("dve", ATv[0:64, :, :], pscv[0:64, :, :], mk[0:64, :].unsqueeze(1).broadcast_to([64, 4, 128]), ALU.mult)
            self.tt("dve", ATv[64:128, :, 64:128], pscv[64:128, :, 64:128],
                    mk[64:128, 64:128].unsqueeze(1).broadcast_to([64, 4, 64]), ALU.mult)
            pov = [self.bank(6), self.bank(7)]
            for b in range(4):
                o = b * 128
                self.act(Sb[:, 0:dv], S[:, 0:dv], AF.Identity, scale=ebr[:, b:b + 1])
                for vc in range(nvc):
                    self.mm(pov[vc][:, o:o + 128], vtok[:, b, vc * 128:(vc + 1) * 128], ATv[:, b, :], start=True, stop=False)
                    self.mm(pov[vc][:, o:o + 128], Sb[:, vc * 128:(vc + 1) * 128], qt[:, o:o + 128], start=False, stop=True)
                pu = self.bank(4 + b % 2)
                self.mm(pu[:, 0:dv], ktok[:, o:o + 128], vtok[:, b, 0:dv])
                self.ts("dve", S2[:, 0:dv], S[:, 0:dv], ebl[:, b:b + 1], ALU.mult)
                self.stt(S[:, 0:dv], pu[:, 0:dv], e1l[:, b:b + 1], S2[:, 0:dv], ALU.mult, ALU.add)
            for vc in range(nvc):
                self.act(sqo[vc][:, :], pov[vc][:, 0:n], AF.Square)
            pss = self.bank(4)
            for vc in range(nvc):
                self.mm(pss[:, 0:n], ones, sqo[vc][:, :], start=(vc == 0), stop=(vc == nvc - 1))
            self.act(rstd[:, :], pss[:, 0:n], AF.Ln, scale=1.0 / dv, bias=RMS_EPS)
            self.act(rstd[:, :], rstd[:, :], AF.Exp, scale=-0.5)
            for vc in range(nvc):
                self.tt("dve", ot[vc][:, :], pov[vc][:, 0:n], rstd[:, :], ALU.mult)
                self.stt(yT[:, vc, :], ot[vc][:, :], ng_col[vc], gs[vc][:, :], ALU.mult, ALU.mult)
            self.outproj_wide(Wo, nvc, yT, c0, n, first, banks=(4, 5))

        tile_small(0, 16, "m")
        for j in range(4):
            tile_super(j)
        self.dma("act", st_out_p, S[:, 0:dv])
        tile_small(2064, 128, "s")

    def store_outputs(self):
        O = self.O
        stg = [self.fa(1024), self.fa(1024), self.fa(1024)]
        for ti, (c0, n, kind) in enumerate(TILES):
            if kind == "m":
                continue
            s = stg[ti % 3]
            pp = self.ps[2 + ti % 2]
            for c in range(8):
                self.tr(pp[0:n, c * 128:(c + 1) * 128], self.h32[:, c, c0:c0 + n])
            self.cp("act", s[0:n, :], pp[0:n, :])
            if kind == "p":
                r0 = c0 - 16
                self.dma("sp", O["y_p"][r0:r0 + n, :], s[0:n, :])
            else:
                self.dma("sp", O["y_s"], s[0:n, :])


_CACHE = {}


def kernel(**inp):
    if "nc" not in _CACHE:
        b = Builder()
        _CACHE["nc"] = b.build()
    nc = _CACHE["nc"]
    f32 = lambda a: np.ascontiguousarray(np.asarray(a, dtype=np.float32))
    pcol = _pack_pcol(inp)
    prow = _pack_prow(inp)
    cmask = _const_masks()
    shared = {
        "meta": f32(inp["meta_tokens"]),
        "w_in_e": f32(inp["w_in_even"][0]), "w_out_e": f32(inp["w_out_even"][0]),
        "w_in_o": f32(inp["w_in_odd"][0]), "w_out_o": f32(inp["w_out_odd"][0]),
        "w1": f32(inp["mlp_w1"]), "w2": f32(inp["mlp_w2"]),
        "rg_wa": f32(inp["rg_wa"][0]), "rg_wx": f32(inp["rg_wx"][0]), "wg2": f32(inp["gla_wg2"][0]),
        "pcol": pcol, "prow": prow, "cmask": cmask,
    }
    in_maps = []
    for i in range(NCORES):
        sl = slice(16 * i, 16 * i + 16)
        m = dict(shared)
        m["xp"] = f32(inp["x_prompt"][i])
        m["xs"] = f32(np.asarray(inp["x_sample"])[sl].reshape(128, D))
        m["st_ssd"] = f32(np.asarray(inp["state_ssd"])[0, sl].reshape(16, 1024, 128))
        m["st_ssdc"] = f32(np.asarray(inp["state_ssd_conv"])[0, sl].reshape(48, 1536))
        m["st_rg"] = f32(np.asarray(inp["state_rglru"])[0, sl])
        m["st_rgc"] = f32(np.asarray(inp["state_rglru_conv"])[0, sl].reshape(48, 1024))
        m["st_gla"] = f32(np.asarray(inp["state_gla"])[0, sl])
        m["st_hg"] = f32(np.asarray(inp["state_hgrn"])[0, sl])
        in_maps.append(m)
    res = run_bass_kernel_spmd(nc, in_maps, core_ids=list(range(NCORES)))
    R = res.results
    y_p = np.stack([R[i]["y_p"] for i in range(NCORES)], 0)
    y_s = np.concatenate([R[i]["y_s"].reshape(16, 8, D) for i in range(NCORES)], 0)
    p_ssd = np.stack([R[i]["ssd_p"].reshape(16, 64, 128) for i in range(NCORES)], 0)[None]
    p_ssdc = np.stack([R[i]["ssdc"][48:51] for i in range(NCORES)], 0)[None]
    p_rg = np.stack([R[i]["rg"][16] for i in range(NCORES)], 0)[None]
    p_rgc = np.stack([R[i]["rgc"][48:51] for i in range(NCORES)], 0)[None]
    p_gla = np.stack([R[i]["gla_p"] for i in range(NCORES)], 0)[None]
    p_hg = np.stack([R[i]["hg_p"] for i in range(NCORES)], 0)[None]
    s_ssd = np.concatenate([R[i]["ssd_s"].reshape(16, 16, 64, 128) for i in range(NCORES)], 0)[None]
    s_ssdc = np.concatenate([R[i]["ssdc"][0:48].reshape(16, 3, 1536) for i in range(NCORES)], 0)[None]
    s_rg = np.concatenate([R[i]["rg"][0:16] for i in range(NCORES)], 0)[None]
    s_rgc = np.concatenate([R[i]["rgc"][0:48].reshape(16, 3, 1024) for i in range(NCORES)], 0)[None]
    s_gla = np.concatenate([R[i]["gla_s"] for i in range(NCORES)], 0)[None]
    s_hg = np.concatenate([R[i]["hg_s"] for i in range(NCORES)], 0)[None]
    outs = (y_p, y_s, p_ssd, p_ssdc, p_rg, p_rgc, p_gla, p_hg, s_ssd, s_ssdc, s_rg, s_rgc, s_gla, s_hg)
    return tuple(np.ascontiguousarray(o, dtype=np.float32) for o in outs)
```

```python
import contextlib
import numpy as np
import concourse.bass as bass
import concourse.mybir as mybir
from concourse.bass_utils import run_bass_kernel_spmd

F32 = mybir.dt.float32
BF16 = mybir.dt.bfloat16
AF = mybir.ActivationFunctionType
ALU = mybir.AluOpType

NCORES = 8
D = 1024
NTOK = 2192
DEPTH = 2
ALPHA = (2.0 * DEPTH) ** 0.25
LN_EPS = 1e-5
RMS_EPS = 1e-6
TILES = [(0, 16, "m")] + [(16 + 128 * j, 128, "p") for j in range(16)] + [(2064, 128, "s")]
CT512 = [(0, 512), (512, 512), (1024, 512), (1536, 512), (2048, 144)]

ENGS = ("pe", "act", "dve", "pool", "sp")
NDMA_SEMS = 32
NSW_SEMS = 8


class _Op:
    __slots__ = ("eng", "fn", "deps", "odeps", "dma", "idx", "signal", "semval", "dsem", "dprev", "cost", "lat", "tbl")

    def __init__(self, eng, fn, dma):
        self.eng = eng
        self.fn = fn
        self.dma = dma
        self.deps = set()
        self.odeps = set()
        self.signal = False
        self.semval = None
        self.dsem = None
        self.dprev = None
        self.cost = 0.3
        self.lat = 0.0
        self.tbl = None


def _rng(ap):
    t = ap.tensor
    if type(t).__name__.startswith("DRam"):
        return None
    row = 1
    for s in list(t.shape)[1:]:
        row *= int(s)
    lo = int(ap.offset) % row
    dims = sorted((abs(int(st)), int(cnt)) for (st, cnt) in list(ap.ap)[1:] if int(cnt) > 1 and int(st) != 0)
    ivs = [(lo, lo + 1)]
    for (st, cnt) in dims:
        ext = ivs[-1][1] - ivs[0][0]
        if st <= ext or len(ivs) * cnt > 32:
            ivs = [(ivs[0][0], ivs[-1][1] + (cnt - 1) * st)]
        else:
            ivs = [(a + k * st, b + k * st) for k in range(cnt) for (a, b) in ivs]
    if type(t).__name__.startswith("PSum"):
        ivs = sorted(set(((a // 512) * 512, ((b + 511) // 512) * 512) for (a, b) in ivs))
    return (t.name, ivs)


def _fsize(ap):
    n = 1
    for (st, cnt) in list(ap.ap)[1:]:
        n *= int(cnt)
    return n


_TBL = {AF.Exp: "le", AF.Ln: "le", AF.Sigmoid: "sg", AF.Silu: "si", AF.Gelu: "ge", AF.Sqrt: "sq", AF.Tanh: "th"}


class Prog:
    def __init__(self, nc):
        self.nc = nc
        self.ops = []
        self.acc = {}
        self.do_sched = True

    def _access(self, o, ap, is_write):
        r = _rng(ap)
        if r is None:
            return
        for (lo, hi) in r[1]:
            self._access1(o, ap, r[0], lo, hi, is_write)

    def _access1(self, o, ap, name, lo, hi, is_write):
        psum = type(ap.tensor).__name__.startswith("PSum")
        lst = self.acc.setdefault(name, [])
        keep = []
        for rec in lst:
            rlo, rhi, oi, w, eng, dma = rec
            overlap = (rlo < hi) and (lo < rhi)
            if overlap and (w or is_write or (psum and eng != o.eng)) and oi != o.idx:
                if o.eng == "pe" and eng == "pe":
                    o.odeps.add(oi)
                else:
                    o.deps.add(oi)
            if is_write and lo <= rlo and rhi <= hi:
                continue
            if (not is_write) and (not w) and rlo == lo and rhi == hi and eng == o.eng and not dma and not o.dma:
                if oi != o.idx:
                    o.odeps.add(oi)
                continue
            keep.append(rec)
        keep.append((lo, hi, o.idx, is_write, o.eng, o.dma))
        self.acc[name] = keep

    def op(self, eng, fn, ins=(), outs=(), dma=False, cost=None, lat=0.0, tbl=None):
        o = _Op(eng, fn, dma)
        o.tbl = tbl
        o.idx = len(self.ops)
        if cost is not None:
            o.cost = cost
        o.lat = lat
        self.ops.append(o)
        for a in ins:
            if a is not None and not isinstance(a, (int, float)):
                self._access(o, a, False)
        for a in outs:
            self._access(o, a, True)
        return o

    def schedule(self):
        import heapq
        ops = self.ops
        n = len(ops)
        succ = [[] for _ in range(n)]
        npred = [0] * n
        for o in ops:
            ps = set(o.deps) | set(o.odeps)
            npred[o.idx] = len(ps)
            for d in ps:
                succ[d].append(o.idx)
        est = [0.0] * n
        rank = [0.0] * n
        import os
        use_cp = os.environ.get("MK_CP", "1") == "1"
        for i_ in range(n - 1, -1, -1):
            o_ = ops[i_]
            m_ = 0.0
            for s_ in succ[i_]:
                if rank[s_] > m_:
                    m_ = rank[s_]
            rank[i_] = o_.cost + o_.lat + m_
        def key(i_):
            return (-rank[i_], i_) if use_cp else (i_, i_)
        fut = {e: [] for e in ENGS}
        avail = {e: [] for e in ENGS}
        free = {e: 0.0 for e in ENGS}
        for o in ops:
            if npred[o.idx] == 0:
                heapq.heappush(fut[o.eng], (0.0, o.idx))
        order = {e: [] for e in ENGS}
        self.tstart = [0.0] * n
        placed = 0
        SEMLAT = float(os.environ.get("MK_SEMLAT", "0.05"))
        TBL_WINDOW = int(os.environ.get("MK_TBLW", "2500"))
        TBL_COST = 1.3
        act_av = {}
        cur_tbl = [None]

        def act_pick():
            heads = [(h[0], k) for k, h in act_av.items() if h]
            if not heads:
                return None
            any_best = min(heads, key=lambda x: x[0])
            same = [x for x in heads if x[1] is None or x[1] == cur_tbl[0]]
            if same:
                sb = min(same, key=lambda x: x[0])
                if sb[0][1] <= any_best[0][1] + TBL_WINDOW:
                    return (sb[0][1], sb[1])
            return (any_best[0][1], any_best[1])

        while placed < n:
            best = None
            for e in ENGS:
                f, a = fut[e], avail[e]
                if e == "act":
                    while f and f[0][0] <= free[e]:
                        ii = heapq.heappop(f)[1]
                        heapq.heappush(act_av.setdefault(ops[ii].tbl, []), (key(ii), ii))
                    pk = act_pick()
                    if pk is not None:
                        cand = (free[e], pk[0], e, True)
                    elif f:
                        cand = (f[0][0], f[0][1], e, False)
                    else:
                        continue
                else:
                    while f and f[0][0] <= free[e]:
                        i2 = heapq.heappop(f)[1]
                        heapq.heappush(a, (key(i2), i2))
                    if a:
                        cand = (free[e], a[0][1], e, True)
                    elif f:
                        cand = (f[0][0], f[0][1], e, False)
                    else:
                        continue
                if best is None or cand[:2] < best[:2]:
                    best = cand
            start, idx, e, from_avail = best
            if e == "act":
                if from_avail:
                    heapq.heappop(act_av[ops[idx].tbl])
                else:
                    heapq.heappop(fut[e])
                t_ = ops[idx].tbl
                if t_ is not None and t_ != cur_tbl[0]:
                    start += TBL_COST
                    cur_tbl[0] = t_
            elif from_avail:
                heapq.heappop(avail[e])
            else:
                heapq.heappop(fut[e])
            o = ops[idx]
            self.tstart[idx] = start
            fin_eng = start + o.cost
            free[e] = fin_eng
            fin = fin_eng + o.lat
            order[e].append(o)
            placed += 1
            for sidx in succ[idx]:
                so = ops[sidx]
                t = fin + (SEMLAT if idx in so.deps else 0.0)
                if idx in so.odeps and idx not in so.deps:
                    t = start
                if t > est[sidx]:
                    est[sidx] = t
                npred[sidx] -= 1
                if npred[sidx] == 0:
                    heapq.heappush(fut[so.eng], (est[sidx], sidx))
        self.est_total = max(free.values())
        return order

    def emit(self):
        nc = self.nc
        ops = self.ops
        if self.do_sched:
            per_eng = self.schedule()
        else:
            per_eng = {e: [o for o in ops if o.eng == e] for e in ENGS}
        for o in ops:
            for d in o.deps:
                ops[d].signal = True
        for e in ENGS:
            last = [o for o in per_eng[e] if not o.dma]
            if last:
                last[-1].signal = True
        cnt = {e: 0 for e in ENGS}
        dcnt = [0] * NDMA_SEMS
        dlast = [None] * NDMA_SEMS
        rr = {"sw": 0, "hw": 0}
        for e in ENGS:
            for o in per_eng[e]:
                if o.dma:
                    if e == "pool":
                        sidx = rr["sw"] % NSW_SEMS
                        rr["sw"] += 1
                    else:
                        sidx = NSW_SEMS + rr["hw"] % (NDMA_SEMS - NSW_SEMS)
                        rr["hw"] += 1
                    o.dsem = sidx
                    o.dprev = dlast[sidx]
                    dlast[sidx] = o.idx
                    dcnt[sidx] += 16
                    o.semval = dcnt[sidx]
                elif o.signal:
                    cnt[e] += 1
                    o.semval = cnt[e]
        hw_queues = [e for e in ENGS if e != "pool" and any(o.dma for o in per_eng[e])]
        assert len(hw_queues) <= 1, hw_queues
        with contextlib.ExitStack() as st:
            esem = {e: st.enter_context(nc.semaphore("s_" + e)) for e in ENGS}
            dsem = [st.enter_context(nc.semaphore("d_%d" % i)) for i in range(NDMA_SEMS)]
            block = st.enter_context(nc.Block())

            def semof(o):
                if o.dma:
                    return ("d", o.dsem), dsem[o.dsem]
                return ("e", o.eng), esem[o.eng]

            def run(eng_name, eng):
                waited = {}
                for o in per_eng[eng_name]:
                    need = {}
                    deps = set(o.deps)
                    if o.dma and o.dprev is not None:
                        deps.add(o.dprev)
                    for d in deps:
                        dop = ops[d]
                        k, s = semof(dop)
                        if need.get(k, (None, 0))[1] < dop.semval:
                            need[k] = (s, dop.semval)
                    for k, (s, v) in need.items():
                        if waited.get(k, 0) >= v:
                            continue
                        eng.wait_ge(s, v)
                        waited[k] = v
                    ins = o.fn(eng)
                    if o.dma:
                        ins.then_inc(dsem[o.dsem], 16)
                    elif o.signal:
                        ins.then_inc(esem[eng_name], 1)
                if eng_name == "sp":
                    for i in range(NDMA_SEMS):
                        if dcnt[i] > 0 and waited.get(("d", i), 0) < dcnt[i]:
                            eng.wait_ge(dsem[i], dcnt[i])
                    for e in ENGS:
                        if e != eng_name and cnt[e] > 0:
                            eng.wait_ge(esem[e], cnt[e])

            @block.sync
            def _(e):
                run("sp", e)

            @block.tensor
            def _(e):
                run("pe", e)

            @block.scalar
            def _(e):
                run("act", e)

            @block.vector
            def _(e):
                run("dve", e)

            @block.gpsimd
            def _(e):
                run("pool", e)


def _cols(v):
    v = np.asarray(v, np.float32).reshape(-1, 128)
    return np.ascontiguousarray(v.T)


PC = {}
_pc_n = 0
for _nm, _n in [("ssd_cw", 48), ("ssd_cb", 12), ("rg_cw", 32), ("rg_cb", 8), ("rg_ba", 8), ("rg_bx", 8),
                ("rg_lam", 8), ("gla_bg2", 4), ("gla_ng", 2), ("lb0", 8), ("lb1", 8), ("hg_ng", 1),
                ("ln1g", 16), ("ln1b", 16), ("ln2g", 16), ("ln2b", 16)]:
    PC[_nm] = _pc_n
    _pc_n += _n
NPC = _pc_n
DC = {"rg_sc": 0, "lb": 8, "oml": 16, "noml": 24, "nbg2": 32}
NDC = 36
PR = {"dtb": 0, "alog": 16, "dd": 32, "ng": 48}
NPR = 48 + 1024
CM = {"ident": 0, "mcumP": 128, "mcumS": 256, "mlastS": 384, "mc64": 512, "rst64": 640, "rst8": 768,
      "ones": 896, "seqm": 1024, "rst128": 1040}
NCM = 1040 + 512


def _pack_pcol(inp):
    parts = []
    cw = np.asarray(inp["ssd_conv_w"][0], np.float32)
    parts.append(np.ascontiguousarray(cw.reshape(4, 12, 128).transpose(2, 1, 0)).reshape(128, 48))
    parts.append(_cols(inp["ssd_conv_b"][0]))
    rw = np.asarray(inp["rg_conv_w"][0], np.float32)
    parts.append(np.ascontiguousarray(rw.reshape(4, 8, 128).transpose(2, 1, 0)).reshape(128, 32))
    parts.append(_cols(inp["rg_conv_b"][0]))
    parts.append(_cols(inp["rg_ba"][0]))
    parts.append(_cols(inp["rg_bx"][0]))
    parts.append(_cols(inp["rg_lambda"][0]))
    parts.append(_cols(inp["gla_bg2"][0]))
    parts.append(_cols(inp["gla_norm_g"][0]))
    parts.append(_cols(inp["hgrn_lb_logits"][0]))
    parts.append(_cols(inp["hgrn_lb_logits"][1]))
    parts.append(_cols(inp["hgrn_norm_g"][0]))
    for nm in ("ln1_g", "ln1_b", "ln2_g", "ln2_b"):
        parts.append(np.concatenate([_cols(inp[nm][0]), _cols(inp[nm][1])], axis=1))
    out = np.ascontiguousarray(np.concatenate(parts, axis=1), dtype=np.float32)
    assert out.shape == (128, NPC), out.shape
    return out


def _pack_prow(inp):
    row = np.concatenate([np.asarray(inp["ssd_dt_bias"][0], np.float32), np.asarray(inp["ssd_a_log"][0], np.float32),
                          np.asarray(inp["ssd_d"][0], np.float32), np.asarray(inp["ssd_norm_g"][0], np.float32)])
    return np.ascontiguousarray(np.broadcast_to(row[None, :], (128, NPR)), dtype=np.float32)


def _const_masks():
    r = np.arange(128)[:, None]
    t = np.arange(128)[None, :]
    m = np.zeros((128, NCM), np.float32)
    m[:, CM["ident"]:CM["ident"] + 128] = (r == t)
    m[:, CM["mcumP"]:CM["mcumP"] + 128] = (r <= t)
    m[:, CM["mcumS"]:CM["mcumS"] + 128] = (r <= t) & (r // 8 == t // 8)
    m[:, CM["mlastS"]:CM["mlastS"] + 128] = (r // 8 == t // 8)
    m[:, CM["mc64"]:CM["mc64"] + 128] = (r <= t) & (r // 64 == t // 64)
    m[:, CM["rst64"]:CM["rst64"] + 128] = (t % 64 != 0)
    m[:, CM["rst8"]:CM["rst8"] + 128] = (t % 8 != 0)
    m[:, CM["ones"]:CM["ones"] + 128] = 1.0
    m[:, CM["seqm"]:CM["seqm"] + 16] = (r // 8 == np.arange(16)[None, :])
    m[:, CM["rst128"]:CM["rst128"] + 512] = (np.arange(512)[None, :] % 128 != 0)
    return m


class Builder:
    def __init__(self):
        self.nc = bass.Bass("TRN2", target_bir_lowering=False)
        self.P = Prog(self.nc)
        self.st = contextlib.ExitStack()

    def din(self, name, shape):
        return self.nc.dram_tensor(name, list(shape), F32, kind="ExternalInput").ap()

    def dout(self, name, shape):
        return self.nc.dram_tensor(name, list(shape), F32, kind="ExternalOutput").ap()

    def sb(self, name, shape, dt):
        return self.st.enter_context(self.nc.sbuf_tensor(name, list(shape), dt))

    def psum(self, name, shape, dt=F32):
        return self.st.enter_context(self.nc.psum_tensor(name, list(shape), dt))

    def mm(self, out, lhsT, rhs, start=True, stop=True):
        c = 0.03 + _fsize(rhs) / 2400.0
        if rhs.dtype == F32:
            c *= 4.0
        self.P.op("pe", lambda e: e.matmul(out, lhsT=lhsT, rhs=rhs, start=start, stop=stop),
                  ins=[lhsT, rhs], outs=[out], cost=max(c, 0.064))

    def tr(self, out, in_):
        k = in_.shape[0]
        ident = self.cm[0:k, CM["ident"]:CM["ident"] + k]
        self.P.op("pe", lambda e: e.transpose(out=out, in_=in_, identity=ident), ins=[in_, ident], outs=[out], cost=0.12)

    def act(self, out, in_, func, bias=None, scale=None, accum=None):
        kw = {}
        if bias is not None:
            kw["bias"] = bias
        if scale is not None:
            kw["scale"] = scale
        if accum is not None:
            kw["accum_out"] = accum
        outs = [out] + ([accum] if accum is not None else [])
        self.P.op("act", lambda e: e.activation(out=out, in_=in_, func=func, **kw), ins=[in_, bias, scale], outs=outs,
                  cost=0.25 + _fsize(out) * 0.00075, tbl=_TBL.get(func))

    def tt(self, eng, out, a, b, op):
        self.P.op(eng, lambda e: e.tensor_tensor(out=out, in0=a, in1=b, op=op), ins=[a, b], outs=[out],
                  cost=self._ecost(eng, out, 1.0))

    def ts(self, eng, out, a, s1, op0, s2=None, op1=None):
        if op1 is None:
            self.P.op(eng, lambda e: e.tensor_scalar(out=out, in0=a, scalar1=s1, scalar2=None, op0=op0),
                      ins=[a, s1], outs=[out], cost=self._ecost(eng, out, 0.6))
        else:
            self.P.op(eng, lambda e: e.tensor_scalar(out=out, in0=a, scalar1=s1, scalar2=s2, op0=op0, op1=op1),
                      ins=[a, s1, s2], outs=[out], cost=self._ecost(eng, out, 0.6))

    def stt(self, out, a, s, b, op0, op1):
        self.P.op("dve", lambda e: e.scalar_tensor_tensor(out=out, in0=a, scalar=s, in1=b, op0=op0, op1=op1),
                  ins=[a, s, b], outs=[out], cost=self._ecost("dve", out, 1.2))

    def cp(self, eng, out, in_):
        if eng == "act":
            self.P.op("act", lambda e: e.copy(out=out, in_=in_), ins=[in_], outs=[out], cost=0.25 + _fsize(out) * 0.00075)
        else:
            self.P.op(eng, lambda e: e.tensor_copy(out=out, in_=in_), ins=[in_], outs=[out],
                      cost=self._ecost(eng, out, 1.5 if eng == "pool" else 0.7))

    def scan(self, out, d0, d1, init):
        self.P.op("dve", lambda e: e.tensor_tensor_scan(out=out, data0=d0, data1=d1, initial=init,
                                                        op0=ALU.mult, op1=ALU.add), ins=[d0, d1, init], outs=[out],
                  cost=0.1 + _fsize(out) * 0.0021)

    def recip(self, out, in_):
        self.P.op("dve", lambda e: e.reciprocal(out=out, in_=in_), ins=[in_], outs=[out], cost=0.1 + _fsize(out) * 0.0065)

    def memset(self, eng, out, val):
        self.P.op(eng, lambda e: e.memset(out, val), outs=[out], cost=0.1 + _fsize(out) * 0.0006)

    def dma(self, q, out, in_):
        if q != "pool":
            q = "sp"
        nbytes = _fsize(out) * 4 * int(out.shape[0])
        self.P.op(q, lambda e: e.dma_start(out=out, in_=in_), ins=[in_], outs=[out], dma=True,
                  cost=(1.0 if q == "pool" else 0.08), lat=2.0 + nbytes / 150000.0)

    def _ecost(self, eng, out, f):
        n = _fsize(out)
        if eng == "pool":
            return 0.12 + n * 0.0023 * max(f, 0.6)
        return 0.08 + n * 0.00105 * max(f, 0.55) / 0.55 * 0.55 if f <= 0.7 else 0.08 + n * 0.00105 * f

    def build(self):
        nc = self.nc
        I = {}
        I["xp"] = self.din("xp", [2048, D])
        I["xs"] = self.din("xs", [128, D])
        I["meta"] = self.din("meta", [16, D])
        I["st_ssd"] = self.din("st_ssd", [16, 1024, 128])
        I["st_ssdc"] = self.din("st_ssdc", [48, 1536])
        I["st_rg"] = self.din("st_rg", [16, 1024])
        I["st_rgc"] = self.din("st_rgc", [48, 1024])
        I["st_gla"] = self.din("st_gla", [16, 4, 128, 256])
        I["st_hg"] = self.din("st_hg", [16, 8, 128, 128])
        I["w_in_e"] = self.din("w_in_e", [D, 4624])
        I["w_out_e"] = self.din("w_out_e", [2048, D])
        I["w_in_o"] = self.din("w_in_o", [D, 7184])
        I["w_out_o"] = self.din("w_out_o", [2048, D])
        I["w1"] = self.din("w1", [2, D, 4096])
        I["w2"] = self.din("w2", [2, 4096, D])
        I["rg_wa"] = self.din("rg_wa", [8, 128, 128])
        I["rg_wx"] = self.din("rg_wx", [8, 128, 128])
        I["wg2"] = self.din("wg2", [16, 512])
        I["pcol"] = self.din("pcol", [128, NPC])
        I["prow"] = self.din("prow", [128, NPR])
        I["cmask"] = self.din("cmask", [128, NCM])
        O = {}
        O["y_p"] = self.dout("y_p", [2048, D])
        O["y_s"] = self.dout("y_s", [128, D])
        O["ssd_p"] = self.dout("ssd_p", [1024, 128])
        O["ssd_s"] = self.dout("ssd_s", [16, 1024, 128])
        O["ssdc"] = self.dout("ssdc", [51, 1536])
        O["rg"] = self.dout("rg", [17, 1024])
        O["rgc"] = self.dout("rgc", [51, 1024])
        O["gla_p"] = self.dout("gla_p", [4, 128, 256])
        O["gla_s"] = self.dout("gla_s", [16, 4, 128, 256])
        O["hg_p"] = self.dout("hg_p", [8, 128, 128])
        O["hg_s"] = self.dout("hg_s", [16, 8, 128, 128])
        self.I, self.O = I, O

        self.h32 = self.sb("h32", [128, 8, NTOK], F32)
        self.hb = self.sb("hb", [128, 8, NTOK], BF16)
        self.wbuf = self.sb("wbuf", [128, 16896], BF16)
        self.pcol = self.sb("pcol_s", [128, NPC], F32)
        self.dcol = self.sb("dcol_s", [128, NDC], F32)
        self.prow = self.sb("prow_s", [128, NPR], F32)
        self.arow = self.sb("arow_s", [128, 16], F32)
        self.cm = self.sb("cm_s", [128, NCM], F32)
        self.AF = self.sb("arenaF", [128, 10240], F32)
        self.AB = self.sb("arenaB", [128, 9216], BF16)
        self.onesb = self.sb("onesb", [128, 128], BF16)
        self.ps = [self.psum("ps%d" % i, [128, 1024]) for i in range(4)]
        self.af_off = 0
        self.ab_off = 0

        import os
        stop = int(os.environ.get("MK_STOP", "99"))
        phases = [self.setup, self.load_inputs,
                  lambda: self.ssd_unit(0, True), lambda: self.ssd_unit(1, False),
                  lambda: [self.rg_unit(b) for b in range(8)],
                  lambda: self.ln(self.PCcol("ln1g", 0), self.PCcol("ln1b", 0)),
                  lambda: self.mlp(0),
                  lambda: self.ln(self.PCcol("ln2g", 0), self.PCcol("ln2b", 0)),
                  self.layer1,
                  lambda: self.ln(self.PCcol("ln1g", 8), self.PCcol("ln1b", 8)),
                  lambda: self.mlp(1),
                  lambda: self.ln(self.PCcol("ln2g", 8), self.PCcol("ln2b", 8), final=True)]
        for pi, ph in enumerate(phases):
            if pi > stop:
                break
            ph()
        self.store_outputs()
        self.P.emit()
        self.st.close()
        return nc

    def reset_arena(self):
        self.af_off = 0
        self.ab_off = 0

    def fa(self, n):
        o = self.af_off
        self.af_off += n
        assert self.af_off <= 10240, self.af_off
        return self.AF[:, o:o + n]

    def ba(self, n):
        o = self.ab_off
        self.ab_off += n + (n % 2)
        assert self.ab_off <= 9216, self.ab_off
        return self.AB[:, o:o + n]

    def bank(self, i):
        return self.ps[i // 2][:, (i % 2) * 512:(i % 2) * 512 + 512]

    def PCcol(self, name, j=0):
        o = PC[name] + j
        return o

    def pc(self, name, j=0, p=128):
        o = PC[name] + j
        return self.pcol[0:p, o:o + 1]

    def dc(self, name, j=0):
        o = DC[name] + j
        return self.dcol[:, o:o + 1]

    def setup(self):
        I = self.I
        self.dma("sp", self.pcol[:], I["pcol"])
        self.dma("sp", self.prow[:], I["prow"])
        self.dma("sp", self.cm[:], I["cmask"])
        self.memset("pool", self.onesb[:, :], 1.0)
        self.act(self.arow[:], self.prow[:, PR["alog"]:PR["alog"] + 16], AF.Exp)
        self.ts("dve", self.arow[:], self.arow[:], -1.0, ALU.mult)
        sc = self.dcol[:, DC["rg_sc"]:DC["rg_sc"] + 8]
        self.act(sc, self.pcol[:, PC["rg_lam"]:PC["rg_lam"] + 8], AF.Exp, scale=-1.0)
        self.act(sc, sc, AF.Ln, bias=1.0)
        self.ts("dve", sc, sc, -8.0, ALU.mult)
        lb = self.dcol[:, DC["lb"]:DC["lb"] + 8]
        oml = self.dcol[:, DC["oml"]:DC["oml"] + 8]
        noml = self.dcol[:, DC["noml"]:DC["noml"] + 8]
        self.tt("dve", lb, self.pcol[:, PC["lb1"]:PC["lb1"] + 8], self.pcol[:, PC["lb0"]:PC["lb0"] + 8], ALU.subtract)
        self.act(lb, lb, AF.Sigmoid)
        self.ts("dve", oml, lb, -1.0, ALU.mult, 1.0, ALU.add)
        self.ts("dve", noml, lb, -1.0, ALU.add)
        self.ts("dve", self.dcol[:, DC["nbg2"]:DC["nbg2"] + 4], self.pcol[:, PC["gla_bg2"]:PC["gla_bg2"] + 4], -1.0, ALU.mult)

    def load_inputs(self):
        I = self.I
        self.reset_arena()
        stg = [self.fa(1024), self.fa(1024)]
        import os
        tsel = os.environ.get("MK_TSEL", "")
        for ti, (c0, n, kind) in enumerate(TILES):
            if tsel and str(ti) not in tsel.split(","):
                continue
            s = stg[ti % 2]
            if kind == "m":
                src = I["meta"]
            elif kind == "p":
                r0 = c0 - 16
                src = I["xp"][r0:r0 + n, :]
            else:
                src = I["xs"]
            import os
            li = int(os.environ.get("MK_LI", "9"))
            self.dma("sp", s[0:n, :], src)
            pp = self.ps[ti % 2]
            if li >= 1:
                for c in range(8):
                    self.tr(pp[:, c * 128:c * 128 + n], s[0:n, c * 128:(c + 1) * 128])
            pv = pp[:, :].rearrange("p (c t) -> p c t", c=8)[:, :, 0:n]
            if li >= 2 and li != 6:
                self.cp("act", self.h32[:, :, c0:c0 + n], pv)
            if li == 6:
                self.cp("dve", self.h32[:, :, c0:c0 + n], pv)
            if li == 3:
                self.cp("dve", self.hb[:, :, c0:c0 + n], self.h32[:, :, c0:c0 + n])
            if li == 4:
                self.cp("pool", self.hb[:, :, c0:c0 + n], self.h32[:, :, c0:c0 + n])
            if li == 5:
                for c in range(8):
                    self.cp("dve", self.hb[:, c, c0:c0 + n], pp[:, c * 128:c * 128 + n])
            if li >= 9:
                self.cp("dve", self.hb[:, :, c0:c0 + n], pv)

    def wload(self, dst, src):
        self.dma("pool", dst, src)

    def wview(self, off, kc, ncol):
        return self.wbuf[:, off:off + kc * ncol].rearrange("p (k j) -> p k j", k=kc)

    def inproj_fm(self, out_ps, W, kcs, colsl, c0, n):
        for k in range(kcs):
            self.mm(out_ps, W[:, k, colsl], self.hb[:, k, c0:c0 + n], start=(k == 0), stop=(k == kcs - 1))

    def inproj_tm(self, out_ps, W, colsl, c0, n):
        for k in range(8):
            self.mm(out_ps, self.hb[:, k, c0:c0 + n], W[:, k, colsl], start=(k == 0), stop=(k == 7))

    def outproj_acc(self, Wo, kcs, yT, c0, n, first):
        po = self.ps[0]
        for oc in range(8):
            for k in range(kcs):
                self.mm(po[:, oc * 128:oc * 128 + n], Wo[:, k, oc * 128:(oc + 1) * 128], yT[:, k, 0:n],
                        start=(k == 0), stop=(k == kcs - 1))
        pv = po[:, :].rearrange("p (c t) -> p c t", c=8)[:, :, 0:n]
        hv = self.h32[:, :, c0:c0 + n]
        if first:
            self.stt(hv, hv, ALPHA, pv, ALU.mult, ALU.add)
        else:
            self.tt("dve", hv, hv, pv, ALU.add)

    def outproj_wide(self, Wo, kcs, yT, c0, n, first, banks=(0, 1, 6, 7)):
        for oc in range(8):
            po = self.bank(banks[oc % len(banks)])
            for k in range(kcs):
                self.mm(po[:, 0:n], Wo[:, k, oc * 128:(oc + 1) * 128], yT[:, k, 0:n], start=(k == 0), stop=(k == kcs - 1))
            hv = self.h32[:, oc, c0:c0 + n]
            if first:
                self.stt(hv, hv, ALPHA, po[:, 0:n], ALU.mult, ALU.add)
            else:
                self.tt("dve", hv, hv, po[:, 0:n], ALU.add)

    def layer0(self):
        for g in range(2):
            self.ssd_unit(g, first=(g == 0))
        for blk in range(8):
            self.rg_unit(blk)

    def ssd_unit(self, g, first):
        I, O = self.I, self.O
        self.reset_arena()
        cm = self.cm
        Wz = self.wview(0, 8, 512)
        Wx = self.wview(4096, 8, 512)
        WB = self.wview(8192, 8, 128)
        WC = self.wview(9216, 8, 128)
        Wdt = self.wview(10240, 8, 8)
        Wo = self.wview(10304, 4, 1024)
        wie = I["w_in_e"]

        def wsrc(c0, ncol):
            return wie[:, c0:c0 + ncol].rearrange("(k p) j -> p k j", p=128)
        self.wload(Wx, wsrc(1024 + g * 512, 512))
        self.wload(WB, wsrc(2048 + g * 128, 128))
        self.wload(WC, wsrc(2304 + g * 128, 128))
        self.wload(Wz, wsrc(g * 512, 512))
        self.wload(Wdt, wsrc(2560 + g * 8, 8))
        self.wload(Wo, I["w_out_e"][g * 512:(g + 1) * 512, :].rearrange("(k p) j -> p k j", p=128))
        cch = [g * 4 + i for i in range(4)] + [8 + g, 10 + g]
        ub = [self.fa(1056), self.fa(1056)]
        xc = self.fa(6 * 128).rearrange("p (c t) -> p c t", c=6)
        xtok = self.fa(512)
        zs = self.fa(512)
        cbm = self.fa(128)
        segc = self.fa(512)
        dec = self.fa(512)
        yy = self.fa(512)
        t1 = self.fa(512)
        S = self.fa(512)
        stg = self.fa(512)
        sm = self.fa(128)
        ebl = self.fa(8)
        tail = self.fa(6 * 51).rearrange("p (c t) -> p c t", c=6)
        cst = self.fa(1536)
        ss = self.fa(2)
        ebl_all = self.fa(128)
        BT = self.ba(128)
        CT = self.ba(128)
        Btok = self.ba(128)
        xd = self.ba(512)
        xdte = self.ba(512)
        MT = self.ba(1024)
        yT = self.ba(512).rearrange("p (k t) -> p k t", k=4)
        Sb = self.ba(512)
        Sb2 = self.ba(512)
        cpf = self.ba(2176)
        Cpad = cpf[:, 0:2048].rearrange("p (s t) -> p s t", s=16)
        Cdiag = cpf[:, 0:2176].rearrange("p (a b) -> p a b", b=136)[:, :, 0:8]
        Bm = self.ba(2048).rearrange("p (s t) -> p s t", s=16)
        dt_, la, nla, bb, blb, eb, te, dtte, nbb = [sm[:, 8 * i:8 * i + 8] for i in range(9)]

        self.memset("dve", S, 0.0)
        self.memset("dve", Sb, 0.0)
        self.memset("dve", ub[1][:, 0:1056], 0.0)
        self.dma("sp", cst[0:48, :], I["st_ssdc"])

        prev_n = None
        for ti, (c0, n, kind) in enumerate(TILES):
            u = ub[ti % 2]
            up = ub[(ti + 1) % 2]
            if kind == "s":
                uv = u[:, 0:1056].rearrange("p (c s t) -> p c s t", c=6, s=16)
                pt = self.bank(4)
                for i, ch in enumerate(cch):
                    self.tr(pt[:, i * 48:(i + 1) * 48], cst[0:48, ch * 128:(ch + 1) * 128])
                self.cp("act", uv[:, :, :, 0:3],
                        pt[:, 0:288].rearrange("p (c s t) -> p c s t", c=6, s=16))
            else:
                uv = u[:, 0:6 * 131].rearrange("p (c t) -> p c t", c=6)
                if prev_n is not None:
                    upv = up[:, 0:6 * 131].rearrange("p (c t) -> p c t", c=6)
                    self.cp("pool", uv[:, :, 0:3], upv[:, :, prev_n:prev_n + 3])
                else:
                    self.memset("pool", uv[:, :, 0:3], 0.0)
            for i in range(6):
                pb = self.bank(2 + (i % 2))
                if i < 4:
                    self.inproj_fm(pb[:, 0:n], Wx, 8, slice(i * 128, (i + 1) * 128), c0, n)
                elif i == 4:
                    self.inproj_fm(pb[:, 0:n], WB, 8, slice(0, 128), c0, n)
                else:
                    self.inproj_fm(pb[:, 0:n], WC, 8, slice(0, 128), c0, n)
                if kind == "s":
                    self.cp("act", uv[:, i, :, 3:11], pb[:, 0:128].rearrange("p (s t) -> p s t", s=16))
                else:
                    self.cp("act", uv[:, i, 3:3 + n], pb[:, 0:n])
            for i, ch in enumerate(cch):
                w = [self.pcol[:, PC["ssd_cw"] + ch * 4 + j:PC["ssd_cw"] + ch * 4 + j + 1] for j in range(4)]
                bcol = self.pcol[:, PC["ssd_cb"] + ch:PC["ssd_cb"] + ch + 1]
                if kind == "s":
                    o_ = xc[:, i, :].rearrange("p (s t) -> p s t", s=16)
                    src = lambda j: uv[:, i, :, j:j + 8]
                else:
                    o_ = xc[:, i, 0:n]
                    src = lambda j: uv[:, i, j:j + n]
                self.ts("dve", o_, src(3), w[3], ALU.mult, bcol, ALU.add)
                for j in (2, 1, 0):
                    self.stt(o_, src(j), w[j], o_, ALU.mult, ALU.add)
            self.act(xc[:, 0:4, 0:n], xc[:, 0:4, 0:n], AF.Silu)
            self.act(BT[:, 0:n], xc[:, 4, 0:n], AF.Silu)
            self.act(CT[:, 0:n], xc[:, 5, 0:n], AF.Silu)
            self.act(xc[:, 4, 0:n], xc[:, 4, 0:n], AF.Silu)
            pt = self.bank(4)
            for i in range(4):
                self.tr(pt[0:n, i * 128:(i + 1) * 128], xc[:, i, 0:n])
            self.cp("act", xtok[0:n, :], pt[0:n, 0:512])
            pt2 = self.bank(5)
            self.tr(pt2[0:n, 0:128], xc[:, 4, 0:n])
            self.cp("act", Btok[0:n, :], pt2[0:n, 0:128])
            pd = self.bank(5)
            self.inproj_tm(pd[0:n, 128:136], Wdt, slice(0, 8), c0, n)
            self.tt("dve", dt_[0:n, :], pd[0:n, 128:136], self.prow[0:n, PR["dtb"] + g * 8:PR["dtb"] + g * 8 + 8], ALU.add)
            self.act(dt_[0:n, :], dt_[0:n, :], AF.Exp)
            self.act(dt_[0:n, :], dt_[0:n, :], AF.Ln, bias=1.0)
            self.tt("dve", la[0:n, :], dt_[0:n, :], self.arow[0:n, g * 8:g * 8 + 8], ALU.mult)
            self.ts("dve", nla[0:n, :], la[0:n, :], -1.0, ALU.mult)
            pz = self.bank(6)
            self.inproj_tm(pz[0:n, 0:512], Wz, slice(0, 512), c0, n)
            self.act(zs[0:n, :], pz[0:n, 0:512], AF.Silu)
            if kind == "s":
                mcum = cm[0:n, CM["mcumS"]:CM["mcumS"] + n]
                mlast = cm[0:n, CM["mlastS"]:CM["mlastS"] + n]
            else:
                mcum = cm[0:n, CM["mcumP"]:CM["mcumP"] + n]
                mlast = cm[0:n, CM["ones"]:CM["ones"] + n]
            pc_ = self.bank(5)
            self.mm(pc_[0:n, 256:264], mcum, la[0:n, :])
            self.mm(pc_[0:n, 264:272], mlast, la[0:n, :])
            self.cp("dve", bb[0:n, :], pc_[0:n, 256:264])
            self.ts("dve", nbb[0:n, :], bb[0:n, :], -1.0, ALU.mult, 0.0, ALU.add)
            self.act(eb[0:n, :], pc_[0:n, 256:264], AF.Exp)
            self.tt("dve", te[0:n, :], pc_[0:n, 264:272], bb[0:n, :], ALU.subtract)
            self.act(te[0:n, :], te[0:n, :], AF.Exp)
            self.tt("dve", dtte[0:n, :], dt_[0:n, :], te[0:n, :], ALU.mult)
            xv = xtok[0:n, :].rearrange("p (h d) -> p h d", h=8)
            self.tt("dve", xd[0:n, :].rearrange("p (h d) -> p h d", h=8), xv,
                    dt_[0:n, :].unsqueeze(2).broadcast_to([n, 8, 64]), ALU.mult)
            self.tt("dve", xdte[0:n, :].rearrange("p (h d) -> p h d", h=8), xv,
                    dtte[0:n, :].unsqueeze(2).broadcast_to([n, 8, 64]), ALU.mult)
            pcb = self.bank(5)
            self.mm(pcb[0:n, 384:384 + n], BT[:, 0:n], CT[:, 0:n])
            self.tt("dve", cbm[0:n, 0:n], pcb[0:n, 384:384 + n], mcum, ALU.mult)
            py = self.bank(7)
            for q in range(2):
                psg = self.bank(2 + q)
                for hh in range(4):
                    h = q * 4 + hh
                    o_ = psg[0:n, hh * 128:hh * 128 + n]
                    self.mm(o_, la[0:n, h:h + 1].broadcast_to([n, n]), mcum, start=True, stop=True)
                sv = psg[0:n, :].rearrange("p (h t) -> p h t", h=4)[:, :, 0:n]
                segv = segc[0:n, :].rearrange("p (h t) -> p h t", h=4)[:, :, 0:n]
                decv = dec[0:n, :].rearrange("p (h t) -> p h t", h=4)[:, :, 0:n]
                mtv = MT[0:n, q * 512:(q + 1) * 512].rearrange("p (h t) -> p h t", h=4)[:, :, 0:n]
                for hh in range(4):
                    h = q * 4 + hh
                    self.ts("dve", segv[:, hh, :], sv[:, hh, :], nbb[0:n, h:h + 1], ALU.add, 0.0, ALU.min)
                self.act(decv, segv, AF.Exp)
                self.tt("dve", mtv, decv, cbm[0:n, 0:n].unsqueeze(1).broadcast_to([n, 4, n]), ALU.mult)
                for hh in range(4):
                    h = q * 4 + hh
                    self.mm(py[0:n, h * 64:(h + 1) * 64], MT[0:n, q * 512 + hh * 128:q * 512 + hh * 128 + n],
                            xd[0:n, h * 64:(h + 1) * 64])
            pyi = self.bank(6)
            pu = self.bank(3)
            if kind != "s":
                self.mm(pyi[0:n, 0:512], CT[:, 0:n], Sb[:, :])
                pe_ = self.bank(5)
                self.mm(pe_[:, 272:280], cm[0:n, CM["ones"]:CM["ones"] + 128], la[0:n, :])
                self.act(ebl[:, :], pe_[:, 272:280], AF.Exp)
                self.mm(pu[:, 0:512], Btok[0:n, :], xdte[0:n, :])
                self.tt("dve", S.rearrange("p (h d) -> p h d", h=8), S.rearrange("p (h d) -> p h d", h=8),
                        ebl[:, :].unsqueeze(2).broadcast_to([128, 8, 64]), ALU.mult)
                self.tt("dve", S, S, pu[:, 0:512], ALU.add)
                self.cp("act", Sb, S)
            else:
                self.memset("pool", Cpad[:, :, :], 0.0)
                self.cp("pool", Cdiag, CT[:, 0:128].rearrange("p (s t) -> p s t", s=16))
                self.tt("pool", Bm[:, :, :], Btok[:, :].unsqueeze(1).broadcast_to([128, 16, 128]),
                        cm[:, CM["seqm"]:CM["seqm"] + 16].unsqueeze(2).broadcast_to([128, 16, 128]), ALU.mult)
                pe_ = self.bank(5)
                for s in range(16):
                    self.mm(pe_[:, s * 8:(s + 1) * 8], cm[:, CM["seqm"] + s:CM["seqm"] + s + 1].broadcast_to([128, 128]), la[:, :])
                self.act(ebl_all[:, :], pe_[:, 0:128], AF.Exp)
                for s in range(16):
                    par = s % 2
                    sin = (stg, segc)[par][:, 0:512].rearrange("p (q n) -> p q n", q=4)
                    Ss = (S, dec)[par]
                    Sbs = (Sb, Sb2)[par]
                    outs_ = (t1, cst[:, 0:512])[par]
                    self.dma("sp", sin, I["st_ssd"][s, g * 512:(g + 1) * 512, :].rearrange("(q p) n -> p q n", p=128))
                    ptr = self.bank((4, 2)[par])
                    for q in range(4):
                        self.tr(ptr[:, q * 128:(q + 1) * 128], sin[:, q, :])
                    self.cp("act", Sbs, ptr[:, 0:512])
                    self.mm(pyi[0:n, 0:512], Cpad[:, s, :], Sbs[:, :], start=(s == 0), stop=(s == 15))
                    pus = self.bank((3, 0)[par])
                    self.mm(pus[:, 0:512], Bm[:, s, :], xdte[:, :])
                    self.tt("dve", Ss.rearrange("p (h d) -> p h d", h=8), ptr[:, 0:512].rearrange("p (h d) -> p h d", h=8),
                            ebl_all[:, s * 8:(s + 1) * 8].unsqueeze(2).broadcast_to([128, 8, 64]), ALU.mult)
                    self.tt("dve", Ss, Ss, pus[:, 0:512], ALU.add)
                    pto = self.bank(1)
                    for q in range(4):
                        self.tr(pto[:, q * 128:(q + 1) * 128], Ss[:, q * 128:(q + 1) * 128])
                    self.cp("act", outs_, pto[:, 0:512])
                    self.dma("act", O["ssd_s"][s, g * 512:(g + 1) * 512, :].rearrange("(q p) n -> p q n", p=128),
                             outs_.rearrange("p (q n) -> p q n", q=4))
            yv = yy[0:n, :].rearrange("p (h d) -> p h d", h=8)
            self.tt("dve", yv, pyi[0:n, 0:512].rearrange("p (h d) -> p h d", h=8),
                    eb[0:n, :].unsqueeze(2).broadcast_to([n, 8, 64]), ALU.mult)
            self.tt("dve", yy[0:n, :], yy[0:n, :], py[0:n, 0:512], ALU.add)
            self.tt("pool", t1[0:n, :].rearrange("p (h d) -> p h d", h=8), xv,
                    self.prow[0:n, PR["dd"] + g * 8:PR["dd"] + g * 8 + 8].unsqueeze(2).broadcast_to([n, 8, 64]), ALU.mult)
            self.tt("dve", yy[0:n, :], yy[0:n, :], t1[0:n, :], ALU.add)
            self.tt("dve", yy[0:n, :], yy[0:n, :], zs[0:n, :], ALU.mult)
            self.act(t1[0:n, :], yy[0:n, :], AF.Square, accum=ss[0:n, 0:1])
            self.act(ss[0:n, 1:2], ss[0:n, 0:1], AF.Sqrt, scale=1.0 / 512.0, bias=RMS_EPS)
            self.recip(ss[0:n, 1:2], ss[0:n, 1:2])
            self.stt(yy[0:n, :], yy[0:n, :], ss[0:n, 1:2], self.prow[0:n, PR["ng"] + g * 512:PR["ng"] + (g + 1) * 512],
                     ALU.mult, ALU.mult)
            pt = self.bank(4)
            for q in range(4):
                self.tr(pt[:, q * 128:q * 128 + n], yy[0:n, q * 128:(q + 1) * 128])
            self.cp("act", yT[:, :, 0:n], pt[:, 0:512].rearrange("p (k t) -> p k t", k=4)[:, :, 0:n])
            self.outproj_acc(Wo, 4, yT, c0, n, first)
            if kind == "s":
                self.cp("pool", tail[:, :, 0:48].rearrange("p c (s t) -> p c s t", s=16), uv[:, :, :, 8:11])
            elif ti == 16:
                self.cp("pool", tail[:, :, 48:51], uv[:, :, n:n + 3])
                pto = self.bank(1)
                for q in range(4):
                    self.tr(pto[:, q * 128:(q + 1) * 128], S[:, q * 128:(q + 1) * 128])
                self.cp("act", t1, pto[:, 0:512])
                self.dma("act", O["ssd_p"][g * 512:(g + 1) * 512, :].rearrange("(q p) n -> p q n", p=128),
                         t1.rearrange("p (q n) -> p q n", q=4))
            prev_n = n
        pt = self.bank(4)
        for i, ch in enumerate(cch):
            self.tr(pt[0:51, (i % 4) * 128:(i % 4) * 128 + 128], tail[:, i, :])
            self.cp("act", stg[0:51, (i % 4) * 128:(i % 4) * 128 + 128], pt[0:51, (i % 4) * 128:(i % 4) * 128 + 128])
            self.dma("act", O["ssdc"][:, ch * 128:(ch + 1) * 128], stg[0:51, (i % 4) * 128:(i % 4) * 128 + 128])

    def rg_unit(self, blk):
        I, O = self.I, self.O
        self.reset_arena()
        half = 5632 * (blk % 3)
        Wg = self.wview(half + 0, 8, 128)
        Wxr = self.wview(half + 1024, 8, 128)
        Wa = self.wbuf[:, half + 2048:half + 2176]
        Wx2 = self.wbuf[:, half + 2176:half + 2304]
        Wo = self.wview(half + 2304, 1, 1024)
        wie = I["w_in_e"]
        self.wload(Wxr, wie[:, 3600 + blk * 128:3600 + (blk + 1) * 128].rearrange("(k p) j -> p k j", p=128))
        self.wload(Wa, I["rg_wa"][blk])
        self.wload(Wx2, I["rg_wx"][blk])
        self.wload(Wg, wie[:, 2576 + blk * 128:2576 + (blk + 1) * 128].rearrange("(k p) j -> p k j", p=128))
        self.wload(Wo, I["w_out_e"][1024 + blk * 128:1024 + (blk + 1) * 128, :].rearrange("(k p) j -> p k j", p=128))
        RT = [(0, 16, "m")] + [(16 + 512 * j, 512, "p") for j in range(4)] + [(2064, 128, "s")]
        sets = []
        for i in range(2):
            d = {}
            d["ub"] = self.fa(516)
            for nm in ("xr", "rr", "ii", "aa", "gt", "hs", "gg"):
                d[nm] = self.fa(512)
            d["xrb"] = self.ba(512)
            d["yT"] = self.ba(512).rearrange("p (k t) -> p k t", k=1)
            sets.append(d)
        st0 = self.fa(128)
        cst = self.fa(128)
        sT = self.fa(16)
        fin = self.fa(17)
        tail = self.fa(51)
        stg = self.fa(128)
        stg2 = self.fa(128)
        hprev = self.fa(1)
        self.dma("sp", st0[0:16, :], I["st_rg"][:, blk * 128:(blk + 1) * 128])
        self.dma("sp", cst[0:48, :], I["st_rgc"][:, blk * 128:(blk + 1) * 128])
        self.memset("dve", hprev, 0.0)
        cw = [self.pcol[:, PC["rg_cw"] + blk * 4 + j:PC["rg_cw"] + blk * 4 + j + 1] for j in range(4)]
        cb = self.pc("rg_cb", blk)

        def stage_a(ti):
            c0, n, kind = RT[ti]
            d = sets[ti % 2]
            dp = sets[(ti + 1) % 2]
            u = d["ub"]
            xr, rr, ii, aa, gt, hs, gg, xrb, yT = (d[k] for k in ("xr", "rr", "ii", "aa", "gt", "hs", "gg", "xrb", "yT"))
            if kind == "s":
                uv = u[:, 0:176].rearrange("p (s t) -> p s t", s=16)
                pt = self.bank(4)
                self.tr(pt[:, 0:48], cst[0:48, :])
                self.cp("act", uv[:, :, 0:3], pt[:, 0:48].rearrange("p (s t) -> p s t", s=16))
                self.tr(pt[:, 64:80], st0[0:16, :])
                self.cp("act", sT[:, :], pt[:, 64:80])
            else:
                uv = u
                if ti > 0:
                    pn = RT[ti - 1][1]
                    self.cp("pool", uv[:, 0:3], dp["ub"][:, pn:pn + 3])
                else:
                    self.memset("pool", uv[:, 0:3], 0.0)
            pb = self.bank(2)
            self.inproj_fm(pb[:, 0:n], Wxr, 8, slice(0, 128), c0, n)
            if kind == "s":
                self.cp("act", uv[:, :, 3:11], pb[:, 0:128].rearrange("p (s t) -> p s t", s=16))
                o_ = xr[:, 0:128].rearrange("p (s t) -> p s t", s=16)
                t_ = gg[:, 0:128].rearrange("p (s t) -> p s t", s=16)
                src = lambda j: uv[:, :, j:j + 8]
            else:
                self.cp("act", uv[:, 3:3 + n], pb[:, 0:n])
                o_ = xr[:, 0:n]
                t_ = gg[:, 0:n]
                src = lambda j: uv[:, j:j + n]
            pgt = self.bank(5)
            self.inproj_fm(pgt[:, 0:n], Wg, 8, slice(0, 128), c0, n)
            self.ts("pool", o_, src(3), cw[3], ALU.mult, cb, ALU.add)
            for j in (2, 1):
                self.ts("pool", t_, src(j), cw[j], ALU.mult, 0.0, ALU.add)
                self.tt("pool", o_, o_, t_, ALU.add)
            self.stt(o_, src(0), cw[0], o_, ALU.mult, ALU.add)
            self.cp("dve", xrb[:, 0:n], xr[:, 0:n])
            pg = self.bank(3)
            pg2 = self.bank(4)
            self.mm(pg[:, 0:n], Wa, xrb[:, 0:n])
            self.mm(pg2[:, 0:n], Wx2, xrb[:, 0:n])
            self.act(rr[:, 0:n], pg[:, 0:n], AF.Sigmoid, bias=self.pc("rg_ba", blk))
            self.act(ii[:, 0:n], pg2[:, 0:n], AF.Sigmoid, bias=self.pc("rg_bx", blk))
            self.act(aa[:, 0:n], rr[:, 0:n], AF.Exp, scale=self.dc("rg_sc", blk))
            self.act(rr[:, 0:n], aa[:, 0:n], AF.Square)
            self.act(rr[:, 0:n], rr[:, 0:n], AF.Sqrt, scale=-1.0, bias=1.0)
            self.tt("pool", gt[:, 0:n], ii[:, 0:n], xr[:, 0:n], ALU.mult)
            self.tt("dve", gt[:, 0:n], gt[:, 0:n], rr[:, 0:n], ALU.mult)
            if kind == "s":
                for s_ in range(16):
                    self.scan(hs[:, s_ * 8:(s_ + 1) * 8], aa[:, s_ * 8:(s_ + 1) * 8], gt[:, s_ * 8:(s_ + 1) * 8], sT[:, s_:s_ + 1])
                self.cp("pool", fin[:, 0:16].unsqueeze(2), hs[:, 0:128].rearrange("p (s t) -> p s t", s=16)[:, :, 7:8])
            else:
                self.scan(hs[:, 0:n], aa[:, 0:n], gt[:, 0:n], hprev[:, 0:1])
                self.cp("pool", hprev[:, 0:1], hs[:, n - 1:n])
                if ti == 4:
                    self.cp("pool", fin[:, 16:17], hs[:, n - 1:n])
            self.act(gg[:, 0:n], pgt[:, 0:n], AF.Gelu)
            self.tt("dve", yT[:, 0, 0:n], hs[:, 0:n], gg[:, 0:n], ALU.mult)
            if kind == "s":
                self.cp("pool", tail[:, 0:48].rearrange("p (s t) -> p s t", s=16), uv[:, :, 8:11])
            elif ti == 4:
                self.cp("pool", tail[:, 48:51], uv[:, n:n + 3])

        def stage_b(ti):
            c0, n, kind = RT[ti]
            self.outproj_wide(Wo, 1, sets[ti % 2]["yT"], c0, n, False)

        stage_a(0)
        for ti in range(len(RT)):
            if ti + 1 < len(RT):
                stage_a(ti + 1)
            stage_b(ti)
        pt = self.bank(4)
        self.tr(pt[0:51, 0:128], tail[:, :])
        self.cp("act", stg[0:51, :], pt[0:51, 0:128])
        self.dma("act", O["rgc"][:, blk * 128:(blk + 1) * 128], stg[0:51, :])
        pt2 = self.bank(5)
        self.tr(pt2[0:17, 0:128], fin[:, :])
        self.cp("act", stg2[0:17, :], pt2[0:17, 0:128])
        self.dma("act", O["rg"][:, blk * 128:(blk + 1) * 128], stg2[0:17, :])

    def ln(self, gcol, bcol, final=False):
        self.reset_arena()
        onesb = self.onesb[:, :]
        W = 256
        tiles = [(c, min(W, NTOK - c)) for c in range(0, NTOK, W)]
        bs = [(self.ba(8 * W).rearrange("p (c t) -> p c t", c=8), self.ba(8 * W).rearrange("p (c t) -> p c t", c=8))
              for _ in range(2)]
        sm = [[self.fa(W) for _ in range(4)] for _ in range(2)]
        for ti, (c0, w) in enumerate(tiles):
            mean, rstd, mr, tmp = sm[ti % 2]
            xb, sqb = bs[ti % 2]
            hv = self.h32[:, :, c0:c0 + w]
            pb = self.bank(2 + (ti % 2))
            p1 = pb[:, 0:w]
            p2 = pb[:, 256:256 + w]
            self.cp("dve", xb[:, :, 0:w], hv)
            self.act(sqb[:, :, 0:w], hv, AF.Square)
            for c in range(8):
                self.mm(p1, onesb, xb[:, c, 0:w], start=(c == 0), stop=(c == 7))
            for c in range(8):
                self.mm(p2, onesb, sqb[:, c, 0:w], start=(c == 0), stop=(c == 7))
            self.act(mean[:, 0:w], p1, AF.Copy, scale=1.0 / D)
            self.tt("pool", tmp[:, 0:w], mean[:, 0:w], mean[:, 0:w], ALU.mult)
            self.stt(rstd[:, 0:w], p2, 1.0 / D, tmp[:, 0:w], ALU.mult, ALU.subtract)
            self.act(rstd[:, 0:w], rstd[:, 0:w], AF.Ln, bias=LN_EPS)
            self.act(rstd[:, 0:w], rstd[:, 0:w], AF.Exp, scale=-0.5)
            self.tt("pool", mr[:, 0:w], mean[:, 0:w], rstd[:, 0:w], ALU.mult)
            self.tt("dve", hv, hv, rstd[:, 0:w].unsqueeze(1).broadcast_to([128, 8, w]), ALU.mult)
            self.tt("pool", self.h32[:, 0:5, c0:c0 + w], self.h32[:, 0:5, c0:c0 + w],
                    mr[:, 0:w].unsqueeze(1).broadcast_to([128, 5, w]), ALU.subtract)
            self.tt("dve", self.h32[:, 5:8, c0:c0 + w], self.h32[:, 5:8, c0:c0 + w],
                    mr[:, 0:w].unsqueeze(1).broadcast_to([128, 3, w]), ALU.subtract)
            for c in range(8):
                hc = self.h32[:, c, c0:c0 + w]
                gc = self.pcol[:, gcol + c:gcol + c + 1]
                bc = self.pcol[:, bcol + c:bcol + c + 1]
                self.act(hc, hc, AF.Identity, scale=gc, bias=bc)
            if not final:
                self.cp("dve", self.hb[:, :, c0:c0 + w], hv)

    def mlp(self, layer):
        I = self.I
        for f in range(8):
            self.reset_arena()
            half = 8448 * (f % 2)
            W1 = self.wview(half, 8, 512)
            W2 = self.wview(half + 4096, 4, 1024)
            self.wload(W1, I["w1"][layer, :, f * 512:(f + 1) * 512].rearrange("(k p) j -> p k j", p=128))
            self.wload(W2, I["w2"][layer, f * 512:(f + 1) * 512, :].rearrange("(k p) j -> p k j", p=128))
            rbuf = self.fa(2048).rearrange("p (c t) -> p c t", c=4)
            abuf = self.ba(2048).rearrange("p (c t) -> p c t", c=4)
            for (c0, w) in CT512:
                for fc in range(4):
                    pb = self.bank(2 + (fc % 2))
                    self.inproj_fm(pb[:, 0:w], W1, 8, slice(fc * 128, (fc + 1) * 128), c0, w)
                    self.act(rbuf[:, fc, 0:w], pb[:, 0:w], AF.Relu)
                    self.tt("pool", abuf[:, fc, 0:w], rbuf[:, fc, 0:w], rbuf[:, fc, 0:w], ALU.mult)
                for oc in range(8):
                    po = self.bank(4 + (oc % 4))
                    for k in range(4):
                        self.mm(po[:, 0:w], W2[:, k, oc * 128:(oc + 1) * 128], abuf[:, k, 0:w], start=(k == 0), stop=(k == 3))
                    hv = self.h32[:, oc, c0:c0 + w]
                    if f == 0:
                        self.stt(hv, hv, ALPHA, po[:, 0:w], ALU.mult, ALU.add)
                    else:
                        self.tt("dve", hv, hv, po[:, 0:w], ALU.add)

    def layer1(self):
        for h in range(4):
            self.gla_unit(h, gla=True, first=(h == 0))
        for h in range(8):
            self.gla_unit(h, gla=False, first=False)

    def gla_unit(self, h, gla, first):
        I, O = self.I, self.O
        self.reset_arena()
        cm = self.cm
        wio = I["w_in_o"]
        uidx = h if gla else 4 + h
        half = 8448 * (uidx % 2) if gla else 5632 * (h % 3)
        dv = 256 if gla else 128
        nvc = dv // 128

        def wsrc(c0, ncol):
            return wio[:, c0:c0 + ncol].rearrange("(k p) j -> p k j", p=128)
        if gla:
            Wq = self.wview(half, 8, 128)
            Wk = self.wview(half + 1024, 8, 128)
            Wv = self.wview(half + 2048, 8, 256)
            Wg = self.wview(half + 4096, 8, 256)
            Wl = self.wview(half + 6144, 8, 16)
            Wg2 = self.wbuf[0:16, half + 6272:half + 6400]
            Wo = self.wview(half + 6400, 2, 1024)
            self.wload(Wq, wsrc(h * 128, 128))
            self.wload(Wk, wsrc(512 + h * 128, 128))
            self.wload(Wl, wsrc(3072, 16))
            self.wload(Wg2, I["wg2"][:, h * 128:(h + 1) * 128])
            self.wload(Wv, wsrc(1024 + h * 256, 256))
            self.wload(Wg, wsrc(2048 + h * 256, 256))
            self.wload(Wo, I["w_out_o"][h * 256:(h + 1) * 256, :].rearrange("(k p) j -> p k j", p=128))
            st_all_in = I["st_gla"][:, h].rearrange("s k v -> k s v")
            st_all_out = O["gla_s"][:, h].rearrange("s k v -> k s v")
            st_out_p = O["gla_p"][h]
            s1 = -1.0 / 16.0
        else:
            Wq = self.wview(half, 8, 128)
            Wk = self.wview(half + 1024, 8, 128)
            Wv = self.wview(half + 2048, 8, 128)
            Wg = self.wview(half + 3072, 8, 128)
            Wo = self.wview(half + 4096, 1, 1024)
            self.wload(Wq, wsrc(3088 + h * 128, 128))
            self.wload(Wk, wsrc(4112 + h * 128, 128))
            self.wload(Wv, wsrc(5136 + h * 128, 128))
            self.wload(Wg, wsrc(6160 + h * 128, 128))
            self.wload(Wo, I["w_out_o"][1024 + h * 128:1024 + (h + 1) * 128, :].rearrange("(k p) j -> p k j", p=128))
            st_all_in = I["st_hg"][:, h].rearrange("s k v -> k s v")
            st_all_out = O["hg_s"][:, h].rearrange("s k v -> k s v")
            st_out_p = O["hg_p"][h]
            s1 = 1.0
        FB = self.fa(7168)
        fbs = [[FB[:, (7 * i + k) * 512:(7 * i + k + 1) * 512] for k in range(7)] for i in range(2)]
        Sall = FB[:, 0:16 * dv].rearrange("p (s v) -> p s v", s=16)
        gsb = self.fa(1024)
        S = self.fa(256)
        S2 = self.fa(256)
        smalls = self.fa(32)
        brf, ebr, ebl, e1l = [smalls[:, 4 * i:4 * i + 4] for i in range(4)]
        lrb = self.ba(512)
        Sb = self.ba(256)
        Sb2 = self.ba(256)
        bsets = []
        for i in range(2):
            Bk = self.ba(4096)
            bsets.append(dict(qt=Bk[:, 0:512], ktb=Bk[:, 512:1024], ktok=Bk[:, 1024:1536], AT=Bk[:, 1536:2048],
                              vtok=Bk[:, 2048:3072], yT=Bk[:, 3072:4096], blk=Bk))
        km = bsets[1]["blk"][:, 0:2048].rearrange("p (s k) -> p s k", s=16)
        qtB, ktbB, ktokB, vtokB, ATB, yTB = (bsets[0][k] for k in ("qt", "ktb", "ktok", "vtok", "AT", "yT"))
        ones = cm[:, CM["ones"]:CM["ones"] + 128]
        self.memset("dve", S[:, 0:dv], 0.0)
        self.memset("dve", Sb[:, 0:dv], 0.0)
        self.memset("pool", bsets[0]["AT"], 0.0)
        self.memset("pool", bsets[1]["AT"], 0.0)
        ng_col = [self.pc("gla_ng", vc) if gla else self.pc("hg_ng", 0) for vc in range(nvc)]

        def tile_small(c0, n, kind):
            G = fbs[1][3:7]
            qf, kf, lf, bp = (G[0][:, 128 * i:128 * i + 128] for i in range(4))
            E1, E2, ktl, rstd = (G[1][:, 128 * i:128 * i + 128] for i in range(4))
            sqo = G[2][:, 0:256].rearrange("p (c t) -> p c t", c=2)
            gs = G[2][:, 256:512].rearrange("p (c t) -> p c t", c=2)
            ot = G[3][:, 0:256].rearrange("p (c t) -> p c t", c=2)
            BSm = bsets[1] if kind == "m" else bsets[0]
            qt, ktb, ktok, AT = BSm["qt"][:, 0:128], BSm["ktb"][:, 0:128], BSm["ktok"][:, 0:128], BSm["AT"][:, 0:128]
            vtok = BSm["vtok"][:, 0:256]
            yT = BSm["yT"][:, 0:256].rearrange("p (k t) -> p k t", k=2)
            pq = self.bank(2)
            self.inproj_fm(pq[:, 0:n], Wq, 8, slice(0, 128), c0, n)
            pk = self.bank(3)
            self.inproj_fm(pk[:, 0:n], Wk, 8, slice(0, 128), c0, n)
            if gla:
                self.act(qf[:, 0:n], pq[:, 0:n], AF.Copy, scale=128.0 ** -0.5)
                self.cp("act", kf[:, 0:n], pk[:, 0:n])
                pl = self.bank(5)
                self.inproj_fm(pl[0:16, 0:n], Wl, 8, slice(0, 16), c0, n)
                self.cp("act", lrb[0:16, 0:n], pl[0:16, 0:n])
                self.mm(pl[:, 128:128 + n], Wg2, lrb[0:16, 0:n])
                self.act(lf[:, 0:n], pl[:, 128:128 + n], AF.Exp, scale=-1.0, bias=self.dc("nbg2", h))
                self.act(lf[:, 0:n], lf[:, 0:n], AF.Ln, bias=1.0)
            else:
                self.act(qf[:, 0:n], pq[:, 0:n], AF.Silu)
                self.act(E1[:, 0:n], pk[:, 0:n], AF.Sigmoid)
                self.ts("dve", lf[:, 0:n], E1[:, 0:n], self.dc("oml", h), ALU.mult, self.dc("lb", h), ALU.add)
                self.act(lf[:, 0:n], lf[:, 0:n], AF.Ln)
                self.ts("dve", kf[:, 0:n], E1[:, 0:n], self.dc("noml", h), ALU.mult, self.dc("oml", h), ALU.add)
            if kind == "s":
                rst = cm[:, CM["rst8"]:CM["rst8"] + n]
                msk = cm[0:n, CM["mcumS"]:CM["mcumS"] + n]
                chunks = [(8 * s_, 8 * s_ + 8) for s_ in range(16)]
            else:
                rst = cm[:, CM["rst64"]:CM["rst64"] + n]
                msk = cm[0:n, CM["mcumP"]:CM["mcumP"] + n]
                chunks = [(0, n)]
            self.scan(bp[:, 0:n], rst, lf[:, 0:n], 0.0)
            self.act(E1[:, 0:n], bp[:, 0:n], AF.Exp, scale=s1)
            self.act(E2[:, 0:n], bp[:, 0:n], AF.Exp, scale=-s1)
            self.tt("dve", qt[:, 0:n], qf[:, 0:n], E1[:, 0:n], ALU.mult)
            self.tt("dve", ktl[:, 0:n], kf[:, 0:n], E2[:, 0:n], ALU.mult)
            self.cp("pool", ktb[:, 0:n], ktl[:, 0:n])
            ptk = self.bank(4)
            self.tr(ptk[0:n, 0:128], ktl[:, 0:n])
            self.cp("act", ktok[0:n, :], ptk[0:n, 0:128])
            pv = self.bank(6)
            self.inproj_tm(pv[0:n, 0:dv], Wv, slice(0, dv), c0, n)
            self.cp("act", vtok[0:n, 0:dv], pv[0:n, 0:dv])
            psc = self.bank(5)
            self.mm(psc[0:n, 256:256 + n], ktb[:, 0:n], qt[:, 0:n])
            self.tt("dve", AT[0:n, 0:n], psc[0:n, 256:256 + n], msk, ALU.mult)
            if kind == "s":
                self.tt("pool", km[:, :, :], ktok[:, :].unsqueeze(1).broadcast_to([128, 16, 128]),
                        cm[:, CM["seqm"]:CM["seqm"] + 16].unsqueeze(2).broadcast_to([128, 16, 128]), ALU.mult)
            pov = [self.bank(0), self.bank(1)]
            for vc in range(nvc):
                self.mm(pov[vc][:, 0:n], vtok[0:n, vc * 128:(vc + 1) * 128], AT[0:n, 0:n], start=True, stop=False)
            if kind == "s":
                self.dma("sp", Sall[:, :, :], st_all_in)
            for ci, (a0, a1) in enumerate(chunks):
                last = (ci == len(chunks) - 1)
                Sbr = Sb
                if kind == "s":
                    Sbr = (Sb, Sb2)[ci % 2]
                    self.cp("act", Sbr[:, 0:dv], Sall[:, ci, :])
                for vc in range(nvc):
                    self.mm(pov[vc][:, a0:a1], Sbr[:, vc * 128:(vc + 1) * 128], qt[:, a0:a1], start=False, stop=last)
                pu = self.bank(2 + ci % 2)
                if kind == "s":
                    self.mm(pu[:, 0:dv], km[:, ci, :], vtok[:, 0:dv])
                    self.tt("dve", Sall[:, ci, :], Sall[:, ci, :], pu[:, 0:dv], ALU.add)
                    self.ts("dve", Sall[:, ci, :], Sall[:, ci, :], E1[:, a1 - 1:a1], ALU.mult)
                else:
                    self.mm(pu[:, 0:dv], ktok[a0:a1, :], vtok[a0:a1, 0:dv])
                    self.tt("dve", S2[:, 0:dv], S[:, 0:dv], pu[:, 0:dv], ALU.add)
                    self.ts("dve", S[:, 0:dv], S2[:, 0:dv], E1[:, a1 - 1:a1], ALU.mult)
            if kind == "s":
                self.dma("act", st_all_out, Sall[:, :, :])
            for vc in range(nvc):
                self.act(sqo[:, vc, 0:n], pov[vc][:, 0:n], AF.Square)
            pss = self.bank(5)
            for vc in range(nvc):
                self.mm(pss[:, 384:384 + n], ones, sqo[:, vc, 0:n], start=(vc == 0), stop=(vc == nvc - 1))
            self.act(rstd[:, 0:n], pss[:, 384:384 + n], AF.Ln, scale=1.0 / dv, bias=RMS_EPS)
            self.act(rstd[:, 0:n], rstd[:, 0:n], AF.Exp, scale=-0.5)
            for vc in range(nvc):
                pg = self.bank(6 + vc)
                self.inproj_fm(pg[:, 0:n], Wg, 8, slice(vc * 128, (vc + 1) * 128), c0, n)
                self.act(gs[:, vc, 0:n], pg[:, 0:n], AF.Silu)
                self.tt("dve", ot[:, vc, 0:n], pov[vc][:, 0:n], rstd[:, 0:n], ALU.mult)
                self.stt(yT[:, vc, 0:n], ot[:, vc, 0:n], ng_col[vc], gs[:, vc, 0:n], ALU.mult, ALU.mult)
            self.outproj_acc(Wo, nvc, yT, c0, n, first)

        def tile_super(j):
            c0 = 16 + 512 * j
            n = 512
            F = fbs[j % 2]
            qf, kf, lf, bp, E1, E2, ktl = F
            sqo = [F[0], F[1]]
            rstd = F[2]
            gs = [gsb[:, 0:512], gsb[:, 512:1024]]
            ot = [F[5], F[6]]
            BS = bsets[j % 2]
            qt, ktb, ktok, AT = BS["qt"], BS["ktb"], BS["ktok"], BS["AT"]
            vtokB = BS["vtok"]
            vtok = vtokB[:, 0:4 * dv].rearrange("p (b v) -> p b v", b=4)
            yT = BS["yT"][:, :].rearrange("p (k t) -> p k t", k=2)
            pq = self.bank(0)
            self.inproj_fm(pq[:, 0:n], Wq, 8, slice(0, 128), c0, n)
            pk = self.bank(1)
            self.inproj_fm(pk[:, 0:n], Wk, 8, slice(0, 128), c0, n)
            if gla:
                pl = self.bank(2)
                self.inproj_fm(pl[0:16, 0:n], Wl, 8, slice(0, 16), c0, n)
            if gla:
                self.act(qf[:, :], pq[:, 0:n], AF.Copy, scale=128.0 ** -0.5)
                self.cp("act", kf[:, :], pk[:, 0:n])
                self.cp("act", lrb[0:16, 0:n], pl[0:16, 0:n])
                pg0 = self.bank(3)
                self.inproj_fm(pg0[:, 0:n], Wg, 8, slice(0, 128), c0, n)
                pl2 = self.bank(0)
                self.mm(pl2[:, 0:n], Wg2, lrb[0:16, 0:n])
                pg1 = self.bank(1)
                self.inproj_fm(pg1[:, 0:n], Wg, 8, slice(128, 256), c0, n)
                self.act(gs[0][:, :], pg0[:, 0:n], AF.Silu)
                self.act(gs[1][:, :], pg1[:, 0:n], AF.Silu)
                self.act(lf[:, :], pl2[:, 0:n], AF.Exp, scale=-1.0, bias=self.dc("nbg2", h))
                self.act(lf[:, :], lf[:, :], AF.Ln, bias=1.0)
            else:
                pg0 = self.bank(2)
                self.inproj_fm(pg0[:, 0:n], Wg, 8, slice(0, 128), c0, n)
                self.act(qf[:, :], pq[:, 0:n], AF.Silu)
                self.act(gs[0][:, :], pg0[:, 0:n], AF.Silu)
                self.act(E1[:, :], pk[:, 0:n], AF.Sigmoid)
                self.ts("dve", lf[:, :], E1[:, :], self.dc("oml", h), ALU.mult, self.dc("lb", h), ALU.add)
                self.act(lf[:, :], lf[:, :], AF.Ln)
                self.ts("dve", kf[:, :], E1[:, :], self.dc("noml", h), ALU.mult, self.dc("oml", h), ALU.add)
            for b in range(4):
                if gla:
                    pvb = self.bank(2 + b // 2)[:, (b % 2) * 256:(b % 2) * 256 + 256]
                else:
                    pvb = self.bank(3)[:, b * 128:(b + 1) * 128]
                self.inproj_tm(pvb, Wv, slice(0, dv), c0 + b * 128, 128)
            if gla:
                self.cp("act", vtokB[:, 0:512], self.bank(2)[:, 0:512])
                self.cp("act", vtokB[:, 512:1024], self.bank(3)[:, 0:512])
            else:
                self.cp("act", vtokB[:, 0:512], self.bank(3)[:, 0:512])
            self.scan(bp[:, :], cm[:, CM["rst128"]:CM["rst128"] + 512], lf[:, :], 0.0)
            bpv = bp[:, :].rearrange("p (b t) -> p b t", b=4)
            self.cp("dve", brf[:, :].unsqueeze(2), bpv[:, :, 64:65])
            self.act(ebr[:, :].unsqueeze(2), bpv[:, :, 64:65], AF.Exp, scale=s1)
            self.act(ebl[:, :].unsqueeze(2), bpv[:, :, 127:128], AF.Exp, scale=s1)
            self.tt("dve", bpv, bpv, brf[:, :].unsqueeze(2).broadcast_to([128, 4, 128]), ALU.subtract)
            self.act(E1[:, :], bp[:, :], AF.Exp, scale=s1)
            self.act(E2[:, :], bp[:, :], AF.Exp, scale=-s1)
            self.cp("dve", e1l[:, :].unsqueeze(2), E1[:, :].rearrange("p (b t) -> p b t", b=4)[:, :, 127:128])
            self.tt("dve", qt[:, :], qf[:, :], E1[:, :], ALU.mult)
            self.tt("dve", ktl[:, :], kf[:, :], E2[:, :], ALU.mult)
            self.cp("pool", ktb[:, :], ktl[:, :])
            ptk = self.bank(4)
            for b in range(4):
                self.tr(ptk[:, b * 128:(b + 1) * 128], ktl[:, b * 128:(b + 1) * 128])
            self.cp("act", ktok[:, :], ptk[:, 0:512])
            psc = self.bank(5)
            for b in range(4):
                o = b * 128
                self.mm(psc[:, o + 64:o + 128], ktb[:, o:o + 128], qt[:, o + 64:o + 128])
                self.mm(psc[0:64, o:o + 64], ktb[:, o:o + 64], qt[:, o:o + 64])
            pscv = psc[:, 0:512].rearrange("p (b t) -> p b t", b=4)
            ATv = AT[:, :].rearrange("p (b t) -> p b t", b=4)
            mk = cm[:, CM["mcumP"]:CM["mcumP"] + 128]
            self.tt("dve", ATv[0:64, :, :], pscv[0:64, :, :], mk[0:64, :].unsqueeze(1).broadcast_to([64, 4, 128]), ALU.mult)
            self.tt("dve", ATv[64:128, :, 64:128], pscv[64:128, :, 64:128],
                    mk[64:128, 64:128].unsqueeze(1).broadcast_to([64, 4, 64]), ALU.mult)
            pov = [self.bank(6), self.bank(7)]
            for b in range(4):
                o = b * 128
                self.act(Sb[:, 0:dv], S[:, 0:dv], AF.Identity, scale=ebr[:, b:b + 1])
                for vc in range(nvc):
                    self.mm(pov[vc][:, o:o + 128], vtok[:, b, vc * 128:(vc + 1) * 128], ATv[:, b, :], start=True, stop=False)
                    self.mm(pov[vc][:, o:o + 128], Sb[:, vc * 128:(vc + 1) * 128], qt[:, o:o + 128], start=False, stop=True)
                pu = self.bank(4 + b % 2)
                self.mm(pu[:, 0:dv], ktok[:, o:o + 128], vtok[:, b, 0:dv])
                self.ts("dve", S2[:, 0:dv], S[:, 0:dv], ebl[:, b:b + 1], ALU.mult)
                self.stt(S[:, 0:dv], pu[:, 0:dv], e1l[:, b:b + 1], S2[:, 0:dv], ALU.mult, ALU.add)
            for vc in range(nvc):
                self.act(sqo[vc][:, :], pov[vc][:, 0:n], AF.Square)
            pss = self.bank(4)
            for vc in range(nvc):
                self.mm(pss[:, 0:n], ones, sqo[vc][:, :], start=(vc == 0), stop=(vc == nvc - 1))
            self.act(rstd[:, :], pss[:, 0:n], AF.Ln, scale=1.0 / dv, bias=RMS_EPS)
            self.act(rstd[:, :], rstd[:, :], AF.Exp, scale=-0.5)
            for vc in range(nvc):
                self.tt("dve", ot[vc][:, :], pov[vc][:, 0:n], rstd[:, :], ALU.mult)
                self.stt(yT[:, vc, :], ot[vc][:, :], ng_col[vc], gs[vc][:, :], ALU.mult, ALU.mult)
            self.outproj_wide(Wo, nvc, yT, c0, n, first, banks=(4, 5))

        tile_small(0, 16, "m")
        for j in range(4):
            tile_super(j)
        self.dma("act", st_out_p, S[:, 0:dv])
        tile_small(2064, 128, "s")

    def store_outputs(self):
        O = self.O
        stg = [self.fa(1024), self.fa(1024), self.fa(1024)]
        for ti, (c0, n, kind) in enumerate(TILES):
            if kind == "m":
                continue
            s = stg[ti % 3]
            pp = self.ps[2 + ti % 2]
            for c in range(8):
                self.tr(pp[0:n, c * 128:(c + 1) * 128], self.h32[:, c, c0:c0 + n])
            self.cp("act", s[0:n, :], pp[0:n, :])
            if kind == "p":
                r0 = c0 - 16
                self.dma("sp", O["y_p"][r0:r0 + n, :], s[0:n, :])
            else:
                self.dma("sp", O["y_s"], s[0:n, :])


_CACHE = {}


def kernel(**inp):
    if "nc" not in _CACHE:
        b = Builder()
        _CACHE["nc"] = b.build()
    nc = _CACHE["nc"]
    f32 = lambda a: np.ascontiguousarray(np.asarray(a, dtype=np.float32))
    pcol = _pack_pcol(inp)
    prow = _pack_prow(inp)
    cmask = _const_masks()
    shared = {
        "meta": f32(inp["meta_tokens"]),
        "w_in_e": f32(inp["w_in_even"][0]), "w_out_e": f32(inp["w_out_even"][0]),
        "w_in_o": f32(inp["w_in_odd"][0]), "w_out_o": f32(inp["w_out_odd"][0]),
        "w1": f32(inp["mlp_w1"]), "w2": f32(inp["mlp_w2"]),
        "rg_wa": f32(inp["rg_wa"][0]), "rg_wx": f32(inp["rg_wx"][0]), "wg2": f32(inp["gla_wg2"][0]),
        "pcol": pcol, "prow": prow, "cmask": cmask,
    }
    in_maps = []
    for i in range(NCORES):
        sl = slice(16 * i, 16 * i + 16)
        m = dict(shared)
        m["xp"] = f32(inp["x_prompt"][i])
        m["xs"] = f32(np.asarray(inp["x_sample"])[sl].reshape(128, D))
        m["st_ssd"] = f32(np.asarray(inp["state_ssd"])[0, sl].reshape(16, 1024, 128))
        m["st_ssdc"] = f32(np.asarray(inp["state_ssd_conv"])[0, sl].reshape(48, 1536))
        m["st_rg"] = f32(np.asarray(inp["state_rglru"])[0, sl])
        m["st_rgc"] = f32(np.asarray(inp["state_rglru_conv"])[0, sl].reshape(48, 1024))
        m["st_gla"] = f32(np.asarray(inp["state_gla"])[0, sl])
        m["st_hg"] = f32(np.asarray(inp["state_hgrn"])[0, sl])
        in_maps.append(m)
    res = run_bass_kernel_spmd(nc, in_maps, core_ids=list(range(NCORES)))
    R = res.results
    y_p = np.stack([R[i]["y_p"] for i in range(NCORES)], 0)
    y_s = np.concatenate([R[i]["y_s"].reshape(16, 8, D) for i in range(NCORES)], 0)
    p_ssd = np.stack([R[i]["ssd_p"].reshape(16, 64, 128) for i in range(NCORES)], 0)[None]
    p_ssdc = np.stack([R[i]["ssdc"][48:51] for i in range(NCORES)], 0)[None]
    p_rg = np.stack([R[i]["rg"][16] for i in range(NCORES)], 0)[None]
    p_rgc = np.stack([R[i]["rgc"][48:51] for i in range(NCORES)], 0)[None]
    p_gla = np.stack([R[i]["gla_p"] for i in range(NCORES)], 0)[None]
    p_hg = np.stack([R[i]["hg_p"] for i in range(NCORES)], 0)[None]
    s_ssd = np.concatenate([R[i]["ssd_s"].reshape(16, 16, 64, 128) for i in range(NCORES)], 0)[None]
    s_ssdc = np.concatenate([R[i]["ssdc"][0:48].reshape(16, 3, 1536) for i in range(NCORES)], 0)[None]
    s_rg = np.concatenate([R[i]["rg"][0:16] for i in range(NCORES)], 0)[None]
    s_rgc = np.concatenate([R[i]["rgc"][0:48].reshape(16, 3, 1024) for i in range(NCORES)], 0)[None]
    s_gla = np.concatenate([R[i]["gla_s"] for i in range(NCORES)], 0)[None]
    s_hg = np.concatenate([R[i]["hg_s"] for i in range(NCORES)], 0)[None]
    outs = (y_p, y_s, p_ssd, p_ssdc, p_rg, p_rgc, p_gla, p_hg, s_ssd, s_ssdc, s_rg, s_rgc, s_gla, s_hg)
    return tuple(np.ascontiguousarray(o, dtype=np.float32) for o in outs)
```

```python
import contextlib
import numpy as np
import concourse.bass as bass
import concourse.mybir as mybir
from concourse.bass_utils import run_bass_kernel_spmd

F32 = mybir.dt.float32
BF16 = mybir.dt.bfloat16
AF = mybir.ActivationFunctionType
ALU = mybir.AluOpType

NCORES = 8
D = 1024
NTOK = 2192
DEPTH = 2
ALPHA = (2.0 * DEPTH) ** 0.25
LN_EPS = 1e-5
RMS_EPS = 1e-6
TILES = [(0, 16, "m")] + [(16 + 128 * j, 128, "p") for j in range(16)] + [(2064, 128, "s")]
CT512 = [(0, 512), (512, 512), (1024, 512), (1536, 512), (2048, 144)]

ENGS = ("pe", "act", "dve", "pool", "sp")
NDMA_SEMS = 32
NSW_SEMS = 8


class _Op:
    __slots__ = ("eng", "fn", "deps", "odeps", "dma", "idx", "signal", "semval", "dsem", "dprev", "cost", "lat", "tbl")

    def __init__(self, eng, fn, dma):
        self.eng = eng
        self.fn = fn
        self.dma = dma
        self.deps = set()
        self.odeps = set()
        self.signal = False
        self.semval = None
        self.dsem = None
        self.dprev = None
        self.cost = 0.3
        self.lat = 0.0
        self.tbl = None


def _rng(ap):
    t = ap.tensor
    if type(t).__name__.startswith("DRam"):
        return None
    row = 1
    for s in list(t.shape)[1:]:
        row *= int(s)
    lo = int(ap.offset) % row
    dims = sorted((abs(int(st)), int(cnt)) for (st, cnt) in list(ap.ap)[1:] if int(cnt) > 1 and int(st) != 0)
    ivs = [(lo, lo + 1)]
    for (st, cnt) in dims:
        ext = ivs[-1][1] - ivs[0][0]
        if st <= ext or len(ivs) * cnt > 32:
            ivs = [(ivs[0][0], ivs[-1][1] + (cnt - 1) * st)]
        else:
            ivs = [(a + k * st, b + k * st) for k in range(cnt) for (a, b) in ivs]
    if type(t).__name__.startswith("PSum"):
        ivs = sorted(set(((a // 512) * 512, ((b + 511) // 512) * 512) for (a, b) in ivs))
    return (t.name, ivs)


def _fsize(ap):
    n = 1
    for (st, cnt) in list(ap.ap)[1:]:
        n *= int(cnt)
    return n


_TBL = {AF.Exp: "le", AF.Ln: "le", AF.Sigmoid: "sg", AF.Silu: "si", AF.Gelu: "ge", AF.Sqrt: "sq", AF.Tanh: "th"}


class Prog:
    def __init__(self, nc):
        self.nc = nc
        self.ops = []
        self.acc = {}
        self.do_sched = True

    def _access(self, o, ap, is_write):
        r = _rng(ap)
        if r is None:
            return
        for (lo, hi) in r[1]:
            self._access1(o, ap, r[0], lo, hi, is_write)

    def _access1(self, o, ap, name, lo, hi, is_write):
        psum = type(ap.tensor).__name__.startswith("PSum")
        lst = self.acc.setdefault(name, [])
        keep = []
        for rec in lst:
            rlo, rhi, oi, w, eng, dma = rec
            overlap = (rlo < hi) and (lo < rhi)
            if overlap and (w or is_write or (psum and eng != o.eng)) and oi != o.idx:
                if o.eng == "pe" and eng == "pe":
                    o.odeps.add(oi)
                else:
                    o.deps.add(oi)
            if is_write and lo <= rlo and rhi <= hi:
                continue
            if (not is_write) and (not w) and rlo == lo and rhi == hi and eng == o.eng and not dma and not o.dma:
                if oi != o.idx:
                    o.odeps.add(oi)
                continue
            keep.append(rec)
        keep.append((lo, hi, o.idx, is_write, o.eng, o.dma))
        self.acc[name] = keep

    def op(self, eng, fn, ins=(), outs=(), dma=False, cost=None, lat=0.0, tbl=None):
        o = _Op(eng, fn, dma)
        o.tbl = tbl
        o.idx = len(self.ops)
        if cost is not None:
            o.cost = cost
        o.lat = lat
        self.ops.append(o)
        for a in ins:
            if a is not None and not isinstance(a, (int, float)):
                self._access(o, a, False)
        for a in outs:
            self._access(o, a, True)
        return o

    def schedule(self):
        import heapq
        ops = self.ops
        n = len(ops)
        succ = [[] for _ in range(n)]
        npred = [0] * n
        for o in ops:
            ps = set(o.deps) | set(o.odeps)
            npred[o.idx] = len(ps)
            for d in ps:
                succ[d].append(o.idx)
        est = [0.0] * n
        rank = [0.0] * n
        import os
        use_cp = os.environ.get("MK_CP", "1") == "1"
        for i_ in range(n - 1, -1, -1):
            o_ = ops[i_]
            m_ = 0.0
            for s_ in succ[i_]:
                if rank[s_] > m_:
                    m_ = rank[s_]
            rank[i_] = o_.cost + o_.lat + m_
        def key(i_):
            return (-rank[i_], i_) if use_cp else (i_, i_)
        fut = {e: [] for e in ENGS}
        avail = {e: [] for e in ENGS}
        free = {e: 0.0 for e in ENGS}
        for o in ops:
            if npred[o.idx] == 0:
                heapq.heappush(fut[o.eng], (0.0, o.idx))
        order = {e: [] for e in ENGS}
        self.tstart = [0.0] * n
        placed = 0
        SEMLAT = float(os.environ.get("MK_SEMLAT", "0.05"))
        TBL_WINDOW = int(os.environ.get("MK_TBLW", "2500"))
        TBL_COST = 1.3
        act_av = {}
        cur_tbl = [None]

        def act_pick():
            heads = [(h[0], k) for k, h in act_av.items() if h]
            if not heads:
                return None
            any_best = min(heads, key=lambda x: x[0])
            same = [x for x in heads if x[1] is None or x[1] == cur_tbl[0]]
            if same:
                sb = min(same, key=lambda x: x[0])
                if sb[0][1] <= any_best[0][1] + TBL_WINDOW:
                    return (sb[0][1], sb[1])
            return (any_best[0][1], any_best[1])

        while placed < n:
            best = None
            for e in ENGS:
                f, a = fut[e], avail[e]
                if e == "act":
                    while f and f[0][0] <= free[e]:
                        ii = heapq.heappop(f)[1]
                        heapq.heappush(act_av.setdefault(ops[ii].tbl, []), (key(ii), ii))
                    pk = act_pick()
                    if pk is not None:
                        cand = (free[e], pk[0], e, True)
                    elif f:
                        cand = (f[0][0], f[0][1], e, False)
                    else:
                        continue
                else:
                    while f and f[0][0] <= free[e]:
                        i2 = heapq.heappop(f)[1]
                        heapq.heappush(a, (key(i2), i2))
                    if a:
                        cand = (free[e], a[0][1], e, True)
                    elif f:
                        cand = (f[0][0], f[0][1], e, False)
                    else:
                        continue
                if best is None or cand[:2] < best[:2]:
                    best = cand
            start, idx, e, from_avail = best
            if e == "act":
                if from_avail:
                    heapq.heappop(act_av[ops[idx].tbl])
                else:
                    heapq.heappop(fut[e])
                t_ = ops[idx].tbl
                if t_ is not None and t_ != cur_tbl[0]:
                    start += TBL_COST
                    cur_tbl[0] = t_
            elif from_avail:
                heapq.heappop(avail[e])
            else:
                heapq.heappop(fut[e])
            o = ops[idx]
            self.tstart[idx] = start
            fin_eng = start + o.cost
            free[e] = fin_eng
            fin = fin_eng + o.lat
            order[e].append(o)
            placed += 1
            for sidx in succ[idx]:
                so = ops[sidx]
                t = fin + (SEMLAT if idx in so.deps else 0.0)
                if idx in so.odeps and idx not in so.deps:
                    t = start
                if t > est[sidx]:
                    est[sidx] = t
                npred[sidx] -= 1
                if npred[sidx] == 0:
                    heapq.heappush(fut[so.eng], (est[sidx], sidx))
        self.est_total = max(free.values())
        return order

    def emit(self):
        nc = self.nc
        ops = self.ops
        if self.do_sched:
            per_eng = self.schedule()
        else:
            per_eng = {e: [o for o in ops if o.eng == e] for e in ENGS}
        for o in ops:
            for d in o.deps:
                ops[d].signal = True
        for e in ENGS:
            last = [o for o in per_eng[e] if not o.dma]
            if last:
                last[-1].signal = True
        cnt = {e: 0 for e in ENGS}
        dcnt = [0] * NDMA_SEMS
        dlast = [None] * NDMA_SEMS
        rr = {"sw": 0, "hw": 0}
        for e in ENGS:
            for o in per_eng[e]:
                if o.dma:
                    if e == "pool":
                        sidx = rr["sw"] % NSW_SEMS
                        rr["sw"] += 1
                    else:
                        sidx = NSW_SEMS + rr["hw"] % (NDMA_SEMS - NSW_SEMS)
                        rr["hw"] += 1
                    o.dsem = sidx
                    o.dprev = dlast[sidx]
                    dlast[sidx] = o.idx
                    dcnt[sidx] += 16
                    o.semval = dcnt[sidx]
                elif o.signal:
                    cnt[e] += 1
                    o.semval = cnt[e]
        hw_queues = [e for e in ENGS if e != "pool" and any(o.dma for o in per_eng[e])]
        assert len(hw_queues) <= 1, hw_queues
        with contextlib.ExitStack() as st:
            esem = {e: st.enter_context(nc.semaphore("s_" + e)) for e in ENGS}
            dsem = [st.enter_context(nc.semaphore("d_%d" % i)) for i in range(NDMA_SEMS)]
            block = st.enter_context(nc.Block())

            def semof(o):
                if o.dma:
                    return ("d", o.dsem), dsem[o.dsem]
                return ("e", o.eng), esem[o.eng]

            def run(eng_name, eng):
                waited = {}
                for o in per_eng[eng_name]:
                    need = {}
                    deps = set(o.deps)
                    if o.dma and o.dprev is not None:
                        deps.add(o.dprev)
                    for d in deps:
                        dop = ops[d]
                        k, s = semof(dop)
                        if need.get(k, (None, 0))[1] < dop.semval:
                            need[k] = (s, dop.semval)
                    for k, (s, v) in need.items():
                        if waited.get(k, 0) >= v:
                            continue
                        eng.wait_ge(s, v)
                        waited[k] = v
                    ins = o.fn(eng)
                    if o.dma:
                        ins.then_inc(dsem[o.dsem], 16)
                    elif o.signal:
                        ins.then_inc(esem[eng_name], 1)
                if eng_name == "sp":
                    for i in range(NDMA_SEMS):
                        if dcnt[i] > 0 and waited.get(("d", i), 0) < dcnt[i]:
                            eng.wait_ge(dsem[i], dcnt[i])
                    for e in ENGS:
                        if e != eng_name and cnt[e] > 0:
                            eng.wait_ge(esem[e], cnt[e])

            @block.sync
            def _(e):
                run("sp", e)

            @block.tensor
            def _(e):
                run("pe", e)

            @block.scalar
            def _(e):
                run("act", e)

            @block.vector
            def _(e):
                run("dve", e)

            @block.gpsimd
            def _(e):
                run("pool", e)


def _cols(v):
    v = np.asarray(v, np.float32).reshape(-1, 128)
    return np.ascontiguousarray(v.T)


PC = {}
_pc_n = 0
for _nm, _n in [("ssd_cw", 48), ("ssd_cb", 12), ("rg_cw", 32), ("rg_cb", 8), ("rg_ba", 8), ("rg_bx", 8),
                ("rg_lam", 8), ("gla_bg2", 4), ("gla_ng", 2), ("lb0", 8), ("lb1", 8), ("hg_ng", 1),
                ("ln1g", 16), ("ln1b", 16), ("ln2g", 16), ("ln2b", 16)]:
    PC[_nm] = _pc_n
    _pc_n += _n
NPC = _pc_n
DC = {"rg_sc": 0, "lb": 8, "oml": 16, "noml": 24, "nbg2": 32}
NDC = 36
PR = {"dtb": 0, "alog": 16, "dd": 32, "ng": 48}
NPR = 48 + 1024
CM = {"ident": 0, "mcumP": 128, "mcumS": 256, "mlastS": 384, "mc64": 512, "rst64": 640, "rst8": 768,
      "ones": 896, "seqm": 1024, "rst128": 1040}
NCM = 1040 + 512


def _pack_pcol(inp):
    parts = []
    cw = np.asarray(inp["ssd_conv_w"][0], np.float32)
    parts.append(np.ascontiguousarray(cw.reshape(4, 12, 128).transpose(2, 1, 0)).reshape(128, 48))
    parts.append(_cols(inp["ssd_conv_b"][0]))
    rw = np.asarray(inp["rg_conv_w"][0], np.float32)
    parts.append(np.ascontiguousarray(rw.reshape(4, 8, 128).transpose(2, 1, 0)).reshape(128, 32))
    parts.append(_cols(inp["rg_conv_b"][0]))
    parts.append(_cols(inp["rg_ba"][0]))
    parts.append(_cols(inp["rg_bx"][0]))
    parts.append(_cols(inp["rg_lambda"][0]))
    parts.append(_cols(inp["gla_bg2"][0]))
    parts.append(_cols(inp["gla_norm_g"][0]))
    parts.append(_cols(inp["hgrn_lb_logits"][0]))
    parts.append(_cols(inp["hgrn_lb_logits"][1]))
    parts.append(_cols(inp["hgrn_norm_g"][0]))
    for nm in ("ln1_g", "ln1_b", "ln2_g", "ln2_b"):
        parts.append(np.concatenate([_cols(inp[nm][0]), _cols(inp[nm][1])], axis=1))
    out = np.ascontiguousarray(np.concatenate(parts, axis=1), dtype=np.float32)
    assert out.shape == (128, NPC), out.shape
    return out


def _pack_prow(inp):
    row = np.concatenate([np.asarray(inp["ssd_dt_bias"][0], np.float32), np.asarray(inp["ssd_a_log"][0], np.float32),
                          np.asarray(inp["ssd_d"][0], np.float32), np.asarray(inp["ssd_norm_g"][0], np.float32)])
    return np.ascontiguousarray(np.broadcast_to(row[None, :], (128, NPR)), dtype=np.float32)


def _const_masks():
    r = np.arange(128)[:, None]
    t = np.arange(128)[None, :]
    m = np.zeros((128, NCM), np.float32)
    m[:, CM["ident"]:CM["ident"] + 128] = (r == t)
    m[:, CM["mcumP"]:CM["mcumP"] + 128] = (r <= t)
    m[:, CM["mcumS"]:CM["mcumS"] + 128] = (r <= t) & (r // 8 == t // 8)
    m[:, CM["mlastS"]:CM["mlastS"] + 128] = (r // 8 == t // 8)
    m[:, CM["mc64"]:CM["mc64"] + 128] = (r <= t) & (r // 64 == t // 64)
    m[:, CM["rst64"]:CM["rst64"] + 128] = (t % 64 != 0)
    m[:, CM["rst8"]:CM["rst8"] + 128] = (t % 8 != 0)
    m[:, CM["ones"]:CM["ones"] + 128] = 1.0
    m[:, CM["seqm"]:CM["seqm"] + 16] = (r // 8 == np.arange(16)[None, :])
    m[:, CM["rst128"]:CM["rst128"] + 512] = (np.arange(512)[None, :] % 128 != 0)
    return m


class Builder:
    def __init__(self):
        self.nc = bass.Bass("TRN2", target_bir_lowering=False)
        self.P = Prog(self.nc)
        self.st = contextlib.ExitStack()

    def din(self, name, shape):
        return self.nc.dram_tensor(name, list(shape), F32, kind="ExternalInput").ap()

    def dout(self, name, shape):
        return self.nc.dram_tensor(name, list(shape), F32, kind="ExternalOutput").ap()

    def sb(self, name, shape, dt):
        return self.st.enter_context(self.nc.sbuf_tensor(name, list(shape), dt))

    def psum(self, name, shape, dt=F32):
        return self.st.enter_context(self.nc.psum_tensor(name, list(shape), dt))

    def mm(self, out, lhsT, rhs, start=True, stop=True):
        c = 0.03 + _fsize(rhs) / 2400.0
        if rhs.dtype == F32:
            c *= 4.0
        self.P.op("pe", lambda e: e.matmul(out, lhsT=lhsT, rhs=rhs, start=start, stop=stop),
                  ins=[lhsT, rhs], outs=[out], cost=max(c, 0.064))

    def tr(self, out, in_):
        k = in_.shape[0]
        ident = self.cm[0:k, CM["ident"]:CM["ident"] + k]
        self.P.op("pe", lambda e: e.transpose(out=out, in_=in_, identity=ident), ins=[in_, ident], outs=[out], cost=0.12)

    def act(self, out, in_, func, bias=None, scale=None, accum=None):
        kw = {}
        if bias is not None:
            kw["bias"] = bias
        if scale is not None:
            kw["scale"] = scale
        if accum is not None:
            kw["accum_out"] = accum
        outs = [out] + ([accum] if accum is not None else [])
        self.P.op("act", lambda e: e.activation(out=out, in_=in_, func=func, **kw), ins=[in_, bias, scale], outs=outs,
                  cost=0.25 + _fsize(out) * 0.00075, tbl=_TBL.get(func))

    def tt(self, eng, out, a, b, op):
        self.P.op(eng, lambda e: e.tensor_tensor(out=out, in0=a, in1=b, op=op), ins=[a, b], outs=[out],
                  cost=self._ecost(eng, out, 1.0))

    def ts(self, eng, out, a, s1, op0, s2=None, op1=None):
        if op1 is None:
            self.P.op(eng, lambda e: e.tensor_scalar(out=out, in0=a, scalar1=s1, scalar2=None, op0=op0),
                      ins=[a, s1], outs=[out], cost=self._ecost(eng, out, 0.6))
        else:
            self.P.op(eng, lambda e: e.tensor_scalar(out=out, in0=a, scalar1=s1, scalar2=s2, op0=op0, op1=op1),
                      ins=[a, s1, s2], outs=[out], cost=self._ecost(eng, out, 0.6))

    def stt(self, out, a, s, b, op0, op1):
        self.P.op("dve", lambda e: e.scalar_tensor_tensor(out=out, in0=a, scalar=s, in1=b, op0=op0, op1=op1),
                  ins=[a, s, b], outs=[out], cost=self._ecost("dve", out, 1.2))

    def cp(self, eng, out, in_):
        if eng == "act":
            self.P.op("act", lambda e: e.copy(out=out, in_=in_), ins=[in_], outs=[out], cost=0.25 + _fsize(out) * 0.00075)
        else:
            self.P.op(eng, lambda e: e.tensor_copy(out=out, in_=in_), ins=[in_], outs=[out],
                      cost=self._ecost(eng, out, 1.5 if eng == "pool" else 0.7))

    def scan(self, out, d0, d1, init):
        self.P.op("dve", lambda e: e.tensor_tensor_scan(out=out, data0=d0, data1=d1, initial=init,
                                                        op0=ALU.mult, op1=ALU.add), ins=[d0, d1, init], outs=[out],
                  cost=0.1 + _fsize(out) * 0.0021)

    def recip(self, out, in_):
        self.P.op("dve", lambda e: e.reciprocal(out=out, in_=in_), ins=[in_], outs=[out], cost=0.1 + _fsize(out) * 0.0065)

    def memset(self, eng, out, val):
        self.P.op(eng, lambda e: e.memset(out, val), outs=[out], cost=0.1 + _fsize(out) * 0.0006)

    def dma(self, q, out, in_):
        if q != "pool":
            q = "sp"
        nbytes = _fsize(out) * 4 * int(out.shape[0])
        self.P.op(q, lambda e: e.dma_start(out=out, in_=in_), ins=[in_], outs=[out], dma=True,
                  cost=(1.0 if q == "pool" else 0.08), lat=2.0 + nbytes / 150000.0)

    def _ecost(self, eng, out, f):
        n = _fsize(out)
        if eng == "pool":
            return 0.12 + n * 0.0023 * max(f, 0.6)
        return 0.08 + n * 0.00105 * max(f, 0.55) / 0.55 * 0.55 if f <= 0.7 else 0.08 + n * 0.00105 * f

    def build(self):
        nc = self.nc
        I = {}
        I["xp"] = self.din("xp", [2048, D])
        I["xs"] = self.din("xs", [128, D])
        I["meta"] = self.din("meta", [16, D])
        I["st_ssd"] = self.din("st_ssd", [16, 1024, 128])
        I["st_ssdc"] = self.din("st_ssdc", [48, 1536])
        I["st_rg"] = self.din("st_rg", [16, 1024])
        I["st_rgc"] = self.din("st_rgc", [48, 1024])
        I["st_gla"] = self.din("st_gla", [16, 4, 128, 256])
        I["st_hg"] = self.din("st_hg", [16, 8, 128, 128])
        I["w_in_e"] = self.din("w_in_e", [D, 4624])
        I["w_out_e"] = self.din("w_out_e", [2048, D])
        I["w_in_o"] = self.din("w_in_o", [D, 7184])
        I["w_out_o"] = self.din("w_out_o", [2048, D])
        I["w1"] = self.din("w1", [2, D, 4096])
        I["w2"] = self.din("w2", [2, 4096, D])
        I["rg_wa"] = self.din("rg_wa", [8, 128, 128])
        I["rg_wx"] = self.din("rg_wx", [8, 128, 128])
        I["wg2"] = self.din("wg2", [16, 512])
        I["pcol"] = self.din("pcol", [128, NPC])
        I["prow"] = self.din("prow", [128, NPR])
        I["cmask"] = self.din("cmask", [128, NCM])
        O = {}
        O["y_p"] = self.dout("y_p", [2048, D])
        O["y_s"] = self.dout("y_s", [128, D])
        O["ssd_p"] = self.dout("ssd_p", [1024, 128])
        O["ssd_s"] = self.dout("ssd_s", [16, 1024, 128])
        O["ssdc"] = self.dout("ssdc", [51, 1536])
        O["rg"] = self.dout("rg", [17, 1024])
        O["rgc"] = self.dout("rgc", [51, 1024])
        O["gla_p"] = self.dout("gla_p", [4, 128, 256])
        O["gla_s"] = self.dout("gla_s", [16, 4, 128, 256])
        O["hg_p"] = self.dout("hg_p", [8, 128, 128])
        O["hg_s"] = self.dout("hg_s", [16, 8, 128, 128])
        self.I, self.O = I, O

        self.h32 = self.sb("h32", [128, 8, NTOK], F32)
        self.hb = self.sb("hb", [128, 8, NTOK], BF16)
        self.wbuf = self.sb("wbuf", [128, 16896], BF16)
        self.pcol = self.sb("pcol_s", [128, NPC], F32)
        self.dcol = self.sb("dcol_s", [128, NDC], F32)
        self.prow = self.sb("prow_s", [128, NPR], F32)
        self.arow = self.sb("arow_s", [128, 16], F32)
        self.cm = self.sb("cm_s", [128, NCM], F32)
        self.AF = self.sb("arenaF", [128, 10240], F32)
        self.AB = self.sb("arenaB", [128, 9216], BF16)
        self.onesb = self.sb("onesb", [128, 128], BF16)
        self.ps = [self.psum("ps%d" % i, [128, 1024]) for i in range(4)]
        self.af_off = 0
        self.ab_off = 0

        import os
        stop = int(os.environ.get("MK_STOP", "99"))
        phases = [self.setup, self.load_inputs,
                  lambda: self.ssd_unit(0, True), lambda: self.ssd_unit(1, False),
                  lambda: [self.rg_unit(b) for b in range(8)],
                  lambda: self.ln(self.PCcol("ln1g", 0), self.PCcol("ln1b", 0)),
                  lambda: self.mlp(0),
                  lambda: self.ln(self.PCcol("ln2g", 0), self.PCcol("ln2b", 0)),
                  self.layer1,
                  lambda: self.ln(self.PCcol("ln1g", 8), self.PCcol("ln1b", 8)),
                  lambda: self.mlp(1),
                  lambda: self.ln(self.PCcol("ln2g", 8), self.PCcol("ln2b", 8), final=True)]
        for pi, ph in enumerate(phases):
            if pi > stop:
                break
            ph()
        self.store_outputs()
        self.P.emit()
        self.st.close()
        return nc

    def reset_arena(self):
        self.af_off = 0
        self.ab_off = 0

    def fa(self, n):
        o = self.af_off
        self.af_off += n
        assert self.af_off <= 10240, self.af_off
        return self.AF[:, o:o + n]

    def ba(self, n):
        o = self.ab_off
        self.ab_off += n + (n % 2)
        assert self.ab_off <= 9216, self.ab_off
        return self.AB[:, o:o + n]

    def bank(self, i):
        return self.ps[i // 2][:, (i % 2) * 512:(i % 2) * 512 + 512]

    def PCcol(self, name, j=0):
        o = PC[name] + j
        return o

    def pc(self, name, j=0, p=128):
        o = PC[name] + j
        return self.pcol[0:p, o:o + 1]

    def dc(self, name, j=0):
        o = DC[name] + j
        return self.dcol[:, o:o + 1]

    def setup(self):
        I = self.I
        self.dma("sp", self.pcol[:], I["pcol"])
        self.dma("sp", self.prow[:], I["prow"])
        self.dma("sp", self.cm[:], I["cmask"])
        self.memset("pool", self.onesb[:, :], 1.0)
        self.act(self.arow[:], self.prow[:, PR["alog"]:PR["alog"] + 16], AF.Exp)
        self.ts("dve", self.arow[:], self.arow[:], -1.0, ALU.mult)
        sc = self.dcol[:, DC["rg_sc"]:DC["rg_sc"] + 8]
        self.act(sc, self.pcol[:, PC["rg_lam"]:PC["rg_lam"] + 8], AF.Exp, scale=-1.0)
        self.act(sc, sc, AF.Ln, bias=1.0)
        self.ts("dve", sc, sc, -8.0, ALU.mult)
        lb = self.dcol[:, DC["lb"]:DC["lb"] + 8]
        oml = self.dcol[:, DC["oml"]:DC["oml"] + 8]
        noml = self.dcol[:, DC["noml"]:DC["noml"] + 8]
        self.tt("dve", lb, self.pcol[:, PC["lb1"]:PC["lb1"] + 8], self.pcol[:, PC["lb0"]:PC["lb0"] + 8], ALU.subtract)
        self.act(lb, lb, AF.Sigmoid)
        self.ts("dve", oml, lb, -1.0, ALU.mult, 1.0, ALU.add)
        self.ts("dve", noml, lb, -1.0, ALU.add)
        self.ts("dve", self.dcol[:, DC["nbg2"]:DC["nbg2"] + 4], self.pcol[:, PC["gla_bg2"]:PC["gla_bg2"] + 4], -1.0, ALU.mult)

    def load_inputs(self):
        I = self.I
        self.reset_arena()
        stg = [self.fa(1024), self.fa(1024)]
        import os
        tsel = os.environ.get("MK_TSEL", "")
        for ti, (c0, n, kind) in enumerate(TILES):
            if tsel and str(ti) not in tsel.split(","):
                continue
            s = stg[ti % 2]
            if kind == "m":
                src = I["meta"]
            elif kind == "p":
                r0 = c0 - 16
                src = I["xp"][r0:r0 + n, :]
            else:
                src = I["xs"]
            import os
            li = int(os.environ.get("MK_LI", "9"))
            self.dma("sp", s[0:n, :], src)
            pp = self.ps[ti % 2]
            if li >= 1:
                for c in range(8):
                    self.tr(pp[:, c * 128:c * 128 + n], s[0:n, c * 128:(c + 1) * 128])
            pv = pp[:, :].rearrange("p (c t) -> p c t", c=8)[:, :, 0:n]
            if li >= 2 and li != 6:
                self.cp("act", self.h32[:, :, c0:c0 + n], pv)
            if li == 6:
                self.cp("dve", self.h32[:, :, c0:c0 + n], pv)
            if li == 3:
                self.cp("dve", self.hb[:, :, c0:c0 + n], self.h32[:, :, c0:c0 + n])
            if li == 4:
                self.cp("pool", self.hb[:, :, c0:c0 + n], self.h32[:, :, c0:c0 + n])
            if li == 5:
                for c in range(8):
                    self.cp("dve", self.hb[:, c, c0:c0 + n], pp[:, c * 128:c * 128 + n])
            if li >= 9:
                self.cp("dve", self.hb[:, :, c0:c0 + n], pv)

    def wload(self, dst, src):
        self.dma("pool", dst, src)

    def wview(self, off, kc, ncol):
        return self.wbuf[:, off:off + kc * ncol].rearrange("p (k j) -> p k j", k=kc)

    def inproj_fm(self, out_ps, W, kcs, colsl, c0, n):
        for k in range(kcs):
            self.mm(out_ps, W[:, k, colsl], self.hb[:, k, c0:c0 + n], start=(k == 0), stop=(k == kcs - 1))

    def inproj_tm(self, out_ps, W, colsl, c0, n):
        for k in range(8):
            self.mm(out_ps, self.hb[:, k, c0:c0 + n], W[:, k, colsl], start=(k == 0), stop=(k == 7))

    def outproj_acc(self, Wo, kcs, yT, c0, n, first):
        po = self.ps[0]
        for oc in range(8):
            for k in range(kcs):
                self.mm(po[:, oc * 128:oc * 128 + n], Wo[:, k, oc * 128:(oc + 1) * 128], yT[:, k, 0:n],
                        start=(k == 0), stop=(k == kcs - 1))
        pv = po[:, :].rearrange("p (c t) -> p c t", c=8)[:, :, 0:n]
        hv = self.h32[:, :, c0:c0 + n]
        if first:
            self.stt(hv, hv, ALPHA, pv, ALU.mult, ALU.add)
        else:
            self.tt("dve", hv, hv, pv, ALU.add)

    def outproj_wide(self, Wo, kcs, yT, c0, n, first, banks=(0, 1, 6, 7)):
        for oc in range(8):
            po = self.bank(banks[oc % len(banks)])
            for k in range(kcs):
                self.mm(po[:, 0:n], Wo[:, k, oc * 128:(oc + 1) * 128], yT[:, k, 0:n], start=(k == 0), stop=(k == kcs - 1))
            hv = self.h32[:, oc, c0:c0 + n]
            if first:
                self.stt(hv, hv, ALPHA, po[:, 0:n], ALU.mult, ALU.add)
            else:
                self.tt("dve", hv, hv, po[:, 0:n], ALU.add)

    def layer0(self):
        for g in range(2):
            self.ssd_unit(g, first=(g == 0))
        for blk in range(8):
            self.rg_unit(blk)

    def ssd_unit(self, g, first):
        I, O = self.I, self.O
        self.reset_arena()
        cm = self.cm
        Wz = self.wview(0, 8, 512)
        Wx = self.wview(4096, 8, 512)
        WB = self.wview(8192, 8, 128)
        WC = self.wview(9216, 8, 128)
        Wdt = self.wview(10240, 8, 8)
        Wo = self.wview(10304, 4, 1024)
        wie = I["w_in_e"]

        def wsrc(c0, ncol):
            return wie[:, c0:c0 + ncol].rearrange("(k p) j -> p k j", p=128)
        self.wload(Wx, wsrc(1024 + g * 512, 512))
        self.wload(WB, wsrc(2048 + g * 128, 128))
        self.wload(WC, wsrc(2304 + g * 128, 128))
        self.wload(Wz, wsrc(g * 512, 512))
        self.wload(Wdt, wsrc(2560 + g * 8, 8))
        self.wload(Wo, I["w_out_e"][g * 512:(g + 1) * 512, :].rearrange("(k p) j -> p k j", p=128))
        cch = [g * 4 + i for i in range(4)] + [8 + g, 10 + g]
        ub = [self.fa(1056), self.fa(1056)]
        xc = self.fa(6 * 128).rearrange("p (c t) -> p c t", c=6)
        xtok = self.fa(512)
        zs = self.fa(512)
        cbm = self.fa(128)
        segc = self.fa(512)
        dec = self.fa(512)
        yy = self.fa(512)
        t1 = self.fa(512)
        S = self.fa(512)
        stg = self.fa(512)
        sm = self.fa(128)
        ebl = self.fa(8)
        tail = self.fa(6 * 51).rearrange("p (c t) -> p c t", c=6)
        cst = self.fa(1536)
        ss = self.fa(2)
        ebl_all = self.fa(128)
        BT = self.ba(128)
        CT = self.ba(128)
        Btok = self.ba(128)
        xd = self.ba(512)
        xdte = self.ba(512)
        MT = self.ba(1024)
        yT = self.ba(512).rearrange("p (k t) -> p k t", k=4)
        Sb = self.ba(512)
        Sb2 = self.ba(512)
        cpf = self.ba(2176)
        Cpad = cpf[:, 0:2048].rearrange("p (s t) -> p s t", s=16)
        Cdiag = cpf[:, 0:2176].rearrange("p (a b) -> p a b", b=136)[:, :, 0:8]
        Bm = self.ba(2048).rearrange("p (s t) -> p s t", s=16)
        dt_, la, nla, bb, blb, eb, te, dtte, nbb = [sm[:, 8 * i:8 * i + 8] for i in range(9)]

        self.memset("dve", S, 0.0)
        self.memset("dve", Sb, 0.0)
        self.memset("dve", ub[1][:, 0:1056], 0.0)
        self.dma("sp", cst[0:48, :], I["st_ssdc"])

        prev_n = None
        for ti, (c0, n, kind) in enumerate(TILES):
            u = ub[ti % 2]
            up = ub[(ti + 1) % 2]
            if kind == "s":
                uv = u[:, 0:1056].rearrange("p (c s t) -> p c s t", c=6, s=16)
                pt = self.bank(4)
                for i, ch in enumerate(cch):
                    self.tr(pt[:, i * 48:(i + 1) * 48], cst[0:48, ch * 128:(ch + 1) * 128])
                self.cp("act", uv[:, :, :, 0:3],
                        pt[:, 0:288].rearrange("p (c s t) -> p c s t", c=6, s=16))
            else:
                uv = u[:, 0:6 * 131].rearrange("p (c t) -> p c t", c=6)
                if prev_n is not None:
                    upv = up[:, 0:6 * 131].rearrange("p (c t) -> p c t", c=6)
                    self.cp("pool", uv[:, :, 0:3], upv[:, :, prev_n:prev_n + 3])
                else:
                    self.memset("pool", uv[:, :, 0:3], 0.0)
            for i in range(6):
                pb = self.bank(2 + (i % 2))
                if i < 4:
                    self.inproj_fm(pb[:, 0:n], Wx, 8, slice(i * 128, (i + 1) * 128), c0, n)
                elif i == 4:
                    self.inproj_fm(pb[:, 0:n], WB, 8, slice(0, 128), c0, n)
                else:
                    self.inproj_fm(pb[:, 0:n], WC, 8, slice(0, 128), c0, n)
                if kind == "s":
                    self.cp("act", uv[:, i, :, 3:11], pb[:, 0:128].rearrange("p (s t) -> p s t", s=16))
                else:
                    self.cp("act", uv[:, i, 3:3 + n], pb[:, 0:n])
            for i, ch in enumerate(cch):
                w = [self.pcol[:, PC["ssd_cw"] + ch * 4 + j:PC["ssd_cw"] + ch * 4 + j + 1] for j in range(4)]
                bcol = self.pcol[:, PC["ssd_cb"] + ch:PC["ssd_cb"] + ch + 1]
                if kind == "s":
                    o_ = xc[:, i, :].rearrange("p (s t) -> p s t", s=16)
                    src = lambda j: uv[:, i, :, j:j + 8]
                else:
                    o_ = xc[:, i, 0:n]
                    src = lambda j: uv[:, i, j:j + n]
                self.ts("dve", o_, src(3), w[3], ALU.mult, bcol, ALU.add)
                for j in (2, 1, 0):
                    self.stt(o_, src(j), w[j], o_, ALU.mult, ALU.add)
            self.act(xc[:, 0:4, 0:n], xc[:, 0:4, 0:n], AF.Silu)
            self.act(BT[:, 0:n], xc[:, 4, 0:n], AF.Silu)
            self.act(CT[:, 0:n], xc[:, 5, 0:n], AF.Silu)
            self.act(xc[:, 4, 0:n], xc[:, 4, 0:n], AF.Silu)
            pt = self.bank(4)
            for i in range(4):
                self.tr(pt[0:n, i * 128:(i + 1) * 128], xc[:, i, 0:n])
            self.cp("act", xtok[0:n, :], pt[0:n, 0:512])
            pt2 = self.bank(5)
            self.tr(pt2[0:n, 0:128], xc[:, 4, 0:n])
            self.cp("act", Btok[0:n, :], pt2[0:n, 0:128])
            pd = self.bank(5)
            self.inproj_tm(pd[0:n, 128:136], Wdt, slice(0, 8), c0, n)
            self.tt("dve", dt_[0:n, :], pd[0:n, 128:136], self.prow[0:n, PR["dtb"] + g * 8:PR["dtb"] + g * 8 + 8], ALU.add)
            self.act(dt_[0:n, :], dt_[0:n, :], AF.Exp)
            self.act(dt_[0:n, :], dt_[0:n, :], AF.Ln, bias=1.0)
            self.tt("dve", la[0:n, :], dt_[0:n, :], self.arow[0:n, g * 8:g * 8 + 8], ALU.mult)
            self.ts("dve", nla[0:n, :], la[0:n, :], -1.0, ALU.mult)
            pz = self.bank(6)
            self.inproj_tm(pz[0:n, 0:512], Wz, slice(0, 512), c0, n)
            self.act(zs[0:n, :], pz[0:n, 0:512], AF.Silu)
            if kind == "s":
                mcum = cm[0:n, CM["mcumS"]:CM["mcumS"] + n]
                mlast = cm[0:n, CM["mlastS"]:CM["mlastS"] + n]
            else:
                mcum = cm[0:n, CM["mcumP"]:CM["mcumP"] + n]
                mlast = cm[0:n, CM["ones"]:CM["ones"] + n]
            pc_ = self.bank(5)
            self.mm(pc_[0:n, 256:264], mcum, la[0:n, :])
            self.mm(pc_[0:n, 264:272], mlast, la[0:n, :])
            self.cp("dve", bb[0:n, :], pc_[0:n, 256:264])
            self.ts("dve", nbb[0:n, :], bb[0:n, :], -1.0, ALU.mult, 0.0, ALU.add)
            self.act(eb[0:n, :], pc_[0:n, 256:264], AF.Exp)
            self.tt("dve", te[0:n, :], pc_[0:n, 264:272], bb[0:n, :], ALU.subtract)
            self.act(te[0:n, :], te[0:n, :], AF.Exp)
            self.tt("dve", dtte[0:n, :], dt_[0:n, :], te[0:n, :], ALU.mult)
            xv = xtok[0:n, :].rearrange("p (h d) -> p h d", h=8)
            self.tt("dve", xd[0:n, :].rearrange("p (h d) -> p h d", h=8), xv,
                    dt_[0:n, :].unsqueeze(2).broadcast_to([n, 8, 64]), ALU.mult)
            self.tt("dve", xdte[0:n, :].rearrange("p (h d) -> p h d", h=8), xv,
                    dtte[0:n, :].unsqueeze(2).broadcast_to([n, 8, 64]), ALU.mult)
            pcb = self.bank(5)
            self.mm(pcb[0:n, 384:384 + n], BT[:, 0:n], CT[:, 0:n])
            self.tt("dve", cbm[0:n, 0:n], pcb[0:n, 384:384 + n], mcum, ALU.mult)
            py = self.bank(7)
            for q in range(2):
                psg = self.bank(2 + q)
                for hh in range(4):
                    h = q * 4 + hh
                    o_ = psg[0:n, hh * 128:hh * 128 + n]
                    self.mm(o_, la[0:n, h:h + 1].broadcast_to([n, n]), mcum, start=True, stop=True)
                sv = psg[0:n, :].rearrange("p (h t) -> p h t", h=4)[:, :, 0:n]
                segv = segc[0:n, :].rearrange("p (h t) -> p h t", h=4)[:, :, 0:n]
                decv = dec[0:n, :].rearrange("p (h t) -> p h t", h=4)[:, :, 0:n]
                mtv = MT[0:n, q * 512:(q + 1) * 512].rearrange("p (h t) -> p h t", h=4)[:, :, 0:n]
                for hh in range(4):
                    h = q * 4 + hh
                    self.ts("dve", segv[:, hh, :], sv[:, hh, :], nbb[0:n, h:h + 1], ALU.add, 0.0, ALU.min)
                self.act(decv, segv, AF.Exp)
                self.tt("dve", mtv, decv, cbm[0:n, 0:n].unsqueeze(1).broadcast_to([n, 4, n]), ALU.mult)
                for hh in range(4):
                    h = q * 4 + hh
                    self.mm(py[0:n, h * 64:(h + 1) * 64], MT[0:n, q * 512 + hh * 128:q * 512 + hh * 128 + n],
                            xd[0:n, h * 64:(h + 1) * 64])
            pyi = self.bank(6)
            pu = self.bank(3)
            if kind != "s":
                self.mm(pyi[0:n, 0:512], CT[:, 0:n], Sb[:, :])
                pe_ = self.bank(5)
                self.mm(pe_[:, 272:280], cm[0:n, CM["ones"]:CM["ones"] + 128], la[0:n, :])
                self.act(ebl[:, :], pe_[:, 272:280], AF.Exp)
                self.mm(pu[:, 0:512], Btok[0:n, :], xdte[0:n, :])
                self.tt("dve", S.rearrange("p (h d) -> p h d", h=8), S.rearrange("p (h d) -> p h d", h=8),
                        ebl[:, :].unsqueeze(2).broadcast_to([128, 8, 64]), ALU.mult)
                self.tt("dve", S, S, pu[:, 0:512], ALU.add)
                self.cp("act", Sb, S)
            else:
                self.memset("pool", Cpad[:, :, :], 0.0)
                self.cp("pool", Cdiag, CT[:, 0:128].rearrange("p (s t) -> p s t", s=16))
                self.tt("pool", Bm[:, :, :], Btok[:, :].unsqueeze(1).broadcast_to([128, 16, 128]),
                        cm[:, CM["seqm"]:CM["seqm"] + 16].unsqueeze(2).broadcast_to([128, 16, 128]), ALU.mult)
                pe_ = self.bank(5)
                for s in range(16):
                    self.mm(pe_[:, s * 8:(s + 1) * 8], cm[:, CM["seqm"] + s:CM["seqm"] + s + 1].broadcast_to([128, 128]), la[:, :])
                self.act(ebl_all[:, :], pe_[:, 0:128], AF.Exp)
                for s in range(16):
                    par = s % 2
                    sin = (stg, segc)[par][:, 0:512].rearrange("p (q n) -> p q n", q=4)
                    Ss = (S, dec)[par]
                    Sbs = (Sb, Sb2)[par]
                    outs_ = (t1, cst[:, 0:512])[par]
                    self.dma("sp", sin, I["st_ssd"][s, g * 512:(g + 1) * 512, :].rearrange("(q p) n -> p q n", p=128))
                    ptr = self.bank((4, 2)[par])
                    for q in range(4):
                        self.tr(ptr[:, q * 128:(q + 1) * 128], sin[:, q, :])
                    self.cp("act", Sbs, ptr[:, 0:512])
                    self.mm(pyi[0:n, 0:512], Cpad[:, s, :], Sbs[:, :], start=(s == 0), stop=(s == 15))
                    pus = self.bank((3, 0)[par])
                    self.mm(pus[:, 0:512], Bm[:, s, :], xdte[:, :])
                    self.tt("dve", Ss.rearrange("p (h d) -> p h d", h=8), ptr[:, 0:512].rearrange("p (h d) -> p h d", h=8),
                            ebl_all[:, s * 8:(s + 1) * 8].unsqueeze(2).broadcast_to([128, 8, 64]), ALU.mult)
                    self.tt("dve", Ss, Ss, pus[:, 0:512], ALU.add)
                    pto = self.bank(1)
                    for q in range(4):
                        self.tr(pto[:, q * 128:(q + 1) * 128], Ss[:, q * 128:(q + 1) * 128])
                    self.cp("act", outs_, pto[:, 0:512])
                    self.dma("act", O["ssd_s"][s, g * 512:(g + 1) * 512, :].rearrange("(q p) n -> p q n", p=128),
                             outs_.rearrange("p (q n) -> p q n", q=4))
            yv = yy[0:n, :].rearrange("p (h d) -> p h d", h=8)
            self.tt("dve", yv, pyi[0:n, 0:512].rearrange("p (h d) -> p h d", h=8),
                    eb[0:n, :].unsqueeze(2).broadcast_to([n, 8, 64]), ALU.mult)
            self.tt("dve", yy[0:n, :], yy[0:n, :], py[0:n, 0:512], ALU.add)
            self.tt("pool", t1[0:n, :].rearrange("p (h d) -> p h d", h=8), xv,
                    self.prow[0:n, PR["dd"] + g * 8:PR["dd"] + g * 8 + 8].unsqueeze(2).broadcast_to([n, 8, 64]), ALU.mult)
            self.tt("dve", yy[0:n, :], yy[0:n, :], t1[0:n, :], ALU.add)
            self.tt("dve", yy[0:n, :], yy[0:n, :], zs[0:n, :], ALU.mult)
            self.act(t1[0:n, :], yy[0:n, :], AF.Square, accum=ss[0:n, 0:1])
            self.act(ss[0:n, 1:2], ss[0:n, 0:1], AF.Sqrt, scale=1.0 / 512.0, bias=RMS_EPS)
            self.recip(ss[0:n, 1:2], ss[0:n, 1:2])
            self.stt(yy[0:n, :], yy[0:n, :], ss[0:n, 1:2], self.prow[0:n, PR["ng"] + g * 512:PR["ng"] + (g + 1) * 512],
                     ALU.mult, ALU.mult)
            pt = self.bank(4)
            for q in range(4):
                self.tr(pt[:, q * 128:q * 128 + n], yy[0:n, q * 128:(q + 1) * 128])
            self.cp("act", yT[:, :, 0:n], pt[:, 0:512].rearrange("p (k t) -> p k t", k=4)[:, :, 0:n])
            self.outproj_acc(Wo, 4, yT, c0, n, first)
            if kind == "s":
                self.cp("pool", tail[:, :, 0:48].rearrange("p c (s t) -> p c s t", s=16), uv[:, :, :, 8:11])
            elif ti == 16:
                self.cp("pool", tail[:, :, 48:51], uv[:, :, n:n + 3])
                pto = self.bank(1)
                for q in range(4):
                    self.tr(pto[:, q * 128:(q + 1) * 128], S[:, q * 128:(q + 1) * 128])
                self.cp("act", t1, pto[:, 0:512])
                self.dma("act", O["ssd_p"][g * 512:(g + 1) * 512, :].rearrange("(q p) n -> p q n", p=128),
                         t1.rearrange("p (q n) -> p q n", q=4))
            prev_n = n
        pt = self.bank(4)
        for i, ch in enumerate(cch):
            self.tr(pt[0:51, (i % 4) * 128:(i % 4) * 128 + 128], tail[:, i, :])
            self.cp("act", stg[0:51, (i % 4) * 128:(i % 4) * 128 + 128], pt[0:51, (i % 4) * 128:(i % 4) * 128 + 128])
            self.dma("act", O["ssdc"][:, ch * 128:(ch + 1) * 128], stg[0:51, (i % 4) * 128:(i % 4) * 128 + 128])

    def rg_unit(self, blk):
        I, O = self.I, self.O
        self.reset_arena()
        half = 5632 * (blk % 3)
        Wg = self.wview(half + 0, 8, 128)
        Wxr = self.wview(half + 1024, 8, 128)
        Wa = self.wbuf[:, half + 2048:half + 2176]
        Wx2 = self.wbuf[:, half + 2176:half + 2304]
        Wo = self.wview(half + 2304, 1, 1024)
        wie = I["w_in_e"]
        self.wload(Wxr, wie[:, 3600 + blk * 128:3600 + (blk + 1) * 128].rearrange("(k p) j -> p k j", p=128))
        self.wload(Wa, I["rg_wa"][blk])
        self.wload(Wx2, I["rg_wx"][blk])
        self.wload(Wg, wie[:, 2576 + blk * 128:2576 + (blk + 1) * 128].rearrange("(k p) j -> p k j", p=128))
        self.wload(Wo, I["w_out_e"][1024 + blk * 128:1024 + (blk + 1) * 128, :].rearrange("(k p) j -> p k j", p=128))
        RT = [(0, 16, "m")] + [(16 + 512 * j, 512, "p") for j in range(4)] + [(2064, 128, "s")]
        sets = []
        for i in range(2):
            d = {}
            d["ub"] = self.fa(516)
            for nm in ("xr", "rr", "ii", "aa", "gt", "hs", "gg"):
                d[nm] = self.fa(512)
            d["xrb"] = self.ba(512)
            d["yT"] = self.ba(512).rearrange("p (k t) -> p k t", k=1)
            sets.append(d)
        st0 = self.fa(128)
        cst = self.fa(128)
        sT = self.fa(16)
        fin = self.fa(17)
        tail = self.fa(51)
        stg = self.fa(128)
        stg2 = self.fa(128)
        hprev = self.fa(1)
        self.dma("sp", st0[0:16, :], I["st_rg"][:, blk * 128:(blk + 1) * 128])
        self.dma("sp", cst[0:48, :], I["st_rgc"][:, blk * 128:(blk + 1) * 128])
        self.memset("dve", hprev, 0.0)
        cw = [self.pcol[:, PC["rg_cw"] + blk * 4 + j:PC["rg_cw"] + blk * 4 + j + 1] for j in range(4)]
        cb = self.pc("rg_cb", blk)

        def stage_a(ti):
            c0, n, kind = RT[ti]
            d = sets[ti % 2]
            dp = sets[(ti + 1) % 2]
            u = d["ub"]
            xr, rr, ii, aa, gt, hs, gg, xrb, yT = (d[k] for k in ("xr", "rr", "ii", "aa", "gt", "hs", "gg", "xrb", "yT"))
            if kind == "s":
                uv = u[:, 0:176].rearrange("p (s t) -> p s t", s=16)
                pt = self.bank(4)
                self.tr(pt[:, 0:48], cst[0:48, :])
                self.cp("act", uv[:, :, 0:3], pt[:, 0:48].rearrange("p (s t) -> p s t", s=16))
                self.tr(pt[:, 64:80], st0[0:16, :])
                self.cp("act", sT[:, :], pt[:, 64:80])
            else:
                uv = u
                if ti > 0:
                    pn = RT[ti - 1][1]
                    self.cp("pool", uv[:, 0:3], dp["ub"][:, pn:pn + 3])
                else:
                    self.memset("pool", uv[:, 0:3], 0.0)
            pb = self.bank(2)
            self.inproj_fm(pb[:, 0:n], Wxr, 8, slice(0, 128), c0, n)
            if kind == "s":
                self.cp("act", uv[:, :, 3:11], pb[:, 0:128].rearrange("p (s t) -> p s t", s=16))
                o_ = xr[:, 0:128].rearrange("p (s t) -> p s t", s=16)
                t_ = gg[:, 0:128].rearrange("p (s t) -> p s t", s=16)
                src = lambda j: uv[:, :, j:j + 8]
            else:
                self.cp("act", uv[:, 3:3 + n], pb[:, 0:n])
                o_ = xr[:, 0:n]
                t_ = gg[:, 0:n]
                src = lambda j: uv[:, j:j + n]
            pgt = self.bank(5)
            self.inproj_fm(pgt[:, 0:n], Wg, 8, slice(0, 128), c0, n)
            self.ts("pool", o_, src(3), cw[3], ALU.mult, cb, ALU.add)
            for j in (2, 1):
                self.ts("pool", t_, src(j), cw[j], ALU.mult, 0.0, ALU.add)
                self.tt("pool", o_, o_, t_, ALU.add)
            self.stt(o_, src(0), cw[0], o_, ALU.mult, ALU.add)
            self.cp("dve", xrb[:, 0:n], xr[:, 0:n])
            pg = self.bank(3)
            pg2 = self.bank(4)
            self.mm(pg[:, 0:n], Wa, xrb[:, 0:n])
            self.mm(pg2[:, 0:n], Wx2, xrb[:, 0:n])
            self.act(rr[:, 0:n], pg[:, 0:n], AF.Sigmoid, bias=self.pc("rg_ba", blk))
            self.act(ii[:, 0:n], pg2[:, 0:n], AF.Sigmoid, bias=self.pc("rg_bx", blk))
            self.act(aa[:, 0:n], rr[:, 0:n], AF.Exp, scale=self.dc("rg_sc", blk))
            self.act(rr[:, 0:n], aa[:, 0:n], AF.Square)
            self.act(rr[:, 0:n], rr[:, 0:n], AF.Sqrt, scale=-1.0, bias=1.0)
            self.tt("pool", gt[:, 0:n], ii[:, 0:n], xr[:, 0:n], ALU.mult)
            self.tt("dve", gt[:, 0:n], gt[:, 0:n], rr[:, 0:n], ALU.mult)
            if kind == "s":
                for s_ in range(16):
                    self.scan(hs[:, s_ * 8:(s_ + 1) * 8], aa[:, s_ * 8:(s_ + 1) * 8], gt[:, s_ * 8:(s_ + 1) * 8], sT[:, s_:s_ + 1])
                self.cp("pool", fin[:, 0:16].unsqueeze(2), hs[:, 0:128].rearrange("p (s t) -> p s t", s=16)[:, :, 7:8])
            else:
                self.scan(hs[:, 0:n], aa[:, 0:n], gt[:, 0:n], hprev[:, 0:1])
                self.cp("pool", hprev[:, 0:1], hs[:, n - 1:n])
                if ti == 4:
                    self.cp("pool", fin[:, 16:17], hs[:, n - 1:n])
            self.act(gg[:, 0:n], pgt[:, 0:n], AF.Gelu)
            self.tt("dve", yT[:, 0, 0:n], hs[:, 0:n], gg[:, 0:n], ALU.mult)
            if kind == "s":
                self.cp("pool", tail[:, 0:48].rearrange("p (s t) -> p s t", s=16), uv[:, :, 8:11])
            elif ti == 4:
                self.cp("pool", tail[:, 48:51], uv[:, n:n + 3])

        def stage_b(ti):
            c0, n, kind = RT[ti]
            self.outproj_wide(Wo, 1, sets[ti % 2]["yT"], c0, n, False)

        stage_a(0)
        for ti in range(len(RT)):
            if ti + 1 < len(RT):
                stage_a(ti + 1)
            stage_b(ti)
        pt = self.bank(4)
        self.tr(pt[0:51, 0:128], tail[:, :])
        self.cp("act", stg[0:51, :], pt[0:51, 0:128])
        self.dma("act", O["rgc"][:, blk * 128:(blk + 1) * 128], stg[0:51, :])
        pt2 = self.bank(5)
        self.tr(pt2[0:17, 0:128], fin[:, :])
        self.cp("act", stg2[0:17, :], pt2[0:17, 0:128])
        self.dma("act", O["rg"][:, blk * 128:(blk + 1) * 128], stg2[0:17, :])

    def ln(self, gcol, bcol, final=False):
        self.reset_arena()
        onesb = self.onesb[:, :]
        W = 256
        tiles = [(c, min(W, NTOK - c)) for c in range(0, NTOK, W)]
        bs = [(self.ba(8 * W).rearrange("p (c t) -> p c t", c=8), self.ba(8 * W).rearrange("p (c t) -> p c t", c=8))
              for _ in range(2)]
        sm = [[self.fa(W) for _ in range(4)] for _ in range(2)]
        for ti, (c0, w) in enumerate(tiles):
            mean, rstd, mr, tmp = sm[ti % 2]
            xb, sqb = bs[ti % 2]
            hv = self.h32[:, :, c0:c0 + w]
            pb = self.bank(2 + (ti % 2))
            p1 = pb[:, 0:w]
            p2 = pb[:, 256:256 + w]
            self.cp("dve", xb[:, :, 0:w], hv)
            self.act(sqb[:, :, 0:w], hv, AF.Square)
            for c in range(8):
                self.mm(p1, onesb, xb[:, c, 0:w], start=(c == 0), stop=(c == 7))
            for c in range(8):
                self.mm(p2, onesb, sqb[:, c, 0:w], start=(c == 0), stop=(c == 7))
            self.act(mean[:, 0:w], p1, AF.Copy, scale=1.0 / D)
            self.tt("pool", tmp[:, 0:w], mean[:, 0:w], mean[:, 0:w], ALU.mult)
            self.stt(rstd[:, 0:w], p2, 1.0 / D, tmp[:, 0:w], ALU.mult, ALU.subtract)
            self.act(rstd[:, 0:w], rstd[:, 0:w], AF.Ln, bias=LN_EPS)
            self.act(rstd[:, 0:w], rstd[:, 0:w], AF.Exp, scale=-0.5)
            self.tt("pool", mr[:, 0:w], mean[:, 0:w], rstd[:, 0:w], ALU.mult)
            self.tt("dve", hv, hv, rstd[:, 0:w].unsqueeze(1).broadcast_to([128, 8, w]), ALU.mult)
            self.tt("pool", self.h32[:, 0:5, c0:c0 + w], self.h32[:, 0:5, c0:c0 + w],
                    mr[:, 0:w].unsqueeze(1).broadcast_to([128, 5, w]), ALU.subtract)
            self.tt("dve", self.h32[:, 5:8, c0:c0 + w], self.h32[:, 5:8, c0:c0 + w],
                    mr[:, 0:w].unsqueeze(1).broadcast_to([128, 3, w]), ALU.subtract)
            for c in range(8):
                hc = self.h32[:, c, c0:c0 + w]
                gc = self.pcol[:, gcol + c:gcol + c + 1]
                bc = self.pcol[:, bcol + c:bcol + c + 1]
                self.act(hc, hc, AF.Identity, scale=gc, bias=bc)
            if not final:
                self.cp("dve", self.hb[:, :, c0:c0 + w], hv)

    def mlp(self, layer):
        I = self.I
        for f in range(8):
            self.reset_arena()
            half = 8448 * (f % 2)
            W1 = self.wview(half, 8, 512)
            W2 = self.wview(half + 4096, 4, 1024)
            self.wload(W1, I["w1"][layer, :, f * 512:(f + 1) * 512].rearrange("(k p) j -> p k j", p=128))
            self.wload(W2, I["w2"][layer, f * 512:(f + 1) * 512, :].rearrange("(k p) j -> p k j", p=128))
            rbuf = self.fa(2048).rearrange("p (c t) -> p c t", c=4)
            abuf = self.ba(2048).rearrange("p (c t) -> p c t", c=4)
            for (c0, w) in CT512:
                for fc in range(4):
                    pb = self.bank(2 + (fc % 2))
                    self.inproj_fm(pb[:, 0:w], W1, 8, slice(fc * 128, (fc + 1) * 128), c0, w)
                    self.act(rbuf[:, fc, 0:w], pb[:, 0:w], AF.Relu)
                    self.tt("pool", abuf[:, fc, 0:w], rbuf[:, fc, 0:w], rbuf[:, fc, 0:w], ALU.mult)
                for oc in range(8):
                    po = self.bank(4 + (oc % 4))
                    for k in range(4):
                        self.mm(po[:, 0:w], W2[:, k, oc * 128:(oc + 1) * 128], abuf[:, k, 0:w], start=(k == 0), stop=(k == 3))
                    hv = self.h32[:, oc, c0:c0 + w]
                    if f == 0:
                        self.stt(hv, hv, ALPHA, po[:, 0:w], ALU.mult, ALU.add)
                    else:
                        self.tt("dve", hv, hv, po[:, 0:w], ALU.add)

    def layer1(self):
        for h in range(4):
            self.gla_unit(h, gla=True, first=(h == 0))
        for h in range(8):
            self.gla_unit(h, gla=False, first=False)

    def gla_unit(self, h, gla, first):
        I, O = self.I, self.O
        self.reset_arena()
        cm = self.cm
        wio = I["w_in_o"]
        uidx = h if gla else 4 + h
        half = 8448 * (uidx % 2) if gla else 5632 * (h % 3)
        dv = 256 if gla else 128
        nvc = dv // 128

        def wsrc(c0, ncol):
            return wio[:, c0:c0 + ncol].rearrange("(k p) j -> p k j", p=128)
        if gla:
            Wq = self.wview(half, 8, 128)
            Wk = self.wview(half + 1024, 8, 128)
            Wv = self.wview(half + 2048, 8, 256)
            Wg = self.wview(half + 4096, 8, 256)
            Wl = self.wview(half + 6144, 8, 16)
            Wg2 = self.wbuf[0:16, half + 6272:half + 6400]
            Wo = self.wview(half + 6400, 2, 1024)
            self.wload(Wq, wsrc(h * 128, 128))
            self.wload(Wk, wsrc(512 + h * 128, 128))
            self.wload(Wl, wsrc(3072, 16))
            self.wload(Wg2, I["wg2"][:, h * 128:(h + 1) * 128])
            self.wload(Wv, wsrc(1024 + h * 256, 256))
            self.wload(Wg, wsrc(2048 + h * 256, 256))
            self.wload(Wo, I["w_out_o"][h * 256:(h + 1) * 256, :].rearrange("(k p) j -> p k j", p=128))
            st_all_in = I["st_gla"][:, h].rearrange("s k v -> k s v")
            st_all_out = O["gla_s"][:, h].rearrange("s k v -> k s v")
            st_out_p = O["gla_p"][h]
            s1 = -1.0 / 16.0
        else:
            Wq = self.wview(half, 8, 128)
            Wk = self.wview(half + 1024, 8, 128)
            Wv = self.wview(half + 2048, 8, 128)
            Wg = self.wview(half + 3072, 8, 128)
            Wo = self.wview(half + 4096, 1, 1024)
            self.wload(Wq, wsrc(3088 + h * 128, 128))
            self.wload(Wk, wsrc(4112 + h * 128, 128))
            self.wload(Wv, wsrc(5136 + h * 128, 128))
            self.wload(Wg, wsrc(6160 + h * 128, 128))
            self.wload(Wo, I["w_out_o"][1024 + h * 128:1024 + (h + 1) * 128, :].rearrange("(k p) j -> p k j", p=128))
            st_all_in = I["st_hg"][:, h].rearrange("s k v -> k s v")
            st_all_out = O["hg_s"][:, h].rearrange("s k v -> k s v")
            st_out_p = O["hg_p"][h]
            s1 = 1.0
        FB = self.fa(7168)
        fbs = [[FB[:, (7 * i + k) * 512:(7 * i + k + 1) * 512] for k in range(7)] for i in range(2)]
        Sall = FB[:, 0:16 * dv].rearrange("p (s v) -> p s v", s=16)
        gsb = self.fa(1024)
        S = self.fa(256)
        S2 = self.fa(256)
        smalls = self.fa(32)
        brf, ebr, ebl, e1l = [smalls[:, 4 * i:4 * i + 4] for i in range(4)]
        lrb = self.ba(512)
        Sb = self.ba(256)
        Sb2 = self.ba(256)
        bsets = []
        for i in range(2):
            Bk = self.ba(4096)
            bsets.append(dict(qt=Bk[:, 0:512], ktb=Bk[:, 512:1024], ktok=Bk[:, 1024:1536], AT=Bk[:, 1536:2048],
                              vtok=Bk[:, 2048:3072], yT=Bk[:, 3072:4096], blk=Bk))
        km = bsets[1]["blk"][:, 0:2048].rearrange("p (s k) -> p s k", s=16)
        qtB, ktbB, ktokB, vtokB, ATB, yTB = (bsets[0][k] for k in ("qt", "ktb", "ktok", "vtok", "AT", "yT"))
        ones = cm[:, CM["ones"]:CM["ones"] + 128]
        self.memset("dve", S[:, 0:dv], 0.0)
        self.memset("dve", Sb[:, 0:dv], 0.0)
        self.memset("pool", bsets[0]["AT"], 0.0)
        self.memset("pool", bsets[1]["AT"], 0.0)
        ng_col = [self.pc("gla_ng", vc) if gla else self.pc("hg_ng", 0) for vc in range(nvc)]

        def tile_small(c0, n, kind):
            G = fbs[1][3:7]
            qf, kf, lf, bp = (G[0][:, 128 * i:128 * i + 128] for i in range(4))
            E1, E2, ktl, rstd = (G[1][:, 128 * i:128 * i + 128] for i in range(4))
            sqo = G[2][:, 0:256].rearrange("p (c t) -> p c t", c=2)
            gs = G[2][:, 256:512].rearrange("p (c t) -> p c t", c=2)
            ot = G[3][:, 0:256].rearrange("p (c t) -> p c t", c=2)
            BSm = bsets[1] if kind == "m" else bsets[0]
            qt, ktb, ktok, AT = BSm["qt"][:, 0:128], BSm["ktb"][:, 0:128], BSm["ktok"][:, 0:128], BSm["AT"][:, 0:128]
            vtok = BSm["vtok"][:, 0:256]
            yT = BSm["yT"][:, 0:256].rearrange("p (k t) -> p k t", k=2)
            pq = self.bank(2)
            self.inproj_fm(pq[:, 0:n], Wq, 8, slice(0, 128), c0, n)
            pk = self.bank(3)
            self.inproj_fm(pk[:, 0:n], Wk, 8, slice(0, 128), c0, n)
            if gla:
                self.act(qf[:, 0:n], pq[:, 0:n], AF.Copy, scale=128.0 ** -0.5)
                self.cp("act", kf[:, 0:n], pk[:, 0:n])
                pl = self.bank(5)
                self.inproj_fm(pl[0:16, 0:n], Wl, 8, slice(0, 16), c0, n)
                self.cp("act", lrb[0:16, 0:n], pl[0:16, 0:n])
                self.mm(pl[:, 128:128 + n], Wg2, lrb[0:16, 0:n])
                self.act(lf[:, 0:n], pl[:, 128:128 + n], AF.Exp, scale=-1.0, bias=self.dc("nbg2", h))
                self.act(lf[:, 0:n], lf[:, 0:n], AF.Ln, bias=1.0)
            else:
                self.act(qf[:, 0:n], pq[:, 0:n], AF.Silu)
                self.act(E1[:, 0:n], pk[:, 0:n], AF.Sigmoid)
                self.ts("dve", lf[:, 0:n], E1[:, 0:n], self.dc("oml", h), ALU.mult, self.dc("lb", h), ALU.add)
                self.act(lf[:, 0:n], lf[:, 0:n], AF.Ln)
                self.ts("dve", kf[:, 0:n], E1[:, 0:n], self.dc("noml", h), ALU.mult, self.dc("oml", h), ALU.add)
            if kind == "s":
                rst = cm[:, CM["rst8"]:CM["rst8"] + n]
                msk = cm[0:n, CM["mcumS"]:CM["mcumS"] + n]
                chunks = [(8 * s_, 8 * s_ + 8) for s_ in range(16)]
            else:
                rst = cm[:, CM["rst64"]:CM["rst64"] + n]
                msk = cm[0:n, CM["mcumP"]:CM["mcumP"] + n]
                chunks = [(0, n)]
            self.scan(bp[:, 0:n], rst, lf[:, 0:n], 0.0)
            self.act(E1[:, 0:n], bp[:, 0:n], AF.Exp, scale=s1)
            self.act(E2[:, 0:n], bp[:, 0:n], AF.Exp, scale=-s1)
            self.tt("dve", qt[:, 0:n], qf[:, 0:n], E1[:, 0:n], ALU.mult)
            self.tt("dve", ktl[:, 0:n], kf[:, 0:n], E2[:, 0:n], ALU.mult)
            self.cp("pool", ktb[:, 0:n], ktl[:, 0:n])
            ptk = self.bank(4)
            self.tr(ptk[0:n, 0:128], ktl[:, 0:n])
            self.cp("act", ktok[0:n, :], ptk[0:n, 0:128])
            pv = self.bank(6)
            self.inproj_tm(pv[0:n, 0:dv], Wv, slice(0, dv), c0, n)
            self.cp("act", vtok[0:n, 0:dv], pv[0:n, 0:dv])
            psc = self.bank(5)
            self.mm(psc[0:n, 256:256 + n], ktb[:, 0:n], qt[:, 0:n])
            self.tt("dve", AT[0:n, 0:n], psc[0:n, 256:256 + n], msk, ALU.mult)
            if kind == "s":
                self.tt("pool", km[:, :, :], ktok[:, :].unsqueeze(1).broadcast_to([128, 16, 128]),
                        cm[:, CM["seqm"]:CM["seqm"] + 16].unsqueeze(2).broadcast_to([128, 16, 128]), ALU.mult)
            pov = [self.bank(0), self.bank(1)]
            for vc in range(nvc):
                self.mm(pov[vc][:, 0:n], vtok[0:n, vc * 128:(vc + 1) * 128], AT[0:n, 0:n], start=True, stop=False)
            if kind == "s":
                self.dma("sp", Sall[:, :, :], st_all_in)
            for ci, (a0, a1) in enumerate(chunks):
                last = (ci == len(chunks) - 1)
                Sbr = Sb
                if kind == "s":
                    Sbr = (Sb, Sb2)[ci % 2]
                    self.cp("act", Sbr[:, 0:dv], Sall[:, ci, :])
                for vc in range(nvc):
                    self.mm(pov[vc][:, a0:a1], Sbr[:, vc * 128:(vc + 1) * 128], qt[:, a0:a1], start=False, stop=last)
                pu = self.bank(2 + ci % 2)
                if kind == "s":
                    self.mm(pu[:, 0:dv], km[:, ci, :], vtok[:, 0:dv])
                    self.tt("dve", Sall[:, ci, :], Sall[:, ci, :], pu[:, 0:dv], ALU.add)
                    self.ts("dve", Sall[:, ci, :], Sall[:, ci, :], E1[:, a1 - 1:a1], ALU.mult)
                else:
                    self.mm(pu[:, 0:dv], ktok[a0:a1, :], vtok[a0:a1, 0:dv])
                    self.tt("dve", S2[:, 0:dv], S[:, 0:dv], pu[:, 0:dv], ALU.add)
                    self.ts("dve", S[:, 0:dv], S2[:, 0:dv], E1[:, a1 - 1:a1], ALU.mult)
            if kind == "s":
                self.dma("act", st_all_out, Sall[:, :, :])
            for vc in range(nvc):
                self.act(sqo[:, vc, 0:n], pov[vc][:, 0:n], AF.Square)
            pss = self.bank(5)
            for vc in range(nvc):
                self.mm(pss[:, 384:384 + n], ones, sqo[:, vc, 0:n], start=(vc == 0), stop=(vc == nvc - 1))
            self.act(rstd[:, 0:n], pss[:, 384:384 + n], AF.Ln, scale=1.0 / dv, bias=RMS_EPS)
            self.act(rstd[:, 0:n], rstd[:, 0:n], AF.Exp, scale=-0.5)
            for vc in range(nvc):
                pg = self.bank(6 + vc)
                self.inproj_fm(pg[:, 0:n], Wg, 8, slice(vc * 128, (vc + 1) * 128), c0, n)
                self.act(gs[:, vc, 0:n], pg[:, 0:n], AF.Silu)
                self.tt("dve", ot[:, vc, 0:n], pov[vc][:, 0:n], rstd[:, 0:n], ALU.mult)
                self.stt(yT[:, vc, 0:n], ot[:, vc, 0:n], ng_col[vc], gs[:, vc, 0:n], ALU.mult, ALU.mult)
            self.outproj_acc(Wo, nvc, yT, c0, n, first)

        def tile_super(j):
            c0 = 16 + 512 * j
            n = 512
            F = fbs[j % 2]
            qf, kf, lf, bp, E1, E2, ktl = F
            sqo = [F[0], F[1]]
            rstd = F[2]
            gs = [gsb[:, 0:512], gsb[:, 512:1024]]
            ot = [F[5], F[6]]
            BS = bsets[j % 2]
            qt, ktb, ktok, AT = BS["qt"], BS["ktb"], BS["ktok"], BS["AT"]
            vtokB = BS["vtok"]
            vtok = vtokB[:, 0:4 * dv].rearrange("p (b v) -> p b v", b=4)
            yT = BS["yT"][:, :].rearrange("p (k t) -> p k t", k=2)
            pq = self.bank(0)
            self.inproj_fm(pq[:, 0:n], Wq, 8, slice(0, 128), c0, n)
            pk = self.bank(1)
            self.inproj_fm(pk[:, 0:n], Wk, 8, slice(0, 128), c0, n)
            if gla:
                pl = self.bank(2)
                self.inproj_fm(pl[0:16, 0:n], Wl, 8, slice(0, 16), c0, n)
            if gla:
                self.act(qf[:, :], pq[:, 0:n], AF.Copy, scale=128.0 ** -0.5)
                self.cp("act", kf[:, :], pk[:, 0:n])
                self.cp("act", lrb[0:16, 0:n], pl[0:16, 0:n])
                pg0 = self.bank(3)
                self.inproj_fm(pg0[:, 0:n], Wg, 8, slice(0, 128), c0, n)
                pl2 = self.bank(0)
                self.mm(pl2[:, 0:n], Wg2, lrb[0:16, 0:n])
                pg1 = self.bank(1)
                self.inproj_fm(pg1[:, 0:n], Wg, 8, slice(128, 256), c0, n)
                self.act(gs[0][:, :], pg0[:, 0:n], AF.Silu)
                self.act(gs[1][:, :], pg1[:, 0:n], AF.Silu)
                self.act(lf[:, :], pl2[:, 0:n], AF.Exp, scale=-1.0, bias=self.dc("nbg2", h))
                self.act(lf[:, :], lf[:, :], AF.Ln, bias=1.0)
            else:
                pg0 = self.bank(2)
                self.inproj_fm(pg0[:, 0:n], Wg, 8, slice(0, 128), c0, n)
                self.act(qf[:, :], pq[:, 0:n], AF.Silu)
                self.act(gs[0][:, :], pg0[:, 0:n], AF.Silu)
                self.act(E1[:, :], pk[:, 0:n], AF.Sigmoid)
                self.ts("dve", lf[:, :], E1[:, :], self.dc("oml", h), ALU.mult, self.dc("lb", h), ALU.add)
                self.act(lf[:, :], lf[:, :], AF.Ln)
                self.ts("dve", kf[:, :], E1[:, :], self.dc("noml", h), ALU.mult, self.dc("oml", h), ALU.add)
            for b in range(4):
                if gla:
                    pvb = self.bank(2 + b // 2)[:, (b % 2) * 256:(b % 2) * 256 + 256]
                else:
                    pvb = self.bank(3)[:, b * 128:(b + 1) * 128]
                self.inproj_tm(pvb, Wv, slice(0, dv), c0 + b * 128, 128)
            if gla:
                self.cp("act", vtokB[:, 0:512], self.bank(2)[:, 0:512])
                self.cp("act", vtokB[:, 512:1024], self.bank(3)[:, 0:512])
            else:
                self.cp("act", vtokB[:, 0:512], self.bank(3)[:, 0:512])
            self.scan(bp[:, :], cm[:, CM["rst128"]:CM["rst128"] + 512], lf[:, :], 0.0)
            bpv = bp[:, :].rearrange("p (b t) -> p b t", b=4)
            self.cp("dve", brf[:, :].unsqueeze(2), bpv[:, :, 64:65])
            self.act(ebr[:, :].unsqueeze(2), bpv[:, :, 64:65], AF.Exp, scale=s1)
            self.act(ebl[:, :].unsqueeze(2), bpv[:, :, 127:128], AF.Exp, scale=s1)
            self.tt("dve", bpv, bpv, brf[:, :].unsqueeze(2).broadcast_to([128, 4, 128]), ALU.subtract)
            self.act(E1[:, :], bp[:, :], AF.Exp, scale=s1)
            self.act(E2[:, :], bp[:, :], AF.Exp, scale=-s1)
            self.cp("dve", e1l[:, :].unsqueeze(2), E1[:, :].rearrange("p (b t) -> p b t", b=4)[:, :, 127:128])
            self.tt("dve", qt[:, :], qf[:, :], E1[:, :], ALU.mult)
            self.tt("dve", ktl[:, :], kf[:, :], E2[:, :], ALU.mult)
            self.cp("pool", ktb[:, :], ktl[:, :])
            ptk = self.bank(4)
            for b in range(4):
                self.tr(ptk[:, b * 128:(b + 1) * 128], ktl[:, b * 128:(b + 1) * 128])
            self.cp("act", ktok[:, :], ptk[:, 0:512])
            psc = self.bank(5)
            for b in range(4):
                o = b * 128
                self.mm(psc[:, o + 64:o + 128], ktb[:, o:o + 128], qt[:, o + 64:o + 128])
                self.mm(psc[0:64, o:o + 64], ktb[:, o:o + 64], qt[:, o:o + 64])
            pscv = psc[:, 0:512].rearrange("p (b t) -> p b t", b=4)
            ATv = AT[:, :].rearrange("p (b t) -> p b t", b=4)
            mk = cm[:, CM["mcumP"]:CM["mcumP"] + 128]
            self.tt("dve", ATv[0:64, :, :], pscv[0:64, :, :], mk[0:64, :].unsqueeze(1).broadcast_to([64, 4, 128]), ALU.mult)
            self.tt("dve", ATv[64:128, :, 64:128], pscv[64:128, :, 64:128],
                    mk[64:128, 64:128].unsqueeze(1).broadcast_to([64, 4, 64]), ALU.mult)
            pov = [self.bank(6), self.bank(7)]
            for b in range(4):
                o = b * 128
                self.act(Sb[:, 0:dv], S[:, 0:dv], AF.Identity, scale=ebr[:, b:b + 1])
                for vc in range(nvc):
                    self.mm(pov[vc][:, o:o + 128], vtok[:, b, vc * 128:(vc + 1) * 128], ATv[:, b, :], start=True, stop=False)
                    self.mm(pov[vc][:, o:o + 128], Sb[:, vc * 128:(vc + 1) * 128], qt[:, o:o + 128], start=False, stop=True)
                pu = self.bank(4 + b % 2)
                self.mm(pu[:, 0:dv], ktok[:, o:o + 128], vtok[:, b, 0:dv])
                self.ts("dve", S2[:, 0:dv], S[:, 0:dv], ebl[:, b:b + 1], ALU.mult)
                self.stt(S[:, 0:dv], pu[:, 0:dv], e1l[:, b:b + 1], S2[:, 0:dv], ALU.mult, ALU.add)
            pss = self.bank(4)
            if gla:
                for vc in range(nvc):
                    self.act(sqo[vc][:, :], pov[vc][:, 0:n], AF.Square)
                for vc in range(nvc):
                    self.mm(pss[:, 0:n], ones, sqo[vc][:, :], start=(vc == 0), stop=(vc == nvc - 1))
            else:
                self.act(lrb[:, 0:n], pov[0][:, 0:n], AF.Square)
                self.mm(pss[:, 0:n], self.onesb[:, :], lrb[:, 0:n])
            self.act(rstd[:, :], pss[:, 0:n], AF.Ln, scale=1.0 / dv, bias=RMS_EPS)
            self.act(rstd[:, :], rstd[:, :], AF.Exp, scale=-0.5)
            for vc in range(nvc):
                self.tt("dve", ot[vc][:, :], pov[vc][:, 0:n], rstd[:, :], ALU.mult)
                self.stt(yT[:, vc, :], ot[vc][:, :], ng_col[vc], gs[vc][:, :], ALU.mult, ALU.mult)
            self.outproj_wide(Wo, nvc, yT, c0, n, first, banks=(4, 5))

        tile_small(0, 16, "m")
        for j in range(4):
            tile_super(j)
        self.dma("act", st_out_p, S[:, 0:dv])
        tile_small(2064, 128, "s")

    def store_outputs(self):
        O = self.O
        stg = [self.fa(1024), self.fa(1024), self.fa(1024)]
        for ti, (c0, n, kind) in enumerate(TILES):
            if kind == "m":
                continue
            s = stg[ti % 3]
            pp = self.ps[2 + ti % 2]
            for c in range(8):
                self.tr(pp[0:n, c * 128:(c + 1) * 128], self.h32[:, c, c0:c0 + n])
            self.cp("act", s[0:n, :], pp[0:n, :])
            if kind == "p":
                r0 = c0 - 16
                self.dma("sp", O["y_p"][r0:r0 + n, :], s[0:n, :])
            else:
                self.dma("sp", O["y_s"], s[0:n, :])


_CACHE = {}


def kernel(**inp):
    if "nc" not in _CACHE:
        b = Builder()
        _CACHE["nc"] = b.build()
    nc = _CACHE["nc"]
    f32 = lambda a: np.ascontiguousarray(np.asarray(a, dtype=np.float32))
    pcol = _pack_pcol(inp)
    prow = _pack_prow(inp)
    cmask = _const_masks()
    shared = {
        "meta": f32(inp["meta_tokens"]),
        "w_in_e": f32(inp["w_in_even"][0]), "w_out_e": f32(inp["w_out_even"][0]),
        "w_in_o": f32(inp["w_in_odd"][0]), "w_out_o": f32(inp["w_out_odd"][0]),
        "w1": f32(inp["mlp_w1"]), "w2": f32(inp["mlp_w2"]),
        "rg_wa": f32(inp["rg_wa"][0]), "rg_wx": f32(inp["rg_wx"][0]), "wg2": f32(inp["gla_wg2"][0]),
        "pcol": pcol, "prow": prow, "cmask": cmask,
    }
    in_maps = []
    for i in range(NCORES):
        sl = slice(16 * i, 16 * i + 16)
        m = dict(shared)
        m["xp"] = f32(inp["x_prompt"][i])
        m["xs"] = f32(np.asarray(inp["x_sample"])[sl].reshape(128, D))
        m["st_ssd"] = f32(np.asarray(inp["state_ssd"])[0, sl].reshape(16, 1024, 128))
        m["st_ssdc"] = f32(np.asarray(inp["state_ssd_conv"])[0, sl].reshape(48, 1536))
        m["st_rg"] = f32(np.asarray(inp["state_rglru"])[0, sl])
        m["st_rgc"] = f32(np.asarray(inp["state_rglru_conv"])[0, sl].reshape(48, 1024))
        m["st_gla"] = f32(np.asarray(inp["state_gla"])[0, sl])
        m["st_hg"] = f32(np.asarray(inp["state_hgrn"])[0, sl])
        in_maps.append(m)
    res = run_bass_kernel_spmd(nc, in_maps, core_ids=list(range(NCORES)))
    R = res.results
    y_p = np.stack([R[i]["y_p"] for i in range(NCORES)], 0)
    y_s = np.concatenate([R[i]["y_s"].reshape(16, 8, D) for i in range(NCORES)], 0)
    p_ssd = np.stack([R[i]["ssd_p"].reshape(16, 64, 128) for i in range(NCORES)], 0)[None]
    p_ssdc = np.stack([R[i]["ssdc"][48:51] for i in range(NCORES)], 0)[None]
    p_rg = np.stack([R[i]["rg"][16] for i in range(NCORES)], 0)[None]
    p_rgc = np.stack([R[i]["rgc"][48:51] for i in range(NCORES)], 0)[None]
    p_gla = np.stack([R[i]["gla_p"] for i in range(NCORES)], 0)[None]
    p_hg = np.stack([R[i]["hg_p"] for i in range(NCORES)], 0)[None]
    s_ssd = np.concatenate([R[i]["ssd_s"].reshape(16, 16, 64, 128) for i in range(NCORES)], 0)[None]
    s_ssdc = np.concatenate([R[i]["ssdc"][0:48].reshape(16, 3, 1536) for i in range(NCORES)], 0)[None]
    s_rg = np.concatenate([R[i]["rg"][0:16] for i in range(NCORES)], 0)[None]
    s_rgc = np.concatenate([R[i]["rgc"][0:48].reshape(16, 3, 1024) for i in range(NCORES)], 0)[None]
    s_gla = np.concatenate([R[i]["gla_s"] for i in range(NCORES)], 0)[None]
    s_hg = np.concatenate([R[i]["hg_s"] for i in range(NCORES)], 0)[None]
    outs = (y_p, y_s, p_ssd, p_ssdc, p_rg, p_rgc, p_gla, p_hg, s_ssd, s_ssdc, s_rg, s_rgc, s_gla, s_hg)
    return tuple(np.ascontiguousarray(o, dtype=np.float32) for o in outs)
```
